# Optimizing a Trainium2 kernel written in Bass

```python
import math
import jax
import jax.numpy as jnp
from jax import lax
import numpy as np

D_MODEL = 1024
BATCH = 16
SEQ = 256
DEPTH = 4
DEC_BATCH = 4
DEC_SEQ = 2048
PAST_LEN = 256

GRID_W = 64
ROPE_BASE = 10000.0
EPS = 1e-6
Q_BLOCK = 128
CHUNK = 128
D_FF = 4 * D_MODEL
N_MOD = 6

H_A = 4
NOPE_A = 64
ROPE_A = 32
V_A = 64
Q_RANK = 256
KV_RANK = 128
H_B = 4
DH_B = 32
H_C = 8
P_C = 64
N_C = 64
G_C = 2
CONV_K = 5
D_INNER = H_C * P_C
CONV_DIM = D_INNER + 2 * G_C * N_C

SPLIT_SIZES = (Q_RANK, KV_RANK, ROPE_A, H_B * 2 * DH_B, H_B * 2 * DH_B, H_B * 2 * DH_B, D_INNER, CONV_DIM, 2 * H_C)
D_IN_PROJ = sum(SPLIT_SIZES)
MIX_WIDTH = H_A * V_A + H_B * 2 * DH_B + D_INNER

kernel_name = 'hybrid_mla_diff_ssd_flow_step'


def rmsnorm(x, g):
    xf = x.astype(jnp.float32)
    y = xf * lax.rsqrt(jnp.mean(xf * xf, axis=-1, keepdims=True) + EPS)
    return (y * g.astype(jnp.float32)).astype(x.dtype)


def grid_angles(n_tok, rot_dim):
    n_rows = n_tok // GRID_W
    rows = jnp.repeat(jnp.arange(n_rows, dtype=jnp.float32), GRID_W)
    cols = jnp.tile(jnp.arange(GRID_W, dtype=jnp.float32), n_rows)
    half = rot_dim // 2
    freqs = ROPE_BASE ** (-jnp.arange(0, half, 2, dtype=jnp.float32) / half)
    return rows[:, None] * freqs, cols[:, None] * freqs


def axial_rope(x, ang_r, ang_c):
    half = x.shape[-1] // 2
    extra = (1,) * (x.ndim - 3)

    def rot(xa, ang):
        m = xa.shape[-1] // 2
        cos = jnp.cos(ang).reshape(ang.shape[0], *extra, m)
        sin = jnp.sin(ang).reshape(ang.shape[0], *extra, m)
        x1 = xa[..., :m].astype(jnp.float32)
        x2 = xa[..., m:].astype(jnp.float32)
        return jnp.concatenate([x1 * cos - x2 * sin, x2 * cos + x1 * sin], axis=-1)

    out = jnp.concatenate([rot(x[..., :half], ang_r), rot(x[..., half:], ang_c)], axis=-1)
    return out.astype(x.dtype)


def to_query_blocks(q):
    b, n = q.shape[:2]
    return jnp.moveaxis(q.reshape(b, n // Q_BLOCK, Q_BLOCK, *q.shape[2:]), 1, 0)


def from_query_blocks(o):
    nb, b = o.shape[:2]
    return jnp.moveaxis(o, 0, 1).reshape(b, nb * Q_BLOCK, *o.shape[3:])


def softmax_attention(q, k, v):
    scale = q.shape[-1] ** -0.5
    kf = k.astype(jnp.float32)
    vf = v.astype(jnp.float32)

    def one_block(qb):
        s = jnp.einsum('bqhd,bkhd->bhqk', qb.astype(jnp.float32), kf) * scale
        p = jax.nn.softmax(s, axis=-1)
        return jnp.einsum('bhqk,bkhd->bqhd', p, vf)

    return from_query_blocks(lax.map(one_block, to_query_blocks(q))).astype(v.dtype)


def differential_attention(q, k, v, lam):
    scale = q.shape[-1] ** -0.5
    kf = k.astype(jnp.float32)
    vf = v.astype(jnp.float32)

    def one_block(qb):
        s = jnp.einsum('bqhmd,bkhmd->bmhqk', qb.astype(jnp.float32), kf) * scale
        p = jax.nn.softmax(s, axis=-1)
        w = p[:, 0] - lam * p[:, 1]
        return jnp.einsum('bhqk,bkhd->bqhd', w, vf)

    return from_query_blocks(lax.map(one_block, to_query_blocks(q))).astype(v.dtype)


def diff_lambda(p, layer):
    lam_init = 0.8 - 0.6 * math.exp(-0.3 * layer)
    lq1 = p['diff_lq1'].astype(jnp.float32)
    lk1 = p['diff_lk1'].astype(jnp.float32)
    lq2 = p['diff_lq2'].astype(jnp.float32)
    lk2 = p['diff_lk2'].astype(jnp.float32)
    lam = jnp.exp(jnp.sum(lq1 * lk1)) - jnp.exp(jnp.sum(lq2 * lk2)) + lam_init
    return lam, lam_init


def ssd_scan(x, dt, a_neg, b_in, c_in, h0):
    bsz, n = x.shape[:2]
    nc = n // CHUNK
    rep = H_C // G_C
    xc = x.astype(jnp.float32).reshape(bsz, nc, CHUNK, H_C, P_C)
    bc = jnp.repeat(b_in.astype(jnp.float32), rep, axis=2).reshape(bsz, nc, CHUNK, H_C, N_C)
    cc = jnp.repeat(c_in.astype(jnp.float32), rep, axis=2).reshape(bsz, nc, CHUNK, H_C, N_C)
    dtc = dt.reshape(bsz, nc, CHUNK, H_C)
    a_cum = jnp.cumsum(dtc * a_neg, axis=2)
    seg = a_cum[:, :, :, None, :] - a_cum[:, :, None, :, :]
    lower = jnp.tril(jnp.ones((CHUNK, CHUNK), dtype=bool))[:, :, None]
    decay_in = jnp.exp(jnp.where(lower, seg, -jnp.inf))
    scores = jnp.einsum('bcihn,bcjhn->bcijh', cc, bc) * decay_in
    y_diag = jnp.einsum('bcijh,bcjh,bcjhp->bcihp', scores, dtc, xc)
    decay_end = jnp.exp(a_cum[:, :, -1:, :] - a_cum)
    states = jnp.einsum('bcjhn,bcjh,bcjhp->bchpn', bc, decay_end * dtc, xc)
    chunk_decay = jnp.exp(a_cum[:, :, -1, :])

    def step(h, inp):
        dec, st = inp
        return dec[:, :, None, None] * h + st, h

    h_final, h_prev = lax.scan(step, h0.astype(jnp.float32),
                               (jnp.moveaxis(chunk_decay, 1, 0), jnp.moveaxis(states, 1, 0)))
    h_prev = jnp.moveaxis(h_prev, 0, 1)
    y_off = jnp.einsum('bcihn,bchpn->bcihp', cc, h_prev) * jnp.exp(a_cum)[..., None]
    return (y_diag + y_off).reshape(bsz, n, H_C, P_C), h_final


def dwconv(u, w, bias):
    out = lax.conv_general_dilated(u, w[:, None, :].astype(u.dtype), window_strides=(1,),
                                   padding=[(CONV_K // 2, CONV_K // 2)],
                                   dimension_numbers=('NWC', 'WIO', 'NWC'),
                                   feature_group_count=u.shape[-1])
    return out + bias


def ssm_mixer(z, xbc, dt_raw, p, h0):
    bsz, n = z.shape[:2]
    xbc = jax.nn.silu(dwconv(xbc, p['ssm_conv_w'], p['ssm_conv_b']))
    xs, b_in, c_in = jnp.split(xbc, [D_INNER, D_INNER + G_C * N_C], axis=-1)
    xs = xs.reshape(bsz, n, H_C, P_C)
    b_in = b_in.reshape(bsz, n, G_C, N_C)
    c_in = c_in.reshape(bsz, n, G_C, N_C)
    dt = jax.nn.softplus(dt_raw.astype(jnp.float32).reshape(bsz, n, 2, H_C)
                         + p['ssm_dt_bias'].astype(jnp.float32))
    a_neg = -jnp.exp(p['ssm_A_log'].astype(jnp.float32))
    y_f, h_f = ssd_scan(xs, dt[:, :, 0], a_neg[0], b_in, c_in, h0[:, 0])
    y_b, h_b = ssd_scan(xs[:, ::-1], dt[:, ::-1, 1], a_neg[1], b_in[:, ::-1], c_in[:, ::-1], h0[:, 1])
    y = y_f + y_b[:, ::-1] + p['ssm_D'].astype(jnp.float32)[:, None] * xs.astype(jnp.float32)
    y = y.reshape(bsz, n, D_INNER) * jax.nn.silu(z.astype(jnp.float32))
    y = rmsnorm(y, p['ssm_norm_g']).astype(z.dtype)
    return y, jnp.stack([h_f, h_b], axis=1)


def mla_expand(ckv, krope, p):
    bsz, n = ckv.shape[:2]
    kv = (ckv @ p['w_ukv']).reshape(bsz, n, H_A, NOPE_A + V_A)
    k_nope, v = kv[..., :NOPE_A], kv[..., NOPE_A:]
    k_rope = jnp.broadcast_to(krope[:, :, None, :], (bsz, n, H_A, ROPE_A))
    k = rmsnorm(jnp.concatenate([k_nope, k_rope], axis=-1), p['mla_qk_norm_k'])
    return k, v


def rope_tail(x, ang_r, ang_c):
    return jnp.concatenate([x[..., :NOPE_A], axial_rope(x[..., NOPE_A:], ang_r, ang_c)], axis=-1)


def mixing_sublayer(h, p, layer, ctx):
    bsz, n, _ = h.shape
    offs = np.cumsum(SPLIT_SIZES)[:-1].tolist()
    cq, ckv, krope, dq, dk, dv, z, xbc, dt_raw = jnp.split(h @ p['w_in'], offs, axis=-1)
    q_a = (rmsnorm(cq, p['mla_q_norm_g']) @ p['w_uq']).reshape(bsz, n, H_A, NOPE_A + ROPE_A)
    q_a = rmsnorm(q_a, p['mla_qk_norm_q'])
    ckv = rmsnorm(ckv, p['mla_kv_norm_g'])
    k_a, v_a = mla_expand(ckv, krope, p)
    q_d = rmsnorm(dq.reshape(bsz, n, H_B, 2, DH_B), p['diff_q_norm_g'])
    k_d = rmsnorm(dk.reshape(bsz, n, H_B, 2, DH_B), p['diff_k_norm_g'])
    v_d = dv.reshape(bsz, n, H_B, 2 * DH_B)
    if ctx is None:
        ctx_out = (ckv, krope, k_d.reshape(bsz, n, H_B, 2 * DH_B), v_d)
        h0 = jnp.zeros((bsz, 2, H_C, P_C, N_C), jnp.float32)
    else:
        ckv_c, krope_c, kd_c, vd_c, h0 = ctx
        ang_r, ang_c = grid_angles(n, ROPE_A)
        q_a = rope_tail(q_a, ang_r, ang_c)
        k_a = rope_tail(k_a, ang_r, ang_c)
        q_d = axial_rope(q_d, ang_r, ang_c)
        k_d = axial_rope(k_d, ang_r, ang_c)
        k_ac, v_ac = mla_expand(ckv_c, krope_c, p)
        k_a = jnp.concatenate([k_ac, k_a], axis=1)
        v_a = jnp.concatenate([v_ac, v_a], axis=1)
        k_d = jnp.concatenate([kd_c.reshape(bsz, -1, H_B, 2, DH_B), k_d], axis=1)
        v_d = jnp.concatenate([vd_c, v_d], axis=1)
        ctx_out = None
    o_a = softmax_attention(q_a, k_a, v_a).reshape(bsz, n, H_A * V_A)
    lam, lam_init = diff_lambda(p, layer)
    o_d = differential_attention(q_d, k_d, v_d, lam)
    o_d = (rmsnorm(o_d, p['diff_subln_g']) * (1.0 - lam_init)).reshape(bsz, n, H_B * 2 * DH_B)
    o_c, h_last = ssm_mixer(z, xbc, dt_raw, p, h0)
    out = jnp.concatenate([o_a, o_d, o_c], axis=-1) @ p['w_out']
    if ctx is None:
        return out, (ctx_out[0], ctx_out[1], ctx_out[2], ctx_out[3], h_last)
    return out, None


def trunk_layer(x, mod, p, layer, ctx):
    shift1, scale1, gate1, shift2, scale2, gate2 = jnp.split(mod, N_MOD, axis=-1)
    h = rmsnorm(x, p['norm1_g']) * (1.0 + scale1) + shift1
    mix, ctx_out = mixing_sublayer(h, p, layer, ctx)
    x = x + gate1 * mix
    h = rmsnorm(x, p['norm2_g']) * (1.0 + scale2) + shift2
    u = jnp.square(jax.nn.relu(h @ p['w_ff1']))
    x = x + gate2 * (u @ p['w_ff2'])
    return x, ctx_out


def setup_inputs(seed: int = 0) -> dict:
    key = jax.random.key(seed)
    ks = iter(jax.random.split(key, 40))

    def nrm(shape, scale):
        return jax.random.normal(next(ks), shape, jnp.float32) * scale

    def gain(shape):
        return 1.0 + 0.02 * jax.random.normal(next(ks), shape, jnp.float32)

    dt0 = jnp.exp(jax.random.uniform(next(ks), (DEPTH, 2, H_C), jnp.float32,
                                     minval=math.log(1e-3), maxval=math.log(1e-1)))
    return {
        'x_prompt': nrm((BATCH, SEQ, D_MODEL), 1.0),
        'x_sample': nrm((DEC_BATCH, DEC_SEQ, D_MODEL), 1.0),
        'cache_mla_ckv': nrm((DEC_BATCH, DEPTH, PAST_LEN, KV_RANK), 1.0),
        'cache_mla_krope': nrm((DEC_BATCH, DEPTH, PAST_LEN, ROPE_A), 1.0),
        'cache_diff_k': nrm((DEC_BATCH, DEPTH, PAST_LEN, H_B, 2 * DH_B), 1.0),
        'cache_diff_v': nrm((DEC_BATCH, DEPTH, PAST_LEN, H_B, 2 * DH_B), 1.0),
        'state_ssm': nrm((DEC_BATCH, DEPTH, 2, H_C, P_C, N_C), 0.1),
        'c': nrm((DEC_BATCH, D_MODEL), 1.0),
        'c_ctx': nrm((D_MODEL,), 1.0),
        'norm1_g': gain((DEPTH, D_MODEL)),
        'norm2_g': gain((DEPTH, D_MODEL)),
        'w_ada': nrm((DEPTH, D_MODEL, N_MOD * D_MODEL), D_MODEL ** -0.5),
        'b_ada': nrm((DEPTH, N_MOD * D_MODEL), 0.01),
        'w_in': nrm((DEPTH, D_MODEL, D_IN_PROJ), D_MODEL ** -0.5),
        'w_out': nrm((DEPTH, MIX_WIDTH, D_MODEL), MIX_WIDTH ** -0.5),
        'mla_q_norm_g': gain((DEPTH, Q_RANK)),
        'mla_kv_norm_g': gain((DEPTH, KV_RANK)),
        'w_uq': nrm((DEPTH, Q_RANK, H_A * (NOPE_A + ROPE_A)), Q_RANK ** -0.5),
        'w_ukv': nrm((DEPTH, KV_RANK, H_A * (NOPE_A + V_A)), KV_RANK ** -0.5),
        'mla_qk_norm_q': gain((DEPTH, NOPE_A + ROPE_A)),
        'mla_qk_norm_k': gain((DEPTH, NOPE_A + ROPE_A)),
        'diff_q_norm_g': gain((DEPTH, DH_B)),
        'diff_k_norm_g': gain((DEPTH, DH_B)),
        'diff_lq1': nrm((DEPTH, DH_B), 0.1),
        'diff_lk1': nrm((DEPTH, DH_B), 0.1),
        'diff_lq2': nrm((DEPTH, DH_B), 0.1),
        'diff_lk2': nrm((DEPTH, DH_B), 0.1),
        'diff_subln_g': gain((DEPTH, 2 * DH_B)),
        'ssm_conv_w': nrm((DEPTH, CONV_K, CONV_DIM), CONV_K ** -0.5),
        'ssm_conv_b': nrm((DEPTH, CONV_DIM), 0.01),
        'ssm_A_log': jnp.log(jax.random.uniform(next(ks), (DEPTH, 2, H_C), jnp.float32, minval=1.0, maxval=16.0)),
        'ssm_dt_bias': dt0 + jnp.log(-jnp.expm1(-dt0)),
        'ssm_D': gain((DEPTH, H_C)),
        'ssm_norm_g': gain((DEPTH, D_INNER)),
        'w_ff1': nrm((DEPTH, D_MODEL, D_FF), D_MODEL ** -0.5),
        'w_ff2': nrm((DEPTH, D_FF, D_MODEL), D_FF ** -0.5),
    }


def reference(x_prompt, x_sample, cache_mla_ckv, cache_mla_krope, cache_diff_k, cache_diff_v, state_ssm,
              c, c_ctx, norm1_g, norm2_g, w_ada, b_ada, w_in, w_out, mla_q_norm_g, mla_kv_norm_g,
              w_uq, w_ukv, mla_qk_norm_q, mla_qk_norm_k, diff_q_norm_g, diff_k_norm_g,
              diff_lq1, diff_lk1, diff_lq2, diff_lk2, diff_subln_g, ssm_conv_w, ssm_conv_b,
              ssm_A_log, ssm_dt_bias, ssm_D, ssm_norm_g, w_ff1, w_ff2):
    xp = x_prompt
    xs = x_sample
    ckv_l, krope_l, kd_l, vd_l, st_l = [], [], [], [], []
    for l in range(DEPTH):
        p = dict(norm1_g=norm1_g[l], norm2_g=norm2_g[l], w_in=w_in[l], w_out=w_out[l],
                 mla_q_norm_g=mla_q_norm_g[l], mla_kv_norm_g=mla_kv_norm_g[l], w_uq=w_uq[l], w_ukv=w_ukv[l],
                 mla_qk_norm_q=mla_qk_norm_q[l], mla_qk_norm_k=mla_qk_norm_k[l],
                 diff_q_norm_g=diff_q_norm_g[l], diff_k_norm_g=diff_k_norm_g[l],
                 diff_lq1=diff_lq1[l], diff_lk1=diff_lk1[l], diff_lq2=diff_lq2[l], diff_lk2=diff_lk2[l],
                 diff_subln_g=diff_subln_g[l], ssm_conv_w=ssm_conv_w[l], ssm_conv_b=ssm_conv_b[l],
                 ssm_A_log=ssm_A_log[l], ssm_dt_bias=ssm_dt_bias[l], ssm_D=ssm_D[l], ssm_norm_g=ssm_norm_g[l],
                 w_ff1=w_ff1[l], w_ff2=w_ff2[l])
        mod_ctx = (jax.nn.silu(c_ctx) @ w_ada[l] + b_ada[l])[None, None, :]
        xp, (ckv, krope, kd, vd, st) = trunk_layer(xp, mod_ctx, p, l, None)
        ckv_l.append(ckv)
        krope_l.append(krope)
        kd_l.append(kd)
        vd_l.append(vd)
        st_l.append(st)
        mod_lat = (jax.nn.silu(c) @ w_ada[l] + b_ada[l])[:, None, :]
        xs, _ = trunk_layer(xs, mod_lat, p, l,
                            (cache_mla_ckv[:, l], cache_mla_krope[:, l], cache_diff_k[:, l],
                             cache_diff_v[:, l], state_ssm[:, l]))
    new_mla_ckv = jnp.stack(ckv_l, axis=1)
    new_mla_krope = jnp.stack(krope_l, axis=1)
    new_diff_k = jnp.stack(kd_l, axis=1)
    new_diff_v = jnp.stack(vd_l, axis=1)
    new_ssm_state = jnp.stack(st_l, axis=1)
    return (xp, xs, new_mla_ckv, new_mla_krope, new_diff_k, new_diff_v, new_ssm_state)
```

```python
import math
from contextlib import ExitStack
import numpy as np
import concourse.bass as bass
import concourse.mybir as mybir
from concourse.bass_utils import run_bass_kernel_spmd

F32 = mybir.dt.float32
BF16 = mybir.dt.bfloat16
AF = mybir.ActivationFunctionType
ALU = mybir.AluOpType
AX = mybir.AxisListType

NL = 4
NT = 2560
EPS = 1e-6
SEM_ROT = 30000
DEPTH_RUN = NL
DEBUG = {}


class StopBuild(Exception):
    pass


STOP_AT = [None]


def ckpt(k):
    if STOP_AT[0] is not None and k >= STOP_AT[0]:
        raise StopBuild()


class Buf:
    __slots__ = ("name", "w", "r", "dkey")

    def __init__(self, name):
        self.name = name
        self.w = None
        self.r = {}
        self.dkey = None


class T:
    __slots__ = ("ap", "b")

    def __init__(self, ap, name):
        self.ap = ap
        self.b = Buf(name)

    def __getitem__(self, k):
        return self.ap[k]


class Prog:
    ENGS = ("pe", "act", "dve", "pool", "sp")

    def __init__(self, nc):
        self.nc = nc
        self.st = ExitStack()
        self.streams = {e: [] for e in self.ENGS}
        self.cnt = {}
        self.known = {e: {} for e in self.ENGS}
        self.engkey = {e: e + "0" for e in self.ENGS}
        self.engrot = {e: 0 for e in self.ENGS}
        self.out_tokens = []
        self.nops = 0
        self.uid = 0
        self.free_dkeys = []
        self.recent = []

    def sb(self, name, shape, dt=F32):
        return self.st.enter_context(self.nc.sbuf_tensor(name, list(shape), dt))

    def ps(self, name, shape, dt=F32):
        return self.st.enter_context(self.nc.psum_tensor(name, list(shape), dt))

    def _deps(self, reads, writes, eng=None):
        deps = []
        for b in reads:
            if b.w is not None:
                deps.append(b.w)

        def own(k):
            return eng is not None and k.startswith(eng) and k[len(eng):].isdigit()
        for b in writes:
            if b.w is not None and not own(b.w[0]):
                deps.append(b.w)
            for k, v in b.r.items():
                if not own(k):
                    deps.append((k, v))
        return deps

    def _waits(self, eng, deps):
        need = {}
        kn = self.known[eng]
        for k, v in deps:
            if eng == "pe" and k.startswith("pe"):
                continue
            if kn.get(k, 0) >= v:
                continue
            if need.get(k, 0) < v:
                need[k] = v
        for k, v in need.items():
            kn[k] = v
            self.streams[eng].append(("wait", k, v))

    def _mark(self, tok, reads, writes):
        k, v = tok
        for b in reads:
            if b.r.get(k, 0) < v:
                b.r[k] = v
        for b in writes:
            b.w = tok
            b.r = {}

    def op(self, eng, fn, reads=(), writes=()):
        reads = [r.b if isinstance(r, T) else r for r in reads]
        writes = [w.b if isinstance(w, T) else w for w in writes]
        self._waits(eng, self._deps(reads, writes, eng))
        k = self.engkey[eng]
        c = self.cnt.get(k, 0) + 1
        if c > SEM_ROT:
            self.engrot[eng] += 1
            k = self.engkey[eng] = eng + str(self.engrot[eng])
            c = 1
        self.cnt[k] = c
        tok = (k, c)
        self.streams[eng].append(("op", fn, k, 1))
        self._mark(tok, reads, writes)
        self.nops += 1
        return tok

    def dma(self, buf, items, reads=(), writes=(), is_out=False):
        reads = [r.b if isinstance(r, T) else r for r in reads]
        writes = [w.b if isinstance(w, T) else w for w in writes]
        if isinstance(buf, T):
            buf = buf.b
        if buf.dkey is None:
            if self.free_dkeys:
                buf.dkey = self.free_dkeys.pop()
            else:
                self.uid += 1
                buf.dkey = "d%d" % self.uid
            self.recent.append(buf)
        k = buf.dkey
        deps = self._deps(reads, writes)
        for q in dict.fromkeys(it[0] for it in items):
            self._waits(q, deps)
        c = self.cnt.get(k, 0)
        for it in items:
            q, o, i = it[0], it[1], it[2]
            c += 16
            self.streams[q].append(("op", (lambda e, o=o, i=i: e.dma_start(out=o, in_=i)), k, 16))
        assert c < 1000000, (k, c)
        self.cnt[k] = c
        tok = (k, c)
        self._mark(tok, reads, writes)
        if is_out:
            self.out_tokens.append(tok)
        return tok

    def barrier(self):
        deps = list(self.cnt.items())
        for e in self.ENGS:
            kn = self.known[e]
            for k, v in deps:
                if v > 0 and kn.get(k, 0) < v:
                    kn[k] = v
                    self.streams[e].append(("wait", k, v))
        for b in self.recent:
            if b.dkey is not None:
                self.free_dkeys.append(b.dkey)
                b.dkey = None
        self.recent = []

    def finish(self):
        nc = self.nc
        fin = {}
        for k, v in self.out_tokens:
            fin[k] = max(fin.get(k, 0), v)
        for k, v in fin.items():
            if self.known["sp"].get(k, 0) < v:
                self.streams["sp"].append(("wait", k, v))
        sems = {}
        for k in self.cnt:
            sems[k] = self.st.enter_context(nc.semaphore(k))
        block = self.st.enter_context(nc.Block())
        streams = self.streams

        def replay(e, lst):
            for item in lst:
                if item[0] == "wait":
                    e.wait_ge(sems[item[1]], item[2])
                else:
                    item[1](e).then_inc(sems[item[2]], item[3])

        @block.tensor
        def _(e):
            replay(e, streams["pe"])

        @block.scalar
        def _(e):
            replay(e, streams["act"])

        @block.vector
        def _(e):
            replay(e, streams["dve"])

        @block.gpsimd
        def _(e):
            replay(e, streams["pool"])

        @block.sync
        def _(e):
            replay(e, streams["sp"])

        self.st.close()


class Arena:
    def __init__(self, P, name, nbytes):
        self.t = P.sb(name, [128, nbytes // 2], BF16)
        self.cap = nbytes // 2
        self.off = 0
        self.n = 0
        self.hi = 0

    def reset(self):
        self.off = 0

    def get(self, shape, dt, name="t"):
        free = 1
        for s in shape[1:]:
            free *= s
        nb = free * (4 if dt == F32 else 2)
        ne = ((nb + 3) // 4) * 2
        assert self.off + ne <= self.cap, ("arena overflow", name, self.off, ne, self.cap)
        v = self.t[0:shape[0], self.off:self.off + nb // 2]
        self.off += ne
        self.hi = max(self.hi, self.off)
        if dt == F32:
            v = v.bitcast(F32)
        if len(shape) == 3:
            v = v.rearrange("p (a b) -> p a b", a=shape[1])
        elif len(shape) == 4:
            v = v.rearrange("p (a b c) -> p a b c", a=shape[1], b=shape[2])
        self.n += 1
        return T(v, "%s%d" % (name, self.n))


class Ring:
    def __init__(self, tiles):
        self.tiles = tiles
        self.i = 0

    def next(self):
        t = self.tiles[self.i % len(self.tiles)]
        self.i += 1
        return t


def run_pipe2(gens):
    prev = None
    for g_ in gens:
        next(g_)
        if prev is not None:
            for _ in prev:
                pass
        prev = g_
    if prev is not None:
        for _ in prev:
            pass


def bc(ap, axis, shape):
    return ap.unsqueeze(axis).broadcast_to(list(shape))


def host_consts():
    c = {}
    ii = np.arange(128)
    U = (ii[:, None] <= ii[None, :]).astype(np.float32)
    UT = (ii[:, None] >= ii[None, :]).astype(np.float32)
    nmf = np.where(ii[:, None] <= ii[None, :], 0.0, -30000.0).astype(np.float32)
    nmb = np.where(ii[:, None] >= ii[None, :], 0.0, -30000.0).astype(np.float32)
    ones = np.ones((128, 128), np.float32)
    ident = np.eye(128, dtype=np.float32)
    bd64 = np.zeros((128, 128), np.float32)
    bd64[:64, :64] = 1
    bd64[64:, 64:] = 1
    bd32 = np.zeros((128, 128), np.float32)
    for k in range(4):
        bd32[k * 32:(k + 1) * 32, k * 32:(k + 1) * 32] = 1
    d = np.arange(32)
    dd = d % 16
    j = dd % 8
    partner = np.where(dd < 8, d + 8, d - 8)
    freqs = (10000.0 ** (-(np.arange(0, 16, 2, dtype=np.float32)) / 16.0)).astype(np.float32)
    t = np.arange(2048)
    rows = (t // 64).astype(np.float32)
    cols = (t % 64).astype(np.float32)
    pos = np.where((d // 16)[:, None] == 0, rows[None, :], cols[None, :]).astype(np.float32)
    ang = (pos * freqs[j][:, None]).astype(np.float32)
    cos32 = np.cos(ang).astype(np.float32)
    sin32 = np.sin(ang).astype(np.float32)
    sins32 = np.where((dd < 8)[:, None], -sin32, sin32).astype(np.float32)
    p32 = np.zeros((32, 32), np.float32)
    p32[partner, d] = 1.0
    p96 = np.zeros((128, 128), np.float32)
    p96[64:96, 64:96] = p32
    p64 = np.zeros((128, 128), np.float32)
    p64[0:32, 0:32] = p32
    p64[32:64, 32:64] = p32
    cos96 = np.ones((128, 2048), np.float32)
    sin96 = np.zeros((128, 2048), np.float32)
    cos96[64:96] = cos32
    sin96[64:96] = sins32
    cos96[0:32] = cos32
    cos96[32:64] = cos32
    shift = np.zeros((128, 128), np.float32)
    for i in range(32):
        shift[i, 64 + i] = 1.0
    sel = np.zeros((16, 16, 128), np.float32)
    for h in range(16):
        sel[h, h, :] = 1.0
    c["cf32"] = np.stack([U, UT, nmf, nmb, ones, ident], axis=1).astype(np.float32)
    c["sel"] = sel.reshape(16, 2048)
    c["cbf"] = np.stack([ones, ident, bd64, bd32, p96, p64, shift], axis=1).astype(np.float32)
    sin64 = np.zeros((128, 2048), np.float32)
    sin64[0:32] = sins32
    sin64[32:64] = sins32
    c["rope"] = np.stack([cos96, sin96, sin64], axis=1).astype(np.float32)
    cosd = np.ones((128, 2048), np.float32)
    cosd[0:32] = cos32
    cosd[32:64] = cos32
    cosa = np.ones((128, 2048), np.float32)
    cosa[64:96] = cos32
    c["rope"] = np.stack([cosa, sin96, cosd, sin64], axis=1).astype(np.float32)
    return c


SM_PER = 44
BC_PER = 1184


def prep_core(inp, core, consts):
    f = np.float32
    b = core // 2
    d = {}
    xs = np.concatenate([inp["x_prompt"][2 * core], inp["x_prompt"][2 * core + 1], inp["x_sample"][b]], axis=0)
    d["xT_in"] = np.ascontiguousarray(xs.reshape(NT, 8, 128).transpose(2, 1, 0))
    cv = np.stack([inp["c_ctx"], inp["c"][b]], axis=-1)
    d["cvec"] = np.ascontiguousarray(cv.reshape(8, 128, 2).transpose(1, 0, 2))
    return d


def prep_shared(inp):
    f = np.float32
    d = {}
    d["w_ada"] = np.ascontiguousarray(inp["w_ada"].reshape(NL, 8, 128, 6144).transpose(0, 2, 1, 3))
    d["b_ada"] = np.ascontiguousarray(inp["b_ada"].reshape(NL, 48, 128).transpose(2, 0, 1))
    d["n1g"] = np.ascontiguousarray(inp["norm1_g"].reshape(NL, 8, 128).transpose(2, 0, 1))
    d["n2g"] = np.ascontiguousarray(inp["norm2_g"].reshape(NL, 8, 128).transpose(2, 0, 1))
    w_in = inp["w_in"]
    wfm = np.concatenate([w_in[:, :, 0:928], w_in[:, :, 1696:2464]], axis=2)
    wtm = np.concatenate([w_in[:, :, 928:1696], w_in[:, :, 2464:2480]], axis=2)
    d["w_fm"] = np.ascontiguousarray(wfm.reshape(NL, 8, 128, 1696).transpose(0, 2, 1, 3))
    d["w_tm"] = np.ascontiguousarray(wtm.reshape(NL, 8, 128, 784).transpose(0, 2, 1, 3))
    d["w_uq"] = np.ascontiguousarray(inp["w_uq"].reshape(NL, 2, 128, 384).transpose(0, 2, 1, 3))
    wukv = inp["w_ukv"].reshape(NL, 128, 4, 128)
    kn = wukv[:, :, :, 0:64]
    vv = wukv[:, :, :, 64:128]
    kn96 = np.concatenate([kn, np.zeros((NL, 128, 4, 32), f)], axis=3)
    d["w_ukv"] = np.ascontiguousarray(np.concatenate([kn96.reshape(NL, 128, 384), vv.reshape(NL, 128, 256)], axis=2))
    d["w_out"] = np.ascontiguousarray(inp["w_out"].reshape(NL, 8, 128, 1024).transpose(0, 2, 1, 3))
    d["w_ff1"] = np.ascontiguousarray(inp["w_ff1"].reshape(NL, 8, 128, 4096).transpose(0, 2, 1, 3))
    d["w_ff2"] = np.ascontiguousarray(inp["w_ff2"].reshape(NL, 32, 128, 1024).transpose(0, 2, 1, 3))
    sm = np.zeros((128, NL, SM_PER), f)
    for l in range(NL):
        sm[:, l, 0:2] = inp["mla_q_norm_g"][l].reshape(2, 128).T
        sm[:, l, 2] = inp["mla_kv_norm_g"][l]
        sm[0:96, l, 3] = inp["mla_qk_norm_q"][l]
        sm[0:96, l, 4] = inp["mla_qk_norm_k"][l]
        sm[0:64, l, 5] = np.tile(inp["diff_q_norm_g"][l], 2)
        sm[0:64, l, 6] = np.tile(inp["diff_k_norm_g"][l], 2)
        sm[:, l, 7] = np.tile(inp["diff_subln_g"][l], 2)
        sm[:, l, 8:38] = inp["ssm_conv_w"][l].reshape(5, 6, 128).transpose(2, 1, 0).reshape(128, 30)
        sm[:, l, 38:44] = inp["ssm_conv_b"][l].reshape(6, 128).T
    d["smallp"] = sm
    bcp = np.zeros((NL, BC_PER), f)
    for l in range(NL):
        bcp[l, 0:512] = np.repeat(inp["ssm_D"][l], 64)
        bcp[l, 512:1024] = inp["ssm_norm_g"][l]
        bcp[l, 1024:1040] = inp["ssm_dt_bias"][l].reshape(16)
        bcp[l, 1040:1056] = inp["ssm_A_log"][l].reshape(16)
        bcp[l, 1056:1088] = inp["diff_lq1"][l]
        bcp[l, 1088:1120] = inp["diff_lk1"][l]
        bcp[l, 1120:1152] = inp["diff_lq2"][l]
        bcp[l, 1152:1184] = inp["diff_lk2"][l]
    d["bcp"] = bcp
    return d


def prep_cache(inp, core):
    b = core // 2
    d = {}
    d["c_ckvT"] = np.ascontiguousarray(inp["cache_mla_ckv"][b].transpose(0, 2, 1))
    kr = np.zeros((NL, 128, 256), np.float32)
    kr[:, 0:32, :] = inp["cache_mla_krope"][b].transpose(0, 2, 1)
    d["c_krT"] = kr
    kd = np.zeros((NL, 128, 4, 256), np.float32)
    kd[:, 0:64] = inp["cache_diff_k"][b].transpose(0, 3, 2, 1)
    d["c_kdT"] = kd
    d["c_vd"] = np.ascontiguousarray(inp["cache_diff_v"][b].reshape(NL, 2, 128, 256).transpose(0, 2, 1, 3))
    st = inp["state_ssm"][b]
    st = st.reshape(NL, 2, 2, 4, 64, 64)
    st = st.transpose(0, 1, 2, 5, 3, 4)
    d["st0"] = np.ascontiguousarray(st.reshape(NL, 2, 128, 256))
    return d


def build(depth=NL, debug=()):
    nc = bass.Bass("TRN2", target_bir_lowering=False)

    def din(name, shape):
        return nc.dram_tensor(name, list(shape), F32, kind="ExternalInput").ap()

    def dout(name, shape):
        return nc.dram_tensor(name, list(shape), F32, kind="ExternalOutput").ap()

    xT_in = din("xT_in", [128, 8, NT])
    cvec = din("cvec", [128, 8, 2])
    w_ada = din("w_ada", [NL, 128, 8, 6144])
    b_ada = din("b_ada", [128, NL, 48])
    n1g = din("n1g", [128, NL, 8])
    n2g = din("n2g", [128, NL, 8])
    w_fm = din("w_fm", [NL, 128, 8, 1696])
    w_tm = din("w_tm", [NL, 128, 8, 784])
    w_uq = din("w_uq", [NL, 128, 2, 384])
    w_ukv = din("w_ukv", [NL, 128, 640])
    w_out = din("w_out", [NL, 128, 8, 1024])
    w_ff1 = din("w_ff1", [NL, 128, 8, 4096])
    w_ff2 = din("w_ff2", [NL, 128, 32, 1024])
    smallp = din("smallp", [128, NL, SM_PER])
    bcp = din("bcp", [NL, BC_PER])
    cf32_d = din("cf32", [128, 6, 128])
    sel_d = din("sel", [16, 2048])
    cbf_d = din("cbf", [128, 7, 128])
    rope_d = din("rope", [128, 4, 2048])
    c_ckvT = din("c_ckvT", [NL, 128, 256])
    c_krT = din("c_krT", [NL, 128, 256])
    c_kdT = din("c_kdT", [NL, 128, 4, 256])
    c_vd = din("c_vd", [NL, 128, 2, 256])
    st0 = din("st0", [NL, 2, 128, 256])

    xT_out = dout("xT_out", [128, 8, NT])
    ckv_o = dout("ckv_o", [NL, 128, 512])
    kr_o = dout("kr_o", [NL, 32, 512])
    kd_o = dout("kd_o", [NL, 64, 4, 512])
    vd_o = dout("vd_o", [NL, 128, 4, 256])
    st_o = dout("st_o", [NL, 2, 2, 128, 256])

    P = Prog(nc)
    dbg_outs = {}

    cf32 = T(P.sb("cf32s", [128, 6, 128], F32)[:], "cf32")
    selS = T(P.sb("selS", [16, 2048], F32)[:], "sel")
    cbf = T(P.sb("cbfs", [128, 7, 128], BF16)[:], "cbf")
    smp = T(P.sb("smp", [128, NL, SM_PER], F32)[:], "smp")
    modT = T(P.sb("modT", [128, NL, 6, 8, 2], F32)[:], "modT")
    gsT = T(P.sb("gsT", [128, NL, 2, 8, 2], F32)[:], "gsT")
    bcl = T(P.sb("bcl", [128, BC_PER], F32)[:], "bcl")
    lay = T(P.sb("lay", [128, 64], F32)[:], "lay")
    U_f = cf32[:, 0, :]
    UT_f = cf32[:, 1, :]
    NMF = cf32[:, 2, :]
    NMB = cf32[:, 3, :]
    ONES_f = cf32[:, 4, :]
    ID_f = cf32[:, 5, :]
    ONES_b = cbf[:, 0, :]
    ID_b = cbf[:, 1, :]
    BD64 = cbf[:, 2, :]
    BD32 = cbf[:, 3, :]
    P96 = cbf[:, 4, :]
    P64 = cbf[:, 5, :]
    SHIFT = cbf[:, 6, :]

    rotbig = P.ps("rotbanks", [128, 4, 512], F32)
    banks = [T(rotbig[:, i, :], "bank%d" % i) for i in range(4)]
    banks += [T(P.ps("bank%d" % i, [128, 512], F32)[:], "bank%d" % i) for i in range(4, 8)]
    rot = Ring(banks[0:4])
    acc = banks[4:8]

    def nb():
        return rot.next()

    arena = Arena(P, "arena", 184 * 1024)
    mixD = nc.dram_tensor("mixD", [128, 8, NT], BF16, kind="Internal").ap()
    mixbufs = [Buf("mixD%d" % i) for i in range(5)]

    xbufs = [Buf("xres%d" % i) for i in range(5)]

    def mm(out, lhsT, rhs, start, stop, reads, writes):
        P.op("pe", lambda e: e.matmul(out, lhsT=lhsT, rhs=rhs, start=start, stop=stop), reads, writes)

    def tr(out, in_, ident, reads, writes):
        P.op("pe", lambda e: e.transpose(out, in_, ident), reads, writes)

    def act(out, in_, func, reads, writes, bias=None, scale=None, accum=None):
        kw = {}
        if bias is not None:
            kw["bias"] = bias
        if scale is not None:
            kw["scale"] = scale
        if accum is not None:
            kw["accum_out"] = accum
        P.op("act", lambda e: e.activation(out=out, in_=in_, func=func, **kw), reads, writes)

    def tt(eng, out, in0, in1, op, reads, writes):
        P.op(eng, lambda e: e.tensor_tensor(out=out, in0=in0, in1=in1, op=op), reads, writes)

    def ts(eng, out, in0, s1, s2, op0, op1, reads, writes):
        if s2 is None:
            P.op(eng, lambda e: e.tensor_scalar(out=out, in0=in0, scalar1=s1, scalar2=None, op0=op0), reads, writes)
        else:
            P.op(eng, lambda e: e.tensor_scalar(out=out, in0=in0, scalar1=s1, scalar2=s2, op0=op0, op1=op1), reads, writes)

    def stt(out, in0, scalar, in1, op0, op1, reads, writes):
        P.op("dve", lambda e: e.scalar_tensor_tensor(out=out, in0=in0, scalar=scalar, in1=in1, op0=op0, op1=op1), reads, writes)

    def cp(eng, out, in_, reads, writes):
        if eng == "act":
            P.op("act", lambda e: e.copy(out=out, in_=in_), reads, writes)
        else:
            P.op(eng, lambda e: e.tensor_copy(out=out, in_=in_), reads, writes)

    def memset(eng, ap, val, writes):
        P.op(eng, lambda e: e.memset(ap, val), (), writes)

    def rstd_act(out, in_, D, reads, writes):
        act(out, in_, AF.Ln, reads, writes, scale=1.0 / D, bias=epsT[0:out.shape[0], 0:1])
        act(out, out, AF.Exp, writes, writes, scale=-0.5)

    def dbg(name, t, ap=None):
        if name not in debug:
            return
        ap = t.ap if ap is None else ap
        shp = list(ap.shape)
        o = nc.dram_tensor("dbg_" + name, shp, ap.dtype, kind="ExternalOutput").ap()
        dbg_outs[name] = shp
        tmpb = Buf("dbg_" + name)
        P.dma(tmpb, [("sp", o, ap)], reads=[t], is_out=True)

    epsT_t = T(P.sb("epsT", [128, 4], F32)[:], "epsT")
    epsT = epsT_t.ap
    memset("dve", epsT[:, 0:1], EPS, [epsT_t])
    memset("dve", epsT[:, 1:2], 1.0, [epsT_t])
    P.dma(cf32, [("sp", cf32.ap, cf32_d)], writes=[cf32])
    P.dma(selS, [("sp", selS.ap, sel_d)], writes=[selS])
    P.dma(cbf, [("pool", cbf.ap, cbf_d)], writes=[cbf])
    P.dma(smp, [("sp", smp.ap, smallp)], writes=[smp])

    arena.reset()
    cvs = arena.get([128, 8, 2], F32, "cvs")
    csb = T(P.sb("csb", [128, 8, 2], BF16)[:], "csb")
    badaS = T(P.sb("badaS", [128, NL, 48], F32)[:], "bada")
    ngS = T(P.sb("ngS", [128, 2, NL, 8], F32)[:], "ngS")
    P.dma(cvs, [("sp", cvs.ap, cvec)], writes=[cvs])
    P.dma(badaS, [("sp", badaS.ap, b_ada)], writes=[badaS])
    P.dma(ngS, [("sp", ngS[:, 0], n1g), ("sp", ngS[:, 1], n2g)], writes=[ngS])
    act(csb.ap, cvs.ap, AF.Silu, [cvs], [csb])

    def mod_group(lm, ng, wring_):
        w = wring_.next()
        P.dma(w, [("pool", w.ap, w_ada[lm][:, :, ng * 512:(ng + 1) * 512])], writes=[w])
        bk = nb()
        for j in range(4):
            for kc in range(8):
                mm(bk[:, 2 * j:2 * j + 2], w[:, kc, j * 128:(j + 1) * 128], csb[:, kc, :], kc == 0, kc == 7, [w, csb], [bk])
        m6, c0 = divmod(ng * 4, 8)
        tt("dve", modT[:, lm, m6, c0:c0 + 4, :], bk[:, 0:8].rearrange("p (j g) -> p j g", g=2),
           bc(badaS[:, lm, ng * 4:ng * 4 + 4], 2, [128, 4, 2]), ALU.add, [bk, badaS], [modT])

    def mod_finish(lm):
        for which, mi in ((0, 1), (1, 4)):
            ts("dve", gsT[:, lm, which], modT[:, lm, mi], 1.0, None, ALU.add, None, [modT], [gsT])
            tt("dve", gsT[:, lm, which], gsT[:, lm, which], bc(ngS[:, which, lm, :], 2, [128, 8, 2]), ALU.mult, [gsT, ngS], [gsT])

    wring = Ring([arena.get([128, 8, 512], BF16, "wada") for _ in range(3)])
    try:
        ckpt(0)
        for ng in range(12):
            mod_group(0, ng, wring)
        mod_finish(0)
        dbg("modT", modT)
        P.barrier()

        def norm_mod(xt, hdst, hdst_t, l, which, g, sq, rs_ring, tmp_ring):
            act(sq.ap, xt.ap, AF.Square, [xt], [sq])
            bk = nb()
            for c in range(8):
                mm(bk.ap, ONES_b, sq[:, c, :], c == 0, c == 7, [sq, cbf], [bk])
            rs = rs_ring.next()
            rstd_act(rs.ap, bk.ap, 1024.0, [bk], [rs])
            sh = 0 if which == 0 else 3
            for c in range(8):
                tm = tmp_ring.next()
                tt("dve", tm.ap, xt[:, c, :], rs.ap, ALU.mult, [xt, rs], [tm])
                act(hdst[:, c, :], tm.ap, AF.Identity, [tm, gsT, modT], [hdst_t],
                    scale=gsT[:, l, which, c, g:g + 1], bias=modT[:, l, sh, c, g:g + 1])

        for l in range(depth):
            xsrc = xT_in if l == 0 else xT_out
            lam_init = 0.8 - 0.6 * math.exp(-0.3 * l)
            P.barrier()
            arena.reset()
            P.dma(bcl, [("sp", bcl.ap, bcp[l:l + 1, :].partition_broadcast(128))], writes=[bcl])
            Dbc = bcl[:, 0:512]
            NGbc = bcl[:, 512:1024]
            dtb = bcl[:, 1024:1040]
            act(lay[:, 0:16], bcl[:, 1040:1056], AF.Exp, [bcl], [lay])
            ts("dve", lay[:, 0:16], lay[:, 0:16], -1.0, None, ALU.mult, None, [lay], [lay])
            aneg = lay[:, 0:16]
            tt("dve", lay[:, 32:64], bcl[:, 1056:1088], bcl[:, 1088:1120], ALU.mult, [bcl], [lay])
            P.op("dve", lambda e: e.reduce_sum(out=lay[:, 16:17], in_=lay[:, 32:64], axis=AX.X), [lay], [lay])
            tt("dve", lay[:, 32:64], bcl[:, 1120:1152], bcl[:, 1152:1184], ALU.mult, [bcl], [lay])
            P.op("dve", lambda e: e.reduce_sum(out=lay[:, 17:18], in_=lay[:, 32:64], axis=AX.X), [lay], [lay])
            act(lay[:, 16:18], lay[:, 16:18], AF.Exp, [lay], [lay])
            tt("dve", lay[:, 18:19], lay[:, 17:18], lay[:, 16:17], ALU.subtract, [lay], [lay])
            ts("dve", lay[:, 18:19], lay[:, 18:19], -lam_init, None, ALU.add, None, [lay], [lay])
            nlam = lay[:, 18:19]
            ts("dve", lay[:, 19:20], smp[:, l, 7:8], 1.0 - lam_init, None, ALU.mult, None, [smp], [lay])
            subg = lay[:, 19:20]

            wuq = arena.get([128, 2, 384], BF16, "wuq")
            wukv = arena.get([128, 640], BF16, "wukv")
            wtm = arena.get([128, 8, 784], BF16, "wtm")
            P.dma(wuq, [("pool", wuq.ap, w_uq[l])], writes=[wuq])
            P.dma(wukv, [("pool", wukv.ap, w_ukv[l])], writes=[wukv])
            P.dma(wtm, [("pool", wtm.ap, w_tm[l])], writes=[wtm])
            base_off = arena.off

            for (t0, n, nseq, L, ctx, g) in ((0, 512, 2, 256, False, 0), (512, 2048, 1, 2048, True, 1)):
                P.barrier()
                arena.off = base_off
                ntile = n // 512
                nch = n // 128
                nchs = L // 128
                hT = arena.get([128, 8, n], BF16, "hT")
                offA = arena.off
                xring = Ring([arena.get([128, 8, 512], F32, "xt") for _ in range(2)])
                sqA = arena.get([128, 8, 512], BF16, "sq")
                rsr = Ring([arena.get([128, 512], F32, "rs") for _ in range(2)])
                tmr = Ring([arena.get([128, 512], F32, "tm") for _ in range(2)])
                for tt_ in range(ntile):
                    ta = t0 + tt_ * 512
                    xt = xring.next()
                    xb = xbufs[ta // 512]
                    P.dma(xt, [("sp", xt.ap, xsrc[:, :, ta:ta + 512])], reads=[xb], writes=[xt])
                    norm_mod(xt, hT[:, :, tt_ * 512:(tt_ + 1) * 512], hT, l, 0, g, sqA, rsr, tmr)
                if l == 0 and t0 == 0:
                    dbg("hT", hT)

                ckpt(1)
                P.barrier()
                arena.off = offA
                xtok = arena.get([128, nch, 512], BF16, "xtok")
                CT = arena.get([128, n], BF16, "CT")
                dtall = arena.get([128, nch, 16], F32, "dtall")
                acum = arena.get([128, nch, 16], F32, "acum")
                hpf = arena.get([128, nch, 512], BF16, "hpf")
                hpb = arena.get([128, nch, 512], BF16, "hpb")
                BTz = arena.get([128, 2, n], BF16, "BTz")
                memset("pool", hpf.ap, 0.0, [hpf])
                memset("pool", hpb.ap, 0.0, [hpb])
                memset("pool", BTz.ap, 0.0, [BTz])
                offB2 = arena.off
                Btok = arena.get([128, nch, 128], BF16, "Btok")
                BT = arena.get([128, n], BF16, "BT")
                aall = arena.get([128, nch, 16], F32, "aall")
                cdall = arena.get([128, nch, 16], F32, "cdall")
                SnB = arena.get([128, nch, 256], BF16, "SnB")
                stF = arena.get([128, 256], F32, "stF")
                stB = arena.get([128, 256], F32, "stB")
                offB = arena.off
                wxr = Ring([arena.get([128, 8, 128], BF16, "wx") for _ in range(2)])
                prer = Ring([arena.get([128, nseq, L + 4], BF16, "pre") for _ in range(2)])
                accr = Ring([arena.get([128, nseq, L], F32, "cacc") for _ in range(2)])
                xcr = Ring([arena.get([128, n], BF16, "xc") for _ in range(2)])
                def conv_item(c6):
                    wx = wxr.next()
                    P.dma(wx, [("pool", wx.ap, w_fm[l][:, :, 928 + c6 * 128:928 + (c6 + 1) * 128])], writes=[wx])
                    pre = prer.next()
                    memset("pool", pre[:, :, 0:2], 0.0, [pre])
                    memset("pool", pre[:, :, L + 2:L + 4], 0.0, [pre])
                    for tt_ in range(ntile):
                        bk = nb()
                        for kc in range(8):
                            mm(bk.ap, wx[:, kc, :], hT[:, kc, tt_ * 512:(tt_ + 1) * 512], kc == 0, kc == 7, [wx, hT], [bk])
                        if L >= 512:
                            s_, o_ = divmod(tt_ * 512, L)
                            cp("act", pre[:, s_, 2 + o_:2 + o_ + 512], bk.ap, [bk], [pre])
                        else:
                            k_ = 512 // L
                            cp("act", pre[:, tt_ * k_:(tt_ + 1) * k_, 2:2 + L], bk.ap.rearrange("p (s t) -> p s t", s=k_), [bk], [pre])
                    yield
                    ca = accr.next()
                    ts("dve", ca.ap, pre[:, :, 0:L], smp[:, l, 8 + c6 * 5:9 + c6 * 5], None, ALU.mult, None, [pre, smp], [ca])
                    for k in range(1, 5):
                        stt(ca.ap, pre[:, :, k:k + L], smp[:, l, 8 + c6 * 5 + k:9 + c6 * 5 + k], ca.ap, ALU.mult, ALU.add, [pre, smp, ca], [ca])
                    caf = ca.ap.rearrange("p s t -> p (s t)")
                    if c6 < 4:
                        xc = xcr.next()
                        dstT = xc
                    elif c6 == 4:
                        dstT = BT
                    else:
                        dstT = CT
                    act(dstT.ap, caf, AF.Silu, [ca, smp], [dstT], bias=smp[:, l, 38 + c6:39 + c6])
                    if c6 <= 4:
                        for ch0 in range(0, nch, 4):
                            bk = nb()
                            bkb = bk.ap.bitcast(BF16)
                            for q in range(4):
                                tr(bkb[:, q * 128:(q + 1) * 128], dstT[:, (ch0 + q) * 128:(ch0 + q + 1) * 128], ID_b, [dstT, cbf], [bk])
                            src = bkb[:, 0:512].rearrange("p (q f) -> p q f", q=4)
                            if c6 < 4:
                                cp("dve", xtok[:, ch0:ch0 + 4, c6 * 128:(c6 + 1) * 128], src, [bk], [xtok])
                            else:
                                cp("dve", Btok[:, ch0:ch0 + 4, :], src, [bk], [Btok])
                run_pipe2([conv_item(c6_) for c6_ in range(6)])
                if l == 0 and t0 == 0:
                    dbg("xtok", xtok)
                    dbg("CT", CT)
                ckpt(2)
                P.barrier()
                arena.off = offB
                cp("act", BTz[0:64, 0, :], BT[0:64, :], [BT], [BTz])
                cp("act", BTz[64:128, 1, :], BT[64:128, :], [BT], [BTz])
                xddr = Ring([arena.get([128, 2, 512], BF16, "xdd") for _ in range(2)])
                tmpD = arena.get([128, nch, 16], F32, "tmpD")
                wdd = arena.get([128, nch, 16], F32, "wdd")
                bkd = nb()
                for ch in range(nch):
                    for kc in range(8):
                        mm(bkd[:, ch * 16:(ch + 1) * 16], hT[:, kc, ch * 128:(ch + 1) * 128], wtm[:, kc, 768:784], kc == 0, kc == 7, [hT, wtm], [bkd])
                bkd3 = bkd[:, 0:nch * 16].rearrange("p (c k) -> p c k", k=16)
                tt("dve", tmpD.ap, bkd3, bc(dtb, 1, [128, nch, 16]), ALU.add, [bkd, bcl], [tmpD])
                act(tmpD.ap, tmpD.ap, AF.Exp, [tmpD], [tmpD])
                act(dtall.ap, tmpD.ap, AF.Ln, [tmpD], [dtall], bias=epsT[:, 1:2])
                tt("dve", aall.ap, dtall.ap, bc(aneg, 1, [128, nch, 16]), ALU.mult, [dtall, lay], [aall])
                bkc2 = nb()
                for ch in range(nch):
                    mm(bkc2[:, ch * 32:ch * 32 + 8], U_f, aall[:, ch, 0:8], True, True, [cf32, aall], [bkc2])
                    mm(bkc2[:, ch * 32 + 8:ch * 32 + 16], UT_f, aall[:, ch, 8:16], True, True, [cf32, aall], [bkc2])
                    mm(bkc2[:, ch * 32 + 16:ch * 32 + 32], ONES_f, aall[:, ch, :], True, True, [cf32, aall], [bkc2])
                bkc3 = bkc2[:, 0:nch * 32].rearrange("p (c k) -> p c k", k=32)
                cp("dve", acum.ap, bkc3[:, :, 0:16], [bkc2], [acum])
                tt("dve", wdd.ap, bkc3[:, :, 16:32], acum.ap, ALU.subtract, [bkc2, acum], [wdd])
                act(wdd.ap, wdd.ap, AF.Exp, [wdd], [wdd])
                act(cdall.ap, bkc3[:, :, 16:32], AF.Exp, [bkc2], [cdall])
                tt("dve", wdd.ap, wdd.ap, dtall.ap, ALU.mult, [wdd, dtall], [wdd])

                def passA_item(ch):
                    s_, cl = divmod(ch, nchs)
                    xdd = xddr.next()
                    for d_ in range(2):
                        tt("dve", xdd[:, d_, :].rearrange("p (h q) -> p h q", h=8), xtok[:, ch, :].rearrange("p (h q) -> p h q", h=8),
                           bc(wdd[:, ch, d_ * 8:(d_ + 1) * 8], 2, [128, 8, 64]), ALU.mult, [xtok, wdd], [xdd])
                    bk3 = nb()
                    mm(bk3.ap, Btok[:, ch, :], xdd[:, 0, :], True, True, [Btok, xdd], [bk3])
                    bk4 = nb()
                    mm(bk4.ap, Btok[:, ch, :], xdd[:, 1, :], True, True, [Btok, xdd], [bk4])
                    yield
                    if cl == 0:
                        if ctx:
                            P.dma(stF, [("sp", stF.ap, st0[l, 0])], writes=[stF])
                        else:
                            memset("dve", stF.ap, 0.0, [stF])
                    cp("act", hpf[0:64, ch, 0:256], stF[0:64, :], [stF], [hpf])
                    cp("act", hpf[64:128, ch, 256:512], stF[64:128, :], [stF], [hpf])
                    for g2 in range(2):
                        rws = slice(g2 * 64, (g2 + 1) * 64)
                        v = stF[rws, :].rearrange("p (h q) -> p h q", h=4)
                        tt("dve", v, v, bc(cdall[rws, ch, g2 * 4:(g2 + 1) * 4], 2, [64, 4, 64]), ALU.mult, [stF, cdall], [stF])
                        tt("dve", stF[rws, :], stF[rws, :], bk3[rws, g2 * 256:(g2 + 1) * 256], ALU.add, [stF, bk3], [stF])
                        cp("act", SnB[rws, ch, :], bk4[rws, g2 * 256:(g2 + 1) * 256], [bk4], [SnB])
                    if cl == nchs - 1 and not ctx:
                        P.dma(stF, [("sp", st_o[l, s_, 0], stF.ap)], reads=[stF], is_out=True)
                run_pipe2([passA_item(ch_) for ch_ in range(nch)])
                ckpt(3)
                for s_ in range(nseq):
                    for cl in range(nchs - 1, -1, -1):
                        ch = s_ * nchs + cl
                        if cl == nchs - 1:
                            if ctx:
                                P.dma(stB, [("sp", stB.ap, st0[l, 1])], writes=[stB])
                            else:
                                memset("dve", stB.ap, 0.0, [stB])
                        cp("act", hpb[0:64, ch, 0:256], stB[0:64, :], [stB], [hpb])
                        cp("act", hpb[64:128, ch, 256:512], stB[64:128, :], [stB], [hpb])
                        for g2 in range(2):
                            rws = slice(g2 * 64, (g2 + 1) * 64)
                            v = stB[rws, :].rearrange("p (h q) -> p h q", h=4)
                            tt("dve", v, v, bc(cdall[rws, ch, 8 + g2 * 4:8 + (g2 + 1) * 4], 2, [64, 4, 64]), ALU.mult, [stB, cdall], [stB])
                            tt("dve", stB[rws, :], stB[rws, :], SnB[rws, ch, :], ALU.add, [stB, SnB], [stB])
                        if cl == 0 and not ctx:
                            P.dma(stB, [("sp", st_o[l, s_, 1], stB.ap)], reads=[stB], is_out=True)
                ckpt(4)
                P.barrier()
                arena.off = offB2
                szr = Ring([arena.get([128, 512], F32, "sz") for _ in range(2)])
                acTr = Ring([arena.get([16, 128], F32, "acT") for _ in range(2)])
                LTp = arena.get([128, 16, 128], F32, "LTp")
                LTr = Ring([arena.get([128, 16, 128], BF16, "LT") for _ in range(2)])
                Ear = Ring([arena.get([128, 16, 128], BF16, "Ea") for _ in range(2)])
                scr = Ring([arena.get([128, 2, 8, 128], BF16, "sc") for _ in range(2)])
                Csr = Ring([arena.get([128, 16, 128], BF16, "Cs") for _ in range(2)])
                xdtr = Ring([arena.get([128, 2, 512], BF16, "xdt") for _ in range(2)])
                y1r = Ring([arena.get([128, 512], F32, "y1") for _ in range(2)])
                y2r = Ring([arena.get([128, 512], F32, "y2") for _ in range(2)])
                ybr = Ring([arena.get([128, 512], BF16, "yb") for _ in range(2)])
                junk = arena.get([128, 512], BF16, "junk")
                s1r = Ring([arena.get([128, 2], F32, "s1") for _ in range(4)])
                ocr = Ring([arena.get([128, 4, 128], BF16, "ocs") for _ in range(2)])
                def ssd_s1(ch):
                    bkz = nb()
                    for kc in range(8):
                        mm(bkz.ap, hT[:, kc, ch * 128:(ch + 1) * 128], wtm[:, kc, 256:768], kc == 0, kc == 7, [hT, wtm], [bkz])
                    sz = szr.next()
                    act(sz.ap, bkz.ap, AF.Silu, [bkz], [sz])
                    bkt = nb()
                    tr(bkt[0:16, 0:128], acum[:, ch, :], ID_f, [acum, cf32], [bkt])
                    acT = acTr.next()
                    cp("dve", acT.ap, bkt[0:16, 0:128], [bkt], [acT])
                    for hd in range(16):
                        b_ = acc[hd // 4]
                        mm(b_[:, (hd % 4) * 128:(hd % 4 + 1) * 128], selS[0:16, hd * 128:(hd + 1) * 128], acT.ap, True, True, [selS, acT], [b_])
                    ckpt(4.1)
                    for hd in range(16):
                        b_ = acc[hd // 4]
                        stt(LTp[:, hd, :], b_[:, (hd % 4) * 128:(hd % 4 + 1) * 128], acum[:, ch, hd:hd + 1], NMF if hd < 8 else NMB,
                            ALU.subtract, ALU.add, [b_, acum, cf32], [LTp])
                    LT = LTr.next()
                    act(LT.ap, LTp.ap, AF.Exp, [LTp], [LT])
                    Ea = Ear.next()
                    for q in range(4):
                        act(Ea[:, q * 4:(q + 1) * 4, :], acc[q].ap.rearrange("p (a b) -> p a b", a=4), AF.Exp, [acc[q]], [Ea])
                    ckpt(4.2)
                    bkc = nb()
                    for g2 in range(2):
                        mm(bkc[:, g2 * 128:(g2 + 1) * 128], BTz[:, g2, ch * 128:(ch + 1) * 128], CT[:, ch * 128:(ch + 1) * 128], True, True, [BTz, CT], [bkc])
                    sc = scr.next()
                    for d_ in range(2):
                        for g2 in range(2):
                            tt("dve", sc[:, d_, g2 * 4:(g2 + 1) * 4, :], bc(bkc[:, g2 * 128:(g2 + 1) * 128], 1, [128, 4, 128]),
                               LT[:, d_ * 8 + g2 * 4:d_ * 8 + (g2 + 1) * 4, :], ALU.mult, [bkc, LT], [sc])
                    Cs = Csr.next()
                    tt("dve", Cs.ap, bc(CT[:, ch * 128:(ch + 1) * 128], 1, [128, 16, 128]), Ea.ap, ALU.mult, [CT, Ea], [Cs])
                    ckpt(4.3)
                    xdt = xdtr.next()
                    for d_ in range(2):
                        tt("dve", xdt[:, d_, :].rearrange("p (h q) -> p h q", h=8), xtok[:, ch, :].rearrange("p (h q) -> p h q", h=8),
                           bc(dtall[:, ch, d_ * 8:(d_ + 1) * 8], 2, [128, 8, 64]), ALU.mult, [xtok, dtall], [xdt])
                    return sz, sc, Cs, xdt

                def ssd_s2(ch, carry):
                    sz, sc, Cs, xdt = carry
                    bky = nb()
                    for h in range(8):
                        g2, hl = divmod(h, 4)
                        rws = slice(g2 * 64, (g2 + 1) * 64)
                        o_ = bky[:, h * 64:(h + 1) * 64]
                        mm(o_, sc[:, 0, h, :], xdt[:, 0, h * 64:(h + 1) * 64], True, False, [sc, xdt], [bky])
                        mm(o_, sc[:, 1, h, :], xdt[:, 1, h * 64:(h + 1) * 64], False, False, [sc, xdt], [bky])
                        mm(o_, Cs[:, h, :], hpf[:, ch, h * 64:(h + 1) * 64], False, False, [Cs, hpf], [bky])
                        mm(o_, Cs[:, 8 + h, :], hpb[:, ch, h * 64:(h + 1) * 64], False, True, [Cs, hpb], [bky])
                    ckpt(4.4)
                    y1 = y1r.next()
                    tt("pool", y1.ap, xtok[:, ch, :], Dbc, ALU.mult, [xtok, bcl], [y1])
                    tt("dve", y1.ap, bky.ap, y1.ap, ALU.add, [bky, y1], [y1])
                    y2 = y2r.next()
                    tt("dve", y2.ap, y1.ap, sz.ap, ALU.mult, [y1, sz], [y2])
                    ckpt(4.5)
                    s1 = s1r.next()
                    act(junk.ap, y2.ap, AF.Square, [y2], [junk, s1], accum=s1[:, 0:1])
                    rstd_act(s1[:, 1:2], s1[:, 0:1], 512.0, [s1], [s1])
                    yb = ybr.next()
                    stt(yb.ap, y2.ap, s1[:, 1:2], NGbc, ALU.mult, ALU.mult, [y2, s1, bcl], [yb])
                    bk = nb()
                    bkb = bk.ap.bitcast(BF16)
                    for q in range(4):
                        tr(bkb[:, q * 128:(q + 1) * 128], yb[:, q * 128:(q + 1) * 128], ID_b, [yb, cbf], [bk])
                    ckpt(4.6)
                    ocs = ocr.next()
                    cp("act", ocs.ap, bkb[:, 0:512].rearrange("p (q f) -> p q f", q=4), [bk], [ocs])
                    ta_ = t0 + ch * 128
                    P.dma(ocs, [("sp", mixD[:, 4:8, ta_:ta_ + 128], ocs.ap)], reads=[ocs], writes=[mixbufs[ta_ // 512]])

                carry_ = {}
                for ch in range(nch + 1):
                    if ch < nch:
                        carry_[ch] = ssd_s1(ch)
                    if ch >= 1:
                        ssd_s2(ch - 1, carry_.pop(ch - 1))
                ckpt(5)
                P.barrier()
                arena.off = offA
                nk = L + (256 if ctx else 0)
                nkt = nk // 128
                koff = 256 if ctx else 0
                wat = arena.get([128, 8, 928], BF16, "wat")
                P.dma(wat, [("pool", wat.ap, w_fm[l][:, :, 0:928])], writes=[wat])
                rope = arena.get([128, 4, 2048], F32, "rope") if False else None
                ropeA = arena.get([128, 2, 2048], F32, "ropeA")
                if ctx:
                    P.dma(ropeA, [("sp", ropeA.ap, rope_d[:, 0:2, :])], writes=[ropeA])
                rope = ropeA
                rbase = 0
                qaT = arena.get([128, 4, n], BF16, "qaT")
                kaT = arena.get([128, 4, nseq * nk], BF16, "kaT")
                vaA = arena.get([128, nseq * nkt, 4, 128], BF16, "vaA")
                cqn = arena.get([128, 2, 512], BF16, "cqn")
                ckvn = arena.get([128, nseq * nk], BF16, "ckvn")
                krb = arena.get([128, nseq * nk], BF16, "krb")
                offC = arena.off
                memset("pool", vaA.ap, 1.0, [vaA])
                ckvf_r = Ring([arena.get([128, 512], F32, "ckvf") for _ in range(2)])
                krf_r = Ring([arena.get([128, 512], F32, "krf") for _ in range(2)])
                sqr = Ring([arena.get([128, 2, 512], BF16, "sqc") for _ in range(2)])
                rsr = Ring([arena.get([128, 512], F32, "rs") for _ in range(2)])
                cqf_r = Ring([arena.get([128, 2, 512], F32, "cqf") for _ in range(1)])
                qn_r = Ring([arena.get([128, 512], F32, "qn") for _ in range(2)])
                qnb_r = Ring([arena.get([128, 512], BF16, "qnb") for _ in range(2)])
                t1_r = Ring([arena.get([128, 512], F32, "t1") for _ in range(2)])
                t2_r = Ring([arena.get([128, 512], F32, "t2") for _ in range(2)])
                cpyr = [Ring([arena.get([128, 512], F32, "cpy") for _ in range(2)])]

                ckpt(5.1)
                def keycol(s_, tl, w):
                    return slice(s_ * nk + koff + tl, s_ * nk + koff + tl + w)

                if ctx:
                    cst = ckvf_r.next()
                    P.dma(cst, [("sp", cst[:, 0:256], c_ckvT[l])], writes=[cst])
                    cp("act", ckvn[:, 0:256], cst[:, 0:256], [cst], [ckvn])
                    kst = krf_r.next()
                    P.dma(kst, [("sp", kst[:, 0:256], c_krT[l])], writes=[kst])
                    cp("act", krb[0:32, 0:256], kst[0:32, 0:256], [kst], [krb])

                def run_pipe(gens):
                    prev = None
                    for g_ in gens:
                        next(g_)
                        if prev is not None:
                            for _ in prev:
                                pass
                        prev = g_
                    if prev is not None:
                        for _ in prev:
                            pass

                def norm_item(emit_src, rows, lhsT_ones, D, gcol, ncol, plain, roped, P_l=None, post=None):
                    bk = emit_src()
                    sq = sqr.next()
                    cpy = cpyr[0].next()
                    cp("dve", cpy[0:rows, 0:ncol], bk[0:rows, 0:ncol], [bk], [cpy])
                    act(sq[0:rows, 0, 0:ncol], cpy[0:rows, 0:ncol], AF.Square, [cpy], [sq])
                    bk2 = nb()
                    mm(bk2[0:rows, 0:ncol], lhsT_ones, sq[0:rows, 0, 0:ncol], True, True, [sq, cbf], [bk2])
                    yield
                    rs = rsr.next()
                    rstd_act(rs[0:rows, 0:ncol], bk2[0:rows, 0:ncol], D, [bk2], [rs])
                    if not roped:
                        for (c0_, w_, dst_ap, dst_t) in plain:
                            stt(dst_ap, cpy[0:rows, c0_:c0_ + w_], gcol, rs[0:rows, c0_:c0_ + w_], ALU.mult, ALU.mult, [cpy, rs, smp], [dst_t])
                    else:
                        qn = qnb_r.next()
                        stt(qn[0:rows, 0:ncol], cpy[0:rows, 0:ncol], gcol, rs[0:rows, 0:ncol], ALU.mult, ALU.mult, [cpy, rs, smp], [qn])
                        for (c0_, w_, dst_ap, dst_t) in plain:
                            cp("act", dst_ap, qn[0:rows, c0_:c0_ + w_], [qn], [dst_t])
                        for (c0_, w_, p0_, dst_ap, dst_t) in roped:
                            bk3 = nb()
                            mm(bk3[0:rows, 0:w_], P_l, qn[0:rows, c0_:c0_ + w_], True, True, [qn, cbf], [bk3])
                            t1 = t1_r.next()
                            tt("pool", t1[0:rows, 0:w_], qn[0:rows, c0_:c0_ + w_], rope[0:rows, 0, p0_:p0_ + w_], ALU.mult, [qn, rope], [t1])
                            t2 = t2_r.next()
                            tt("dve", t2[0:rows, 0:w_], bk3[0:rows, 0:w_], rope[0:rows, 1, p0_:p0_ + w_], ALU.mult, [bk3, rope], [t2])
                            tt("dve", dst_ap, t1[0:rows, 0:w_], t2[0:rows, 0:w_], ALU.add, [t1, t2], [dst_t])
                    if post is not None:
                        post()

                def tile_segs(tt_):
                    if L >= 512:
                        s_, o_ = divmod(tt_ * 512, L)
                        return [(s_, o_, 0, 512)]
                    k_ = 512 // L
                    return [(tt_ * k_ + i, 0, i * L, L) for i in range(k_)]

                def cq_item(tt_):
                    tsl = slice(tt_ * 512, (tt_ + 1) * 512)
                    cqf = cqf_r.next()
                    sq = sqr.next()
                    for c in range(2):
                        bk = nb()
                        for kc in range(8):
                            mm(bk.ap, wat[:, kc, c * 128:(c + 1) * 128], hT[:, kc, tsl], kc == 0, kc == 7, [wat, hT], [bk])
                        cp("dve", cqf[:, c, :], bk.ap, [bk], [cqf])
                        act(sq[:, c, :], cqf[:, c, :], AF.Square, [cqf], [sq])
                    bk = nb()
                    for c in range(2):
                        mm(bk.ap, ONES_b, sq[:, c, :], c == 0, c == 1, [sq, cbf], [bk])
                    yield
                    rs = rsr.next()
                    rstd_act(rs.ap, bk.ap, 256.0, [bk], [rs])
                    for c in range(2):
                        stt(cqn[:, c, :], cqf[:, c, :], smp[:, l, c:c + 1], rs.ap, ALU.mult, ALU.mult, [cqf, rs, smp], [cqn])

                def ckv_item(tt_):
                    tsl = slice(tt_ * 512, (tt_ + 1) * 512)
                    ckvf = ckvf_r.next()

                    def src():
                        bk = nb()
                        for kc in range(8):
                            mm(bk.ap, wat[:, kc, 256:384], hT[:, kc, tsl], kc == 0, kc == 7, [wat, hT], [bk])
                        return bk

                    def post():
                        for (s_, o_, c0, w_) in tile_segs(tt_):
                            cp("act", ckvn[:, keycol(s_, o_, w_)], ckvf[:, c0:c0 + w_], [ckvf], [ckvn])
                        if not ctx:
                            P.dma(ckvf, [("sp", ckv_o[l][:, t0 + tt_ * 512:t0 + (tt_ + 1) * 512], ckvf.ap)], reads=[ckvf], is_out=True)
                    return norm_item(src, 128, ONES_b, 128.0, smp[:, l, 2:3], 512, [(0, 512, ckvf.ap, ckvf)], [], post=post)

                def kr_item(tt_):
                    tsl = slice(tt_ * 512, (tt_ + 1) * 512)
                    bk = nb()
                    for kc in range(8):
                        mm(bk[0:32, :], wat[:, kc, 384:416], hT[:, kc, tsl], kc == 0, kc == 7, [wat, hT], [bk])
                    krf = krf_r.next()
                    cp("dve", krf[0:32, :], bk[0:32, :], [bk], [krf])
                    yield
                    for (s_, o_, c0, w_) in tile_segs(tt_):
                        cp("act", krb[0:32, keycol(s_, o_, w_)], krf[0:32, c0:c0 + w_], [krf], [krb])
                    if not ctx:
                        P.dma(krf, [("sp", kr_o[l][:, t0 + tt_ * 512:t0 + (tt_ + 1) * 512], krf[0:32, :])], reads=[krf], is_out=True)

                def q_item(tt_, h):
                    tsl = slice(tt_ * 512, (tt_ + 1) * 512)

                    def src():
                        bk = nb()
                        for c in range(2):
                            mm(bk[0:96, :], wuq[:, c, h * 96:(h + 1) * 96], cqn[:, c, :], c == 0, c == 1, [wuq, cqn], [bk])
                        return bk
                    if ctx:
                        return norm_item(src, 96, ONES_b[0:96, 0:96], 96.0, smp[0:96, l, 3:4], 512, [], [(0, 512, tt_ * 512, qaT[0:96, h, tsl], qaT)], P_l=P96[0:96, 0:96])
                    return norm_item(src, 96, ONES_b[0:96, 0:96], 96.0, smp[0:96, l, 3:4], 512, [(0, 512, qaT[0:96, h, tsl], qaT)], [])

                gens = []
                for tt_ in range(ntile):
                    gens.append(cq_item(tt_))
                    gens.append(ckv_item(tt_))
                    gens.append(kr_item(tt_))
                    for h in range(4):
                        gens.append(q_item(tt_, h))
                run_pipe(gens)

                ckpt(6)
                ktot = nseq * nk

                def k_item(kb, h):
                    w_ = min(512, ktot - kb)
                    ksl = slice(kb, kb + w_)

                    def src():
                        bk = nb()
                        mm(bk[0:96, 0:w_], SHIFT[0:32, 0:96], krb[0:32, ksl], True, False, [cbf, krb], [bk])
                        mm(bk[0:96, 0:w_], wukv[:, h * 96:(h + 1) * 96], ckvn[:, ksl], False, True, [wukv, ckvn], [bk])
                        return bk
                    if ctx:
                        if kb == 0:
                            plain = [(0, 256, kaT[0:96, h, 0:256], kaT)]
                            roped = [(256, 256, 0, kaT[0:96, h, 256:512], kaT)]
                        else:
                            plain = []
                            roped = [(0, w_, kb - 256, kaT[0:96, h, kb:kb + w_], kaT)]
                        return norm_item(src, 96, ONES_b[0:96, 0:96], 96.0, smp[0:96, l, 4:5], w_, plain, roped, P_l=P96[0:96, 0:96])
                    return norm_item(src, 96, ONES_b[0:96, 0:96], 96.0, smp[0:96, l, 4:5], w_, [(0, w_, kaT[0:96, h, ksl], kaT)], [])

                def v_item(kt):
                    bk = nb()
                    mm(bk[:, 0:256], ckvn[:, kt * 128:(kt + 1) * 128], wukv[:, 384:640], True, True, [ckvn, wukv], [bk])
                    yield
                    srcv = bk[:, 0:256].rearrange("p (h d) -> p h d", h=4)
                    cp("act", vaA[:, kt, 0::2, 0:64], srcv[:, 0::2, :], [bk], [vaA])
                    cp("dve", vaA[:, kt, 1::2, 64:128], srcv[:, 1::2, :], [bk], [vaA])

                gens = []
                for kb in range(0, ktot, 512):
                    w_ = min(512, ktot - kb)
                    for h in range(4):
                        gens.append(k_item(kb, h))
                    for kt in range(kb // 128, (kb + w_) // 128):
                        gens.append(v_item(kt))
                run_pipe(gens)

                if l == 0 and t0 == 0:
                    dbg("qaT", qaT)
                    dbg("kaT", kaT)

                ckpt(7)
                P.barrier()
                arena.off = offC
                ptr = Ring([arena.get([128, 2, 512], BF16, "pt") for _ in range(3)])
                rrr = Ring([arena.get([128, 512], F32, "rr") for _ in range(2)])
                odr = Ring([arena.get([128, 512], F32, "od") for _ in range(2)])
                o2r = Ring([arena.get([128, 512], F32, "o2") for _ in range(2)])
                sq2r = Ring([arena.get([128, 512], BF16, "sq2") for _ in range(2)])
                rs2r = Ring([arena.get([128, 512], F32, "rs2") for _ in range(2)])
                NQ = min(512, L)
                mstr = Ring([arena.get([128, 512], BF16, "mst") for _ in range(2)])

                def attn_pass(q_of, k_of, v_of, scale, accs, D=1):
                    npair = len(accs) // 2
                    items = [(kt, p) for kt in range(nkt) for p in range(npair)]
                    pts = {}
                    for j in range(len(items) + D):
                        if j < len(items):
                            kt, p = items[j]
                            if rot.i % 2 == 1:
                                rot.i += 1
                            b0 = nb()
                            b1 = nb()
                            for bi, a in ((b0, 2 * p), (b1, 2 * p + 1)):
                                ab, qv, kf, vf = accs[a]
                                kv, kreads = kf(kt)
                                mm(bi[:, 0:NQ], kv, qv[0], True, True, kreads + qv[1], [bi])
                            pt = ptr.next()
                            i0 = banks.index(b0)
                            act(pt[:, :, 0:NQ], rotbig[:, i0:i0 + 2, 0:NQ], AF.Exp, [b0, b1], [pt], scale=scale)
                            pts[j] = pt
                        i = j - D
                        if i >= 0:
                            kt, p = items[i]
                            pt = pts.pop(i)
                            for h_, a in enumerate((2 * p, 2 * p + 1)):
                                ab, qv, kf, vf = accs[a]
                                vv, vreads = vf(kt)
                                mm(ab[:, 0:NQ], vv, pt[:, h_, 0:NQ], kt == 0, kt == nkt - 1, vreads + [pt], [ab])

                def finish_pair(ab0, ab1, dst_ap, dst_t, use_act=False):
                    rr = rrr.next()
                    if use_act:
                        act(rr[0:64, 0:NQ], ab0[64:128, 0:NQ], AF.Ln, [ab0], [rr])
                        act(rr[64:128, 0:NQ], ab1[0:64, 0:NQ], AF.Ln, [ab1], [rr])
                        act(rr[:, 0:NQ], rr[:, 0:NQ], AF.Exp, [rr], [rr], scale=-1.0)
                    else:
                        P.op("dve", lambda e: e.reciprocal(out=rr[0:64, 0:NQ], in_=ab0[64:128, 0:NQ]), [ab0], [rr])
                        P.op("dve", lambda e: e.reciprocal(out=rr[64:128, 0:NQ], in_=ab1[0:64, 0:NQ]), [ab1], [rr])
                    tt("dve", dst_ap[0:64], ab0[0:64, 0:NQ], rr[0:64, 0:NQ], ALU.mult, [ab0, rr], [dst_t])
                    tt("dve", dst_ap[64:128], ab1[64:128, 0:NQ], rr[64:128, 0:NQ], ALU.mult, [ab1, rr], [dst_t])

                mla_cnt = [0]
                for s_ in range(nseq):
                    for q0 in range(0, L, NQ):
                        qs = slice(s_ * L + q0, s_ * L + q0 + NQ)
                        for pr in range(2):
                            accs = []
                            ab_ = 2 * (mla_cnt[0] % 2)
                            mla_cnt[0] += 1
                            for i in range(2):
                                h = pr * 2 + i
                                accs.append((acc[ab_ + i], (qaT[0:96, h, qs], [qaT]),
                                             (lambda kt, h=h: (kaT[0:96, h, s_ * nk + kt * 128:s_ * nk + (kt + 1) * 128], [kaT])),
                                             (lambda kt, h=h: (vaA[:, s_ * nkt + kt, h, :], [vaA]))))
                            attn_pass(None, None, None, 96.0 ** -0.5, accs)
                            mst = mstr.next()
                            finish_pair(acc[ab_], acc[ab_ + 1], mst[:, 0:NQ], mst)
                            ta_ = t0 + s_ * L + q0
                            P.dma(mst, [("sp", mixD[:, pr, ta_:ta_ + NQ], mst[:, 0:NQ])], reads=[mst], writes=[mixbufs[ta_ // 512]])

                ckpt(8)
                P.barrier()
                arena.off = offA
                wat2 = arena.get([128, 8, 928], BF16, "wat")
                ropeD = arena.get([128, 2, 2048], F32, "ropeD")
                if ctx:
                    P.dma(ropeD, [("sp", ropeD.ap, rope_d[:, 2:4, :])], writes=[ropeD])
                rope = ropeD
                qdT = arena.get([128, 4, n], BF16, "qdT")
                kdT = arena.get([128, 4, nseq * nk], BF16, "kdT")
                vdA = arena.get([128, nseq * nkt, 4, 128], BF16, "vdA")
                offD = arena.off
                memset("pool", vdA.ap, 1.0, [vdA])
                memset("pool", kdT[64:128, :, :], 0.0, [kdT])
                sqr = Ring([arena.get([128, 2, 512], BF16, "sqc") for _ in range(2)])
                rsr = Ring([arena.get([128, 512], F32, "rs") for _ in range(3)])
                qn_r = Ring([arena.get([128, 512], F32, "qn") for _ in range(2)])
                qnb_r = Ring([arena.get([128, 512], BF16, "qnb") for _ in range(2)])
                t1_r = Ring([arena.get([128, 512], F32, "t1") for _ in range(2)])
                t2_r = Ring([arena.get([128, 512], F32, "t2") for _ in range(2)])
                kdf_r = Ring([arena.get([128, 4, 512], F32, "kdf") for _ in range(2)])
                cpyr[0] = Ring([arena.get([128, 512], F32, "cpy") for _ in range(2)])
                vdf_r = Ring([arena.get([128, 4, 256], F32, "vdf") for _ in range(2)])
                if ctx:
                    kst = kdf_r.next()
                    P.dma(kst, [("sp", kst[:, :, 0:256], c_kdT[l])], writes=[kst])
                    cp("act", kdT[0:64, :, 0:256], kst[0:64, :, 0:256], [kst], [kdT])
                    vst = vdf_r.next()
                    P.dma(vst, [("sp", vst[:, 0:2, :], c_vd[l])], writes=[vst])
                    for kt in range(2):
                        srcv = vst[:, kt, :].rearrange("p (h d) -> p h d", h=4)
                        cp("act", vdA[:, kt, 0::2, 0:64], srcv[:, 0::2, :], [vst], [vdA])
                        cp("dve", vdA[:, kt, 1::2, 64:128], srcv[:, 1::2, :], [vst], [vdA])
                def dqk_item(tt_, which, h, kdf):
                    tsl = slice(tt_ * 512, (tt_ + 1) * 512)
                    c0 = 416 + which * 256 + h * 64
                    gcol = smp[0:64, l, 5 + which:6 + which]

                    def src():
                        bk = nb()
                        for kc in range(8):
                            mm(bk[0:64, :], wat[:, kc, c0:c0 + 64], hT[:, kc, tsl], kc == 0, kc == 7, [wat, hT], [bk])
                        return bk
                    if ctx:
                        if which == 0:
                            dst, dst_t = qdT[0:64, h, tsl], qdT
                        else:
                            dst, dst_t = kdT[0:64, h, keycol(0, tt_ * 512, 512)], kdT
                        return norm_item(src, 64, BD32[0:64, 0:64], 32.0, gcol, 512, [], [(0, 512, tt_ * 512, dst, dst_t)], P_l=P64[0:64, 0:64])
                    if which == 0:
                        return norm_item(src, 64, BD32[0:64, 0:64], 32.0, gcol, 512, [(0, 512, qdT[0:64, h, tsl], qdT)], [])

                    def post():
                        for (s_, o_, c0_, w_) in tile_segs(tt_):
                            cp("act", kdT[0:64, h, keycol(s_, o_, w_)], kdf[0:64, h, c0_:c0_ + w_], [kdf], [kdT])
                        if h == 3:
                            P.dma(kdf, [("sp", kd_o[l][:, :, t0 + tt_ * 512:t0 + (tt_ + 1) * 512], kdf[0:64, :, :])], reads=[kdf], is_out=True)
                    return norm_item(src, 64, BD32[0:64, 0:64], 32.0, gcol, 512, [(0, 512, kdf[0:64, h, :], kdf)], [], post=post)

                def vd_item(tt_, j, vdf):
                    tok0 = tt_ * 512 + j * 128
                    bk = nb()
                    for kc in range(8):
                        mm(bk[:, 0:256], hT[:, kc, tok0:tok0 + 128], wtm[:, kc, 0:256], kc == 0, kc == 7, [hT, wtm], [bk])
                    yield
                    s_, tl = divmod(tok0, L)
                    kt = s_ * nkt + (koff + tl) // 128
                    srcv = bk[:, 0:256].rearrange("p (h d) -> p h d", h=4)
                    cp("act", vdA[:, kt, 0::2, 0:64], srcv[:, 0::2, :], [bk], [vdA])
                    cp("dve", vdA[:, kt, 1::2, 64:128], srcv[:, 1::2, :], [bk], [vdA])
                    if not ctx:
                        cp("dve", vdf[:, j, :], bk[:, 0:256], [bk], [vdf])
                        if j == 3:
                            P.dma(vdf, [("sp", vd_o[l], vdf.ap)], reads=[vdf], is_out=True)

                gens = []
                for tt_ in range(ntile):
                    kdf = kdf_r.next()
                    vdf = vdf_r.next()
                    for which in range(2):
                        for h in range(4):
                            gens.append(dqk_item(tt_, which, h, kdf))
                    for j in range(4):
                        gens.append(vd_item(tt_, j, vdf))
                run_pipe(gens)


                ckpt(9)
                P.barrier()
                arena.off = offD
                ptr = Ring([arena.get([128, 2, 512], BF16, "pt") for _ in range(3)])
                rrr = Ring([arena.get([128, 512], F32, "rr") for _ in range(2)])
                odr = Ring([arena.get([128, 512], F32, "od") for _ in range(2)])
                o2r = Ring([arena.get([128, 512], F32, "o2") for _ in range(2)])
                sq2r = Ring([arena.get([128, 512], BF16, "sq2") for _ in range(2)])
                rs2r = Ring([arena.get([128, 512], F32, "rs2") for _ in range(2)])
                mstr = Ring([arena.get([128, 512], BF16, "mst") for _ in range(2)])
                qmr = Ring([arena.get([128, 2, 512], BF16, "qm") for _ in range(4)])
                for qm_ in qmr.tiles:
                    memset("pool", qm_.ap, 0.0, [qm_])
                wout = arena.get([128, 8, 1024], BF16, "wout")
                P.dma(wout, [("pool", wout.ap, w_out[l])], writes=[wout])
                for s_ in range(nseq):
                    for q0 in range(0, L, NQ):
                        qs = slice(s_ * L + q0, s_ * L + q0 + NQ)
                        for pr in range(2):
                            accs = []
                            for i in range(2):
                                h = pr * 2 + i
                                qm = qmr.next()
                                for m in range(2):
                                    cp("pool", qm[m * 32:(m + 1) * 32, m, 0:NQ], qdT[m * 32:(m + 1) * 32, h, qs], [qdT], [qm])
                                for m in range(2):
                                    accs.append((acc[i * 2 + m], (qm[:, m, 0:NQ], [qm]),
                                                 (lambda kt, h=h, m=m: (kdT[:, h, s_ * nk + kt * 128:s_ * nk + (kt + 1) * 128], [kdT])),
                                                 (lambda kt, h=h: (vdA[:, s_ * nkt + kt, h, :], [vdA]))))
                            attn_pass(None, None, None, 32.0 ** -0.5, accs)
                            od = odr.next()
                            o2 = o2r.next()
                            finish_pair(acc[0], acc[2], od[:, 0:NQ], od, use_act=True)
                            finish_pair(acc[1], acc[3], o2[:, 0:NQ], o2, use_act=True)
                            stt(od[:, 0:NQ], o2[:, 0:NQ], nlam, od[:, 0:NQ], ALU.mult, ALU.add, [o2, od, lay], [od])
                            sq2 = sq2r.next()
                            act(sq2[:, 0:NQ], od[:, 0:NQ], AF.Square, [od], [sq2])
                            bk = nb()
                            mm(bk[:, 0:NQ], BD64, sq2[:, 0:NQ], True, True, [sq2, cbf], [bk])
                            rs2 = rs2r.next()
                            rstd_act(rs2[:, 0:NQ], bk[:, 0:NQ], 64.0, [bk], [rs2])
                            mst = mstr.next()
                            stt(mst[:, 0:NQ], od[:, 0:NQ], subg, rs2[:, 0:NQ], ALU.mult, ALU.mult, [od, rs2, lay], [mst])
                            ta_ = t0 + s_ * L + q0
                            P.dma(mst, [("sp", mixD[:, 2 + pr, ta_:ta_ + NQ], mst[:, 0:NQ])], reads=[mst], writes=[mixbufs[ta_ // 512]])

                ckpt(10)
                P.barrier()
                arena.off = offA
                xring = Ring([arena.get([128, 8, 512], F32, "xt") for _ in range(2)])
                mxr = Ring([arena.get([128, 8, 512], BF16, "mx") for _ in range(2)])
                do_mod = ctx and (l + 1 < depth)
                if do_mod:
                    wringD = Ring([arena.get([128, 8, 512], BF16, "wada") for _ in range(3)])
                def stepD_load(tt_):
                    ta_ = t0 + tt_ * 512
                    xt_ = xring.next()
                    P.dma(xt_, [("sp", xt_.ap, xsrc[:, :, ta_:ta_ + 512])], reads=[xbufs[ta_ // 512]], writes=[xt_])
                    mx_ = mxr.next()
                    P.dma(mx_, [("sp", mx_.ap, mixD[:, :, ta_:ta_ + 512])], reads=[mixbufs[ta_ // 512]], writes=[mx_])
                    return xt_, mx_
                nxtD = stepD_load(0)
                for tt_ in range(ntile):
                    xt, mx = nxtD
                    if tt_ + 1 < ntile:
                        nxtD = stepD_load(tt_ + 1)
                    if do_mod:
                        for g_ in range(3):
                            mod_group(l + 1, tt_ * 3 + g_, wringD)
                        if tt_ == ntile - 1:
                            mod_finish(l + 1)
                    ta = t0 + tt_ * 512
                    tsl = slice(tt_ * 512, (tt_ + 1) * 512)
                    xb = xbufs[ta // 512]
                    for m in range(8):
                        bk = nb()
                        for kc in range(8):
                            mm(bk.ap, wout[:, kc, m * 128:(m + 1) * 128], mx[:, kc, :], kc == 0, kc == 7, [wout, mx], [bk])
                        stt(xt[:, m, :], bk.ap, modT[:, l, 2, m, g:g + 1], xt[:, m, :], ALU.mult, ALU.add, [bk, modT, xt], [xt])
                    P.dma(xt, [("sp", xT_out[:, :, ta:ta + 512], xt.ap)], reads=[xt], writes=[xb], is_out=True)
                    if l == 0 and t0 == 0 and "mix" in debug:
                        dbg("mix", mx)

            ckpt(11)
            P.barrier()
            arena.reset()
            xall = arena.get([128, 8, NT], F32, "xall")
            h2T = arena.get([128, 8, NT], BF16, "h2T")
            xts = [T(xall[:, :, i * 512:(i + 1) * 512], "xall%d" % i) for i in range(5)]
            sqF = arena.get([128, 8, 512], BF16, "sqF")
            rsr = Ring([arena.get([128, 512], F32, "rs") for _ in range(2)])
            tmr = Ring([arena.get([128, 512], F32, "tm") for _ in range(2)])
            w1r = Ring([arena.get([128, 8, 512], BF16, "w1") for _ in range(2)])
            w2r = Ring([arena.get([128, 4, 1024], BF16, "w2") for _ in range(2)])
            rlr = Ring([arena.get([128, 512], BF16, "rl") for _ in range(3)])
            ur = Ring([arena.get([128, 4, 512], BF16, "u") for _ in range(2)])
            h2Ts = [T(h2T[:, :, i * 512:(i + 1) * 512], "h2T%d" % i) for i in range(5)]
            for i in range(5):
                P.dma(xts[i], [("sp", xts[i].ap, xT_out[:, :, i * 512:(i + 1) * 512])], reads=[xbufs[i]], writes=[xts[i]])

            def ffn_load(e8_):
                w1_ = w1r.next()
                w2_ = w2r.next()
                P.dma(w1_, [("pool", w1_.ap, w_ff1[l][:, :, e8_ * 512:(e8_ + 1) * 512])], writes=[w1_])
                P.dma(w2_, [("pool", w2_.ap, w_ff2[l][:, e8_ * 4:(e8_ + 1) * 4, :])], writes=[w2_])
                return w1_, w2_

            def do_norm(i):
                norm_mod(xts[i], h2Ts[i].ap, h2Ts[i], l, 1, 0 if i == 0 else 1, sqF, rsr, tmr)
            wsets = {0: ffn_load(0)}
            do_norm(0)
            do_norm(1)

            def ffn_item(e8, i):
                g = 0 if i == 0 else 1
                if e8 == 0 and i + 2 < 5:
                    do_norm(i + 2)
                w1, w2 = wsets[e8]
                u = ur.next()
                for jc in range(4):
                    bk = nb()
                    for kc in range(8):
                        mm(bk.ap, w1[:, kc, jc * 128:(jc + 1) * 128], h2Ts[i][:, kc, :], kc == 0, kc == 7, [w1, h2Ts[i]], [bk])
                    rl = rlr.next()
                    act(rl.ap, bk.ap, AF.Relu, [bk], [rl])
                    tt("pool", u[:, jc, :], rl.ap, rl.ap, ALU.mult, [rl], [u])
                yield
                if i == 0 and e8 + 1 < 8:
                    wsets[e8 + 1] = ffn_load(e8 + 1)
                for m in range(8):
                    bk = nb()
                    for jc in range(4):
                        mm(bk.ap, w2[:, jc, m * 128:(m + 1) * 128], u[:, jc, :], jc == 0, jc == 3, [w2, u], [bk])
                    stt(xts[i][:, m, :], bk.ap, modT[:, l, 5, m, g:g + 1], xts[i][:, m, :], ALU.mult, ALU.add, [bk, modT, xts[i]], [xts[i]])
            run_pipe2([ffn_item(e8_, i_) for e8_ in range(8) for i_ in range(5)])
            for i in range(5):
                P.dma(xts[i], [("sp", xT_out[:, :, i * 512:(i + 1) * 512], xts[i].ap)], reads=[xts[i]], writes=[xbufs[i]], is_out=True)

    except StopBuild:
        pass
    P.finish()
    return nc, dbg_outs, P, arena


_CACHE = {}


def kernel(**inputs):
    inp = {k: np.asarray(v) for k, v in inputs.items()}
    consts = host_consts()
    shared = prep_shared(inp)
    in_maps = []
    for core in range(8):
        d = {}
        d.update(shared)
        d.update(consts)
        d.update(prep_core(inp, core, consts))
        d.update(prep_cache(inp, core))
        in_maps.append({k: np.ascontiguousarray(v, dtype=np.float32) for k, v in d.items()})
    if "nc" not in _CACHE:
        _CACHE["nc"] = build(DEPTH_RUN)[0]
    nc = _CACHE["nc"]
    res = run_bass_kernel_spmd(nc, in_maps, core_ids=list(range(8)))
    R = res.results
    B, S, Dm = 16, 256, 1024
    y_prompt = np.zeros((16, 256, 1024), np.float32)
    y_sample = np.zeros((4, 2048, 1024), np.float32)
    new_ckv = np.zeros((16, NL, 256, 128), np.float32)
    new_kr = np.zeros((16, NL, 256, 32), np.float32)
    new_kd = np.zeros((16, NL, 256, 4, 64), np.float32)
    new_vd = np.zeros((16, NL, 256, 4, 64), np.float32)
    new_st = np.zeros((16, NL, 2, 8, 64, 64), np.float32)
    for core in range(8):
        r = R[core]
        xo = r["xT_out"].transpose(2, 1, 0).reshape(NT, 1024)
        y_prompt[2 * core] = xo[0:256]
        y_prompt[2 * core + 1] = xo[256:512]
        if core % 2 == 0:
            y_sample[core // 2] = xo[512:]
        for s in range(2):
            bidx = 2 * core + s
            new_ckv[bidx] = r["ckv_o"][:, :, s * 256:(s + 1) * 256].transpose(0, 2, 1)
            new_kr[bidx] = r["kr_o"][:, :, s * 256:(s + 1) * 256].transpose(0, 2, 1)
            kd = r["kd_o"][:, :, :, s * 256:(s + 1) * 256]
            new_kd[bidx] = kd.transpose(0, 3, 2, 1)
            vd = r["vd_o"].reshape(NL, 128, 4, 4, 64).transpose(0, 2, 1, 3, 4).reshape(NL, 512, 4, 64)
            new_vd[bidx] = vd[:, s * 256:(s + 1) * 256]
            st = r["st_o"][:, s]
            st = st.reshape(NL, 2, 2, 64, 4, 64).transpose(0, 1, 2, 4, 5, 3)
            new_st[bidx] = st.reshape(NL, 2, 8, 64, 64)
    return (y_prompt, y_sample, new_ckv, new_kr, new_kd, new_vd, new_st)
```

```python
import math
from contextlib import ExitStack
import numpy as np
import concourse.bass as bass
import concourse.mybir as mybir
from concourse.bass_utils import run_bass_kernel_spmd

F32 = mybir.dt.float32
BF16 = mybir.dt.bfloat16
AF = mybir.ActivationFunctionType
ALU = mybir.AluOpType
AX = mybir.AxisListType

NL = 4
NT = 2560
EPS = 1e-6
SEM_ROT = 30000
DEPTH_RUN = NL
DEBUG = {}


class StopBuild(Exception):
    pass


STOP_AT = [None]


def ckpt(k):
    if STOP_AT[0] is not None and k >= STOP_AT[0]:
        raise StopBuild()


class Buf:
    __slots__ = ("name", "w", "r", "dkey")

    def __init__(self, name):
        self.name = name
        self.w = None
        self.r = {}
        self.dkey = None


class T:
    __slots__ = ("ap", "b")

    def __init__(self, ap, name):
        self.ap = ap
        self.b = Buf(name)

    def __getitem__(self, k):
        return self.ap[k]


class Prog:
    ENGS = ("pe", "act", "dve", "pool", "sp")

    def __init__(self, nc):
        self.nc = nc
        self.st = ExitStack()
        self.streams = {e: [] for e in self.ENGS}
        self.cnt = {}
        self.known = {e: {} for e in self.ENGS}
        self.engkey = {e: e + "0" for e in self.ENGS}
        self.engrot = {e: 0 for e in self.ENGS}
        self.out_tokens = []
        self.nops = 0
        self.uid = 0
        self.free_dkeys = []
        self.recent = []

    def sb(self, name, shape, dt=F32):
        return self.st.enter_context(self.nc.sbuf_tensor(name, list(shape), dt))

    def ps(self, name, shape, dt=F32):
        return self.st.enter_context(self.nc.psum_tensor(name, list(shape), dt))

    def _deps(self, reads, writes, eng=None):
        deps = []
        for b in reads:
            if b.w is not None:
                deps.append(b.w)

        def own(k):
            return eng is not None and k.startswith(eng) and k[len(eng):].isdigit()
        for b in writes:
            if b.w is not None and not own(b.w[0]):
                deps.append(b.w)
            for k, v in b.r.items():
                if not own(k):
                    deps.append((k, v))
        return deps

    def _waits(self, eng, deps):
        need = {}
        kn = self.known[eng]
        for k, v in deps:
            if eng == "pe" and k.startswith("pe"):
                continue
            if kn.get(k, 0) >= v:
                continue
            if need.get(k, 0) < v:
                need[k] = v
        for k, v in need.items():
            kn[k] = v
            self.streams[eng].append(("wait", k, v))

    def _mark(self, tok, reads, writes):
        k, v = tok
        for b in reads:
            if b.r.get(k, 0) < v:
                b.r[k] = v
        for b in writes:
            b.w = tok
            b.r = {}

    def op(self, eng, fn, reads=(), writes=()):
        reads = [r.b if isinstance(r, T) else r for r in reads]
        writes = [w.b if isinstance(w, T) else w for w in writes]
        self._waits(eng, self._deps(reads, writes, eng))
        k = self.engkey[eng]
        c = self.cnt.get(k, 0) + 1
        if c > SEM_ROT:
            self.engrot[eng] += 1
            k = self.engkey[eng] = eng + str(self.engrot[eng])
            c = 1
        self.cnt[k] = c
        tok = (k, c)
        self.streams[eng].append(("op", fn, k, 1))
        self._mark(tok, reads, writes)
        self.nops += 1
        return tok

    def dma(self, buf, items, reads=(), writes=(), is_out=False):
        reads = [r.b if isinstance(r, T) else r for r in reads]
        writes = [w.b if isinstance(w, T) else w for w in writes]
        if isinstance(buf, T):
            buf = buf.b
        if buf.dkey is None:
            if self.free_dkeys:
                buf.dkey = self.free_dkeys.pop()
            else:
                self.uid += 1
                buf.dkey = "d%d" % self.uid
            self.recent.append(buf)
        k = buf.dkey
        deps = self._deps(reads, writes)
        for q in dict.fromkeys(it[0] for it in items):
            self._waits(q, deps)
        c = self.cnt.get(k, 0)
        for it in items:
            q, o, i = it[0], it[1], it[2]
            c += 16
            self.streams[q].append(("op", (lambda e, o=o, i=i: e.dma_start(out=o, in_=i)), k, 16))
        assert c < 1000000, (k, c)
        self.cnt[k] = c
        tok = (k, c)
        self._mark(tok, reads, writes)
        if is_out:
            self.out_tokens.append(tok)
        return tok

    def barrier(self):
        deps = list(self.cnt.items())
        for e in self.ENGS:
            kn = self.known[e]
            for k, v in deps:
                if v > 0 and kn.get(k, 0) < v:
                    kn[k] = v
                    self.streams[e].append(("wait", k, v))
        for b in self.recent:
            if b.dkey is not None:
                self.free_dkeys.append(b.dkey)
                b.dkey = None
        self.recent = []

    def finish(self):
        nc = self.nc
        fin = {}
        for k, v in self.out_tokens:
            fin[k] = max(fin.get(k, 0), v)
        for k, v in fin.items():
            if self.known["sp"].get(k, 0) < v:
                self.streams["sp"].append(("wait", k, v))
        sems = {}
        for k in self.cnt:
            sems[k] = self.st.enter_context(nc.semaphore(k))
        block = self.st.enter_context(nc.Block())
        streams = self.streams

        def replay(e, lst):
            for item in lst:
                if item[0] == "wait":
                    e.wait_ge(sems[item[1]], item[2])
                else:
                    item[1](e).then_inc(sems[item[2]], item[3])

        @block.tensor
        def _(e):
            replay(e, streams["pe"])

        @block.scalar
        def _(e):
            replay(e, streams["act"])

        @block.vector
        def _(e):
            replay(e, streams["dve"])

        @block.gpsimd
        def _(e):
            replay(e, streams["pool"])

        @block.sync
        def _(e):
            replay(e, streams["sp"])

        self.st.close()


class Arena:
    def __init__(self, P, name, nbytes):
        self.t = P.sb(name, [128, nbytes // 2], BF16)
        self.cap = nbytes // 2
        self.off = 0
        self.n = 0
        self.hi = 0

    def reset(self):
        self.off = 0

    def get(self, shape, dt, name="t"):
        free = 1
        for s in shape[1:]:
            free *= s
        nb = free * (4 if dt == F32 else 2)
        ne = ((nb + 3) // 4) * 2
        assert self.off + ne <= self.cap, ("arena overflow", name, self.off, ne, self.cap)
        v = self.t[0:shape[0], self.off:self.off + nb // 2]
        self.off += ne
        self.hi = max(self.hi, self.off)
        if dt == F32:
            v = v.bitcast(F32)
        if len(shape) == 3:
            v = v.rearrange("p (a b) -> p a b", a=shape[1])
        elif len(shape) == 4:
            v = v.rearrange("p (a b c) -> p a b c", a=shape[1], b=shape[2])
        self.n += 1
        return T(v, "%s%d" % (name, self.n))


class Ring:
    def __init__(self, tiles):
        self.tiles = tiles
        self.i = 0

    def next(self):
        t = self.tiles[self.i % len(self.tiles)]
        self.i += 1
        return t


def run_pipe2(gens):
    prev = None
    for g_ in gens:
        next(g_)
        if prev is not None:
            for _ in prev:
                pass
        prev = g_
    if prev is not None:
        for _ in prev:
            pass


def bc(ap, axis, shape):
    return ap.unsqueeze(axis).broadcast_to(list(shape))


def host_consts():
    c = {}
    ii = np.arange(128)
    U = (ii[:, None] <= ii[None, :]).astype(np.float32)
    UT = (ii[:, None] >= ii[None, :]).astype(np.float32)
    nmf = np.where(ii[:, None] <= ii[None, :], 0.0, -30000.0).astype(np.float32)
    nmb = np.where(ii[:, None] >= ii[None, :], 0.0, -30000.0).astype(np.float32)
    ones = np.ones((128, 128), np.float32)
    ident = np.eye(128, dtype=np.float32)
    bd64 = np.zeros((128, 128), np.float32)
    bd64[:64, :64] = 1
    bd64[64:, 64:] = 1
    bd32 = np.zeros((128, 128), np.float32)
    for k in range(4):
        bd32[k * 32:(k + 1) * 32, k * 32:(k + 1) * 32] = 1
    d = np.arange(32)
    dd = d % 16
    j = dd % 8
    partner = np.where(dd < 8, d + 8, d - 8)
    freqs = (10000.0 ** (-(np.arange(0, 16, 2, dtype=np.float32)) / 16.0)).astype(np.float32)
    t = np.arange(2048)
    rows = (t // 64).astype(np.float32)
    cols = (t % 64).astype(np.float32)
    pos = np.where((d // 16)[:, None] == 0, rows[None, :], cols[None, :]).astype(np.float32)
    ang = (pos * freqs[j][:, None]).astype(np.float32)
    cos32 = np.cos(ang).astype(np.float32)
    sin32 = np.sin(ang).astype(np.float32)
    sins32 = np.where((dd < 8)[:, None], -sin32, sin32).astype(np.float32)
    p32 = np.zeros((32, 32), np.float32)
    p32[partner, d] = 1.0
    p96 = np.zeros((128, 128), np.float32)
    p96[64:96, 64:96] = p32
    p64 = np.zeros((128, 128), np.float32)
    p64[0:32, 0:32] = p32
    p64[32:64, 32:64] = p32
    cos96 = np.ones((128, 2048), np.float32)
    sin96 = np.zeros((128, 2048), np.float32)
    cos96[64:96] = cos32
    sin96[64:96] = sins32
    cos96[0:32] = cos32
    cos96[32:64] = cos32
    shift = np.zeros((128, 128), np.float32)
    for i in range(32):
        shift[i, 64 + i] = 1.0
    sel = np.zeros((16, 16, 128), np.float32)
    for h in range(16):
        sel[h, h, :] = 1.0
    c["cf32"] = np.stack([U, UT, nmf, nmb, ones, ident], axis=1).astype(np.float32)
    c["sel"] = sel.reshape(16, 2048)
    c["cbf"] = np.stack([ones, ident, bd64, bd32, p96, p64, shift], axis=1).astype(np.float32)
    sin64 = np.zeros((128, 2048), np.float32)
    sin64[0:32] = sins32
    sin64[32:64] = sins32
    c["rope"] = np.stack([cos96, sin96, sin64], axis=1).astype(np.float32)
    cosd = np.ones((128, 2048), np.float32)
    cosd[0:32] = cos32
    cosd[32:64] = cos32
    cosa = np.ones((128, 2048), np.float32)
    cosa[64:96] = cos32
    c["rope"] = np.stack([cosa, sin96, cosd, sin64], axis=1).astype(np.float32)
    return c


SM_PER = 44
BC_PER = 1184


def prep_core(inp, core, consts):
    f = np.float32
    b = core // 2
    d = {}
    xs = np.concatenate([inp["x_prompt"][2 * core], inp["x_prompt"][2 * core + 1], inp["x_sample"][b]], axis=0)
    d["xT_in"] = np.ascontiguousarray(xs.reshape(NT, 8, 128).transpose(2, 1, 0))
    cv = np.stack([inp["c_ctx"], inp["c"][b]], axis=-1)
    d["cvec"] = np.ascontiguousarray(cv.reshape(8, 128, 2).transpose(1, 0, 2))
    return d


def prep_shared(inp):
    f = np.float32
    d = {}
    d["w_ada"] = np.ascontiguousarray(inp["w_ada"].reshape(NL, 8, 128, 6144).transpose(0, 2, 1, 3))
    d["b_ada"] = np.ascontiguousarray(inp["b_ada"].reshape(NL, 48, 128).transpose(2, 0, 1))
    d["n1g"] = np.ascontiguousarray(inp["norm1_g"].reshape(NL, 8, 128).transpose(2, 0, 1))
    d["n2g"] = np.ascontiguousarray(inp["norm2_g"].reshape(NL, 8, 128).transpose(2, 0, 1))
    w_in = inp["w_in"]
    wfm = np.concatenate([w_in[:, :, 0:928], w_in[:, :, 1696:2464]], axis=2)
    wtm = np.concatenate([w_in[:, :, 928:1696], w_in[:, :, 2464:2480]], axis=2)
    d["w_fm"] = np.ascontiguousarray(wfm.reshape(NL, 8, 128, 1696).transpose(0, 2, 1, 3))
    d["w_tm"] = np.ascontiguousarray(wtm.reshape(NL, 8, 128, 784).transpose(0, 2, 1, 3))
    d["w_uq"] = np.ascontiguousarray(inp["w_uq"].reshape(NL, 2, 128, 384).transpose(0, 2, 1, 3))
    wukv = inp["w_ukv"].reshape(NL, 128, 4, 128)
    kn = wukv[:, :, :, 0:64]
    vv = wukv[:, :, :, 64:128]
    kn96 = np.concatenate([kn, np.zeros((NL, 128, 4, 32), f)], axis=3)
    d["w_ukv"] = np.ascontiguousarray(np.concatenate([kn96.reshape(NL, 128, 384), vv.reshape(NL, 128, 256)], axis=2))
    d["w_out"] = np.ascontiguousarray(inp["w_out"].reshape(NL, 8, 128, 1024).transpose(0, 2, 1, 3))
    d["w_ff1"] = np.ascontiguousarray(inp["w_ff1"].reshape(NL, 8, 128, 4096).transpose(0, 2, 1, 3))
    d["w_ff2"] = np.ascontiguousarray(inp["w_ff2"].reshape(NL, 32, 128, 1024).transpose(0, 2, 1, 3))
    sm = np.zeros((128, NL, SM_PER), f)
    for l in range(NL):
        sm[:, l, 0:2] = inp["mla_q_norm_g"][l].reshape(2, 128).T
        sm[:, l, 2] = inp["mla_kv_norm_g"][l]
        sm[0:96, l, 3] = inp["mla_qk_norm_q"][l]
        sm[0:96, l, 4] = inp["mla_qk_norm_k"][l]
        sm[0:64, l, 5] = np.tile(inp["diff_q_norm_g"][l], 2)
        sm[0:64, l, 6] = np.tile(inp["diff_k_norm_g"][l], 2)
        sm[:, l, 7] = np.tile(inp["diff_subln_g"][l], 2)
        sm[:, l, 8:38] = inp["ssm_conv_w"][l].reshape(5, 6, 128).transpose(2, 1, 0).reshape(128, 30)
        sm[:, l, 38:44] = inp["ssm_conv_b"][l].reshape(6, 128).T
    d["smallp"] = sm
    bcp = np.zeros((NL, BC_PER), f)
    for l in range(NL):
        bcp[l, 0:512] = np.repeat(inp["ssm_D"][l], 64)
        bcp[l, 512:1024] = inp["ssm_norm_g"][l]
        bcp[l, 1024:1040] = inp["ssm_dt_bias"][l].reshape(16)
        bcp[l, 1040:1056] = inp["ssm_A_log"][l].reshape(16)
        bcp[l, 1056:1088] = inp["diff_lq1"][l]
        bcp[l, 1088:1120] = inp["diff_lk1"][l]
        bcp[l, 1120:1152] = inp["diff_lq2"][l]
        bcp[l, 1152:1184] = inp["diff_lk2"][l]
    d["bcp"] = bcp
    return d


def prep_cache(inp, core):
    b = core // 2
    d = {}
    d["c_ckvT"] = np.ascontiguousarray(inp["cache_mla_ckv"][b].transpose(0, 2, 1))
    kr = np.zeros((NL, 128, 256), np.float32)
    kr[:, 0:32, :] = inp["cache_mla_krope"][b].transpose(0, 2, 1)
    d["c_krT"] = kr
    kd = np.zeros((NL, 128, 4, 256), np.float32)
    kd[:, 0:64] = inp["cache_diff_k"][b].transpose(0, 3, 2, 1)
    d["c_kdT"] = kd
    d["c_vd"] = np.ascontiguousarray(inp["cache_diff_v"][b].reshape(NL, 2, 128, 256).transpose(0, 2, 1, 3))
    st = inp["state_ssm"][b]
    st = st.reshape(NL, 2, 2, 4, 64, 64)
    st = st.transpose(0, 1, 2, 5, 3, 4)
    d["st0"] = np.ascontiguousarray(st.reshape(NL, 2, 128, 256))
    return d


def build(depth=NL, debug=()):
    nc = bass.Bass("TRN2", target_bir_lowering=False)

    def din(name, shape):
        return nc.dram_tensor(name, list(shape), F32, kind="ExternalInput").ap()

    def dout(name, shape):
        return nc.dram_tensor(name, list(shape), F32, kind="ExternalOutput").ap()

    xT_in = din("xT_in", [128, 8, NT])
    cvec = din("cvec", [128, 8, 2])
    w_ada = din("w_ada", [NL, 128, 8, 6144])
    b_ada = din("b_ada", [128, NL, 48])
    n1g = din("n1g", [128, NL, 8])
    n2g = din("n2g", [128, NL, 8])
    w_fm = din("w_fm", [NL, 128, 8, 1696])
    w_tm = din("w_tm", [NL, 128, 8, 784])
    w_uq = din("w_uq", [NL, 128, 2, 384])
    w_ukv = din("w_ukv", [NL, 128, 640])
    w_out = din("w_out", [NL, 128, 8, 1024])
    w_ff1 = din("w_ff1", [NL, 128, 8, 4096])
    w_ff2 = din("w_ff2", [NL, 128, 32, 1024])
    smallp = din("smallp", [128, NL, SM_PER])
    bcp = din("bcp", [NL, BC_PER])
    cf32_d = din("cf32", [128, 6, 128])
    sel_d = din("sel", [16, 2048])
    cbf_d = din("cbf", [128, 7, 128])
    rope_d = din("rope", [128, 4, 2048])
    c_ckvT = din("c_ckvT", [NL, 128, 256])
    c_krT = din("c_krT", [NL, 128, 256])
    c_kdT = din("c_kdT", [NL, 128, 4, 256])
    c_vd = din("c_vd", [NL, 128, 2, 256])
    st0 = din("st0", [NL, 2, 128, 256])

    xT_out = dout("xT_out", [128, 8, NT])
    ckv_o = dout("ckv_o", [NL, 128, 512])
    kr_o = dout("kr_o", [NL, 32, 512])
    kd_o = dout("kd_o", [NL, 64, 4, 512])
    vd_o = dout("vd_o", [NL, 128, 4, 256])
    st_o = dout("st_o", [NL, 2, 2, 128, 256])

    P = Prog(nc)
    dbg_outs = {}

    cf32 = T(P.sb("cf32s", [128, 6, 128], F32)[:], "cf32")
    selS = T(P.sb("selS", [16, 2048], F32)[:], "sel")
    cbf = T(P.sb("cbfs", [128, 7, 128], BF16)[:], "cbf")
    smp = T(P.sb("smp", [128, NL, SM_PER], F32)[:], "smp")
    modT = T(P.sb("modT", [128, NL, 6, 8, 2], F32)[:], "modT")
    gsT = T(P.sb("gsT", [128, NL, 2, 8, 2], F32)[:], "gsT")
    bcl = T(P.sb("bcl", [128, BC_PER], F32)[:], "bcl")
    lay = T(P.sb("lay", [128, 64], F32)[:], "lay")
    U_f = cf32[:, 0, :]
    UT_f = cf32[:, 1, :]
    NMF = cf32[:, 2, :]
    NMB = cf32[:, 3, :]
    ONES_f = cf32[:, 4, :]
    ID_f = cf32[:, 5, :]
    ONES_b = cbf[:, 0, :]
    ID_b = cbf[:, 1, :]
    BD64 = cbf[:, 2, :]
    BD32 = cbf[:, 3, :]
    P96 = cbf[:, 4, :]
    P64 = cbf[:, 5, :]
    SHIFT = cbf[:, 6, :]

    banks = [T(P.ps("bank%d" % i, [128, 512], F32)[:], "bank%d" % i) for i in range(8)]
    rot = Ring(banks[0:4])
    rot8 = Ring(banks[0:8])
    acc = banks[4:8]
    cur_rot = [rot8]

    def nb():
        return cur_rot[0].next()

    def use_rot(n):
        cur_rot[0] = rot8 if n == 8 else rot

    arena = Arena(P, "arena", 184 * 1024)
    mixD = nc.dram_tensor("mixD", [128, 8, NT], BF16, kind="Internal").ap()
    mixbufs = [Buf("mixD%d" % i) for i in range(5)]

    xbufs = [Buf("xres%d" % i) for i in range(5)]

    def mm(out, lhsT, rhs, start, stop, reads, writes):
        P.op("pe", lambda e: e.matmul(out, lhsT=lhsT, rhs=rhs, start=start, stop=stop), reads, writes)

    def tr(out, in_, ident, reads, writes):
        P.op("pe", lambda e: e.transpose(out, in_, ident), reads, writes)

    def act(out, in_, func, reads, writes, bias=None, scale=None, accum=None):
        kw = {}
        if bias is not None:
            kw["bias"] = bias
        if scale is not None:
            kw["scale"] = scale
        if accum is not None:
            kw["accum_out"] = accum
        P.op("act", lambda e: e.activation(out=out, in_=in_, func=func, **kw), reads, writes)

    def tt(eng, out, in0, in1, op, reads, writes):
        P.op(eng, lambda e: e.tensor_tensor(out=out, in0=in0, in1=in1, op=op), reads, writes)

    def ts(eng, out, in0, s1, s2, op0, op1, reads, writes):
        if s2 is None:
            P.op(eng, lambda e: e.tensor_scalar(out=out, in0=in0, scalar1=s1, scalar2=None, op0=op0), reads, writes)
        else:
            P.op(eng, lambda e: e.tensor_scalar(out=out, in0=in0, scalar1=s1, scalar2=s2, op0=op0, op1=op1), reads, writes)

    def stt(out, in0, scalar, in1, op0, op1, reads, writes):
        P.op("dve", lambda e: e.scalar_tensor_tensor(out=out, in0=in0, scalar=scalar, in1=in1, op0=op0, op1=op1), reads, writes)

    def cp(eng, out, in_, reads, writes):
        if eng == "act":
            P.op("act", lambda e: e.copy(out=out, in_=in_), reads, writes)
        else:
            P.op(eng, lambda e: e.tensor_copy(out=out, in_=in_), reads, writes)

    def memset(eng, ap, val, writes):
        P.op(eng, lambda e: e.memset(ap, val), (), writes)

    def rstd_act(out, in_, D, reads, writes):
        act(out, in_, AF.Ln, reads, writes, scale=1.0 / D, bias=epsT[0:out.shape[0], 0:1])
        act(out, out, AF.Exp, writes, writes, scale=-0.5)

    def dbg(name, t, ap=None):
        if name not in debug:
            return
        ap = t.ap if ap is None else ap
        shp = list(ap.shape)
        o = nc.dram_tensor("dbg_" + name, shp, ap.dtype, kind="ExternalOutput").ap()
        dbg_outs[name] = shp
        tmpb = Buf("dbg_" + name)
        P.dma(tmpb, [("sp", o, ap)], reads=[t], is_out=True)

    epsT_t = T(P.sb("epsT", [128, 4], F32)[:], "epsT")
    epsT = epsT_t.ap
    memset("dve", epsT[:, 0:1], EPS, [epsT_t])
    memset("dve", epsT[:, 1:2], 1.0, [epsT_t])
    P.dma(cf32, [("sp", cf32.ap, cf32_d)], writes=[cf32])
    P.dma(selS, [("sp", selS.ap, sel_d)], writes=[selS])
    P.dma(cbf, [("pool", cbf.ap, cbf_d)], writes=[cbf])
    P.dma(smp, [("sp", smp.ap, smallp)], writes=[smp])

    arena.reset()
    cvs = arena.get([128, 8, 2], F32, "cvs")
    csb = T(P.sb("csb", [128, 8, 2], BF16)[:], "csb")
    badaS = T(P.sb("badaS", [128, NL, 48], F32)[:], "bada")
    ngS = T(P.sb("ngS", [128, 2, NL, 8], F32)[:], "ngS")
    P.dma(cvs, [("sp", cvs.ap, cvec)], writes=[cvs])
    P.dma(badaS, [("sp", badaS.ap, b_ada)], writes=[badaS])
    P.dma(ngS, [("sp", ngS[:, 0], n1g), ("sp", ngS[:, 1], n2g)], writes=[ngS])
    act(csb.ap, cvs.ap, AF.Silu, [cvs], [csb])

    def mod_group(lm, ng, wring_):
        w = wring_.next()
        P.dma(w, [("pool", w.ap, w_ada[lm][:, :, ng * 512:(ng + 1) * 512])], writes=[w])
        bk = nb()
        for j in range(4):
            for kc in range(8):
                mm(bk[:, 2 * j:2 * j + 2], w[:, kc, j * 128:(j + 1) * 128], csb[:, kc, :], kc == 0, kc == 7, [w, csb], [bk])
        m6, c0 = divmod(ng * 4, 8)
        tt("dve", modT[:, lm, m6, c0:c0 + 4, :], bk[:, 0:8].rearrange("p (j g) -> p j g", g=2),
           bc(badaS[:, lm, ng * 4:ng * 4 + 4], 2, [128, 4, 2]), ALU.add, [bk, badaS], [modT])

    def mod_finish(lm):
        for which, mi in ((0, 1), (1, 4)):
            ts("dve", gsT[:, lm, which], modT[:, lm, mi], 1.0, None, ALU.add, None, [modT], [gsT])
            tt("dve", gsT[:, lm, which], gsT[:, lm, which], bc(ngS[:, which, lm, :], 2, [128, 8, 2]), ALU.mult, [gsT, ngS], [gsT])

    wring = Ring([arena.get([128, 8, 512], BF16, "wada") for _ in range(3)])
    try:
        ckpt(0)
        for ng in range(12):
            mod_group(0, ng, wring)
        mod_finish(0)
        dbg("modT", modT)
        P.barrier()

        def norm_mod(xt, hdst, hdst_t, l, which, g, sq, rs_ring, tmp_ring):
            act(sq.ap, xt.ap, AF.Square, [xt], [sq])
            bk = nb()
            for c in range(8):
                mm(bk.ap, ONES_b, sq[:, c, :], c == 0, c == 7, [sq, cbf], [bk])
            rs = rs_ring.next()
            rstd_act(rs.ap, bk.ap, 1024.0, [bk], [rs])
            sh = 0 if which == 0 else 3
            for c in range(8):
                tm = tmp_ring.next()
                tt("dve", tm.ap, xt[:, c, :], rs.ap, ALU.mult, [xt, rs], [tm])
                act(hdst[:, c, :], tm.ap, AF.Identity, [tm, gsT, modT], [hdst_t],
                    scale=gsT[:, l, which, c, g:g + 1], bias=modT[:, l, sh, c, g:g + 1])

        for l in range(depth):
            xsrc = xT_in if l == 0 else xT_out
            lam_init = 0.8 - 0.6 * math.exp(-0.3 * l)
            P.barrier()
            arena.reset()
            P.dma(bcl, [("sp", bcl.ap, bcp[l:l + 1, :].partition_broadcast(128))], writes=[bcl])
            Dbc = bcl[:, 0:512]
            NGbc = bcl[:, 512:1024]
            dtb = bcl[:, 1024:1040]
            act(lay[:, 0:16], bcl[:, 1040:1056], AF.Exp, [bcl], [lay])
            ts("dve", lay[:, 0:16], lay[:, 0:16], -1.0, None, ALU.mult, None, [lay], [lay])
            aneg = lay[:, 0:16]
            tt("dve", lay[:, 32:64], bcl[:, 1056:1088], bcl[:, 1088:1120], ALU.mult, [bcl], [lay])
            P.op("dve", lambda e: e.reduce_sum(out=lay[:, 16:17], in_=lay[:, 32:64], axis=AX.X), [lay], [lay])
            tt("dve", lay[:, 32:64], bcl[:, 1120:1152], bcl[:, 1152:1184], ALU.mult, [bcl], [lay])
            P.op("dve", lambda e: e.reduce_sum(out=lay[:, 17:18], in_=lay[:, 32:64], axis=AX.X), [lay], [lay])
            act(lay[:, 16:18], lay[:, 16:18], AF.Exp, [lay], [lay])
            tt("dve", lay[:, 18:19], lay[:, 17:18], lay[:, 16:17], ALU.subtract, [lay], [lay])
            ts("dve", lay[:, 18:19], lay[:, 18:19], -lam_init, None, ALU.add, None, [lay], [lay])
            nlam = lay[:, 18:19]
            ts("dve", lay[:, 19:20], smp[:, l, 7:8], 1.0 - lam_init, None, ALU.mult, None, [smp], [lay])
            subg = lay[:, 19:20]

            wuq = arena.get([128, 2, 384], BF16, "wuq")
            wukv = arena.get([128, 640], BF16, "wukv")
            wtm = arena.get([128, 8, 784], BF16, "wtm")
            P.dma(wuq, [("pool", wuq.ap, w_uq[l])], writes=[wuq])
            P.dma(wukv, [("pool", wukv.ap, w_ukv[l])], writes=[wukv])
            P.dma(wtm, [("pool", wtm.ap, w_tm[l])], writes=[wtm])
            base_off = arena.off

            for (t0, n, nseq, L, ctx, g) in ((0, 512, 2, 256, False, 0), (512, 2048, 1, 2048, True, 1)):
                P.barrier()
                arena.off = base_off
                ntile = n // 512
                nch = n // 128
                nchs = L // 128
                hT = arena.get([128, 8, n], BF16, "hT")
                offA = arena.off
                xring = Ring([arena.get([128, 8, 512], F32, "xt") for _ in range(2)])
                sqA = arena.get([128, 8, 512], BF16, "sq")
                rsr = Ring([arena.get([128, 512], F32, "rs") for _ in range(2)])
                tmr = Ring([arena.get([128, 512], F32, "tm") for _ in range(2)])
                for tt_ in range(ntile):
                    ta = t0 + tt_ * 512
                    xt = xring.next()
                    xb = xbufs[ta // 512]
                    P.dma(xt, [("sp", xt.ap, xsrc[:, :, ta:ta + 512])], reads=[xb], writes=[xt])
                    norm_mod(xt, hT[:, :, tt_ * 512:(tt_ + 1) * 512], hT, l, 0, g, sqA, rsr, tmr)
                if l == 0 and t0 == 0:
                    dbg("hT", hT)

                ckpt(1)
                P.barrier()
                arena.off = offA
                xtok = arena.get([128, nch, 512], BF16, "xtok")
                CT = arena.get([128, n], BF16, "CT")
                dtall = arena.get([128, nch, 16], F32, "dtall")
                acum = arena.get([128, nch, 16], F32, "acum")
                hpf = arena.get([128, nch, 512], BF16, "hpf")
                hpb = arena.get([128, nch, 512], BF16, "hpb")
                BTz = arena.get([128, 2, n], BF16, "BTz")
                memset("pool", hpf.ap, 0.0, [hpf])
                memset("pool", hpb.ap, 0.0, [hpb])
                memset("pool", BTz.ap, 0.0, [BTz])
                offB2 = arena.off
                Btok = arena.get([128, nch, 128], BF16, "Btok")
                BT = arena.get([128, n], BF16, "BT")
                aall = arena.get([128, nch, 16], F32, "aall")
                cdall = arena.get([128, nch, 16], F32, "cdall")
                SnB = arena.get([128, nch, 256], BF16, "SnB")
                stF = arena.get([128, 256], F32, "stF")
                stB = arena.get([128, 256], F32, "stB")
                offB = arena.off
                wxr = Ring([arena.get([128, 8, 128], BF16, "wx") for _ in range(2)])
                prer = Ring([arena.get([128, nseq, L + 4], BF16, "pre") for _ in range(2)])
                accr = Ring([arena.get([128, nseq, L], F32, "cacc") for _ in range(2)])
                xcr = Ring([arena.get([128, n], BF16, "xc") for _ in range(2)])
                def conv_item(c6):
                    wx = wxr.next()
                    P.dma(wx, [("pool", wx.ap, w_fm[l][:, :, 928 + c6 * 128:928 + (c6 + 1) * 128])], writes=[wx])
                    pre = prer.next()
                    memset("pool", pre[:, :, 0:2], 0.0, [pre])
                    memset("pool", pre[:, :, L + 2:L + 4], 0.0, [pre])
                    for tt_ in range(ntile):
                        bk = nb()
                        for kc in range(8):
                            mm(bk.ap, wx[:, kc, :], hT[:, kc, tt_ * 512:(tt_ + 1) * 512], kc == 0, kc == 7, [wx, hT], [bk])
                        if L >= 512:
                            s_, o_ = divmod(tt_ * 512, L)
                            cp("act", pre[:, s_, 2 + o_:2 + o_ + 512], bk.ap, [bk], [pre])
                        else:
                            k_ = 512 // L
                            cp("act", pre[:, tt_ * k_:(tt_ + 1) * k_, 2:2 + L], bk.ap.rearrange("p (s t) -> p s t", s=k_), [bk], [pre])
                    yield
                    ca = accr.next()
                    ts("dve", ca.ap, pre[:, :, 0:L], smp[:, l, 8 + c6 * 5:9 + c6 * 5], None, ALU.mult, None, [pre, smp], [ca])
                    for k in range(1, 5):
                        stt(ca.ap, pre[:, :, k:k + L], smp[:, l, 8 + c6 * 5 + k:9 + c6 * 5 + k], ca.ap, ALU.mult, ALU.add, [pre, smp, ca], [ca])
                    caf = ca.ap.rearrange("p s t -> p (s t)")
                    if c6 < 4:
                        xc = xcr.next()
                        dstT = xc
                    elif c6 == 4:
                        dstT = BT
                    else:
                        dstT = CT
                    act(dstT.ap, caf, AF.Silu, [ca, smp], [dstT], bias=smp[:, l, 38 + c6:39 + c6])
                    if c6 <= 4:
                        for ch0 in range(0, nch, 4):
                            bk = nb()
                            bkb = bk.ap.bitcast(BF16)
                            for q in range(4):
                                tr(bkb[:, q * 128:(q + 1) * 128], dstT[:, (ch0 + q) * 128:(ch0 + q + 1) * 128], ID_b, [dstT, cbf], [bk])
                            src = bkb[:, 0:512].rearrange("p (q f) -> p q f", q=4)
                            if c6 < 4:
                                cp("dve", xtok[:, ch0:ch0 + 4, c6 * 128:(c6 + 1) * 128], src, [bk], [xtok])
                            else:
                                cp("dve", Btok[:, ch0:ch0 + 4, :], src, [bk], [Btok])
                run_pipe2([conv_item(c6_) for c6_ in range(6)])
                if l == 0 and t0 == 0:
                    dbg("xtok", xtok)
                    dbg("CT", CT)
                ckpt(2)
                P.barrier()
                arena.off = offB
                cp("act", BTz[0:64, 0, :], BT[0:64, :], [BT], [BTz])
                cp("act", BTz[64:128, 1, :], BT[64:128, :], [BT], [BTz])
                xddr = Ring([arena.get([128, 2, 512], BF16, "xdd") for _ in range(2)])
                tmpD = arena.get([128, nch, 16], F32, "tmpD")
                wdd = arena.get([128, nch, 16], F32, "wdd")
                bkd = nb()
                for ch in range(nch):
                    for kc in range(8):
                        mm(bkd[:, ch * 16:(ch + 1) * 16], hT[:, kc, ch * 128:(ch + 1) * 128], wtm[:, kc, 768:784], kc == 0, kc == 7, [hT, wtm], [bkd])
                bkd3 = bkd[:, 0:nch * 16].rearrange("p (c k) -> p c k", k=16)
                tt("dve", tmpD.ap, bkd3, bc(dtb, 1, [128, nch, 16]), ALU.add, [bkd, bcl], [tmpD])
                act(tmpD.ap, tmpD.ap, AF.Exp, [tmpD], [tmpD])
                act(dtall.ap, tmpD.ap, AF.Ln, [tmpD], [dtall], bias=epsT[:, 1:2])
                tt("dve", aall.ap, dtall.ap, bc(aneg, 1, [128, nch, 16]), ALU.mult, [dtall, lay], [aall])
                bkc2 = nb()
                for ch in range(nch):
                    mm(bkc2[:, ch * 32:ch * 32 + 8], U_f, aall[:, ch, 0:8], True, True, [cf32, aall], [bkc2])
                    mm(bkc2[:, ch * 32 + 8:ch * 32 + 16], UT_f, aall[:, ch, 8:16], True, True, [cf32, aall], [bkc2])
                    mm(bkc2[:, ch * 32 + 16:ch * 32 + 32], ONES_f, aall[:, ch, :], True, True, [cf32, aall], [bkc2])
                bkc3 = bkc2[:, 0:nch * 32].rearrange("p (c k) -> p c k", k=32)
                cp("dve", acum.ap, bkc3[:, :, 0:16], [bkc2], [acum])
                tt("dve", wdd.ap, bkc3[:, :, 16:32], acum.ap, ALU.subtract, [bkc2, acum], [wdd])
                act(wdd.ap, wdd.ap, AF.Exp, [wdd], [wdd])
                act(cdall.ap, bkc3[:, :, 16:32], AF.Exp, [bkc2], [cdall])
                tt("dve", wdd.ap, wdd.ap, dtall.ap, ALU.mult, [wdd, dtall], [wdd])

                def passA_item(ch):
                    s_, cl = divmod(ch, nchs)
                    xdd = xddr.next()
                    for d_ in range(2):
                        tt("dve", xdd[:, d_, :].rearrange("p (h q) -> p h q", h=8), xtok[:, ch, :].rearrange("p (h q) -> p h q", h=8),
                           bc(wdd[:, ch, d_ * 8:(d_ + 1) * 8], 2, [128, 8, 64]), ALU.mult, [xtok, wdd], [xdd])
                    bk3 = nb()
                    mm(bk3.ap, Btok[:, ch, :], xdd[:, 0, :], True, True, [Btok, xdd], [bk3])
                    bk4 = nb()
                    mm(bk4.ap, Btok[:, ch, :], xdd[:, 1, :], True, True, [Btok, xdd], [bk4])
                    yield
                    if cl == 0:
                        if ctx:
                            P.dma(stF, [("sp", stF.ap, st0[l, 0])], writes=[stF])
                        else:
                            memset("dve", stF.ap, 0.0, [stF])
                    cp("act", hpf[0:64, ch, 0:256], stF[0:64, :], [stF], [hpf])
                    cp("act", hpf[64:128, ch, 256:512], stF[64:128, :], [stF], [hpf])
                    for g2 in range(2):
                        rws = slice(g2 * 64, (g2 + 1) * 64)
                        v = stF[rws, :].rearrange("p (h q) -> p h q", h=4)
                        tt("dve", v, v, bc(cdall[rws, ch, g2 * 4:(g2 + 1) * 4], 2, [64, 4, 64]), ALU.mult, [stF, cdall], [stF])
                        tt("dve", stF[rws, :], stF[rws, :], bk3[rws, g2 * 256:(g2 + 1) * 256], ALU.add, [stF, bk3], [stF])
                        cp("act", SnB[rws, ch, :], bk4[rws, g2 * 256:(g2 + 1) * 256], [bk4], [SnB])
                    if cl == nchs - 1 and not ctx:
                        P.dma(stF, [("sp", st_o[l, s_, 0], stF.ap)], reads=[stF], is_out=True)
                run_pipe2([passA_item(ch_) for ch_ in range(nch)])
                ckpt(3)
                for s_ in range(nseq):
                    for cl in range(nchs - 1, -1, -1):
                        ch = s_ * nchs + cl
                        if cl == nchs - 1:
                            if ctx:
                                P.dma(stB, [("sp", stB.ap, st0[l, 1])], writes=[stB])
                            else:
                                memset("dve", stB.ap, 0.0, [stB])
                        cp("act", hpb[0:64, ch, 0:256], stB[0:64, :], [stB], [hpb])
                        cp("act", hpb[64:128, ch, 256:512], stB[64:128, :], [stB], [hpb])
                        for g2 in range(2):
                            rws = slice(g2 * 64, (g2 + 1) * 64)
                            v = stB[rws, :].rearrange("p (h q) -> p h q", h=4)
                            tt("dve", v, v, bc(cdall[rws, ch, 8 + g2 * 4:8 + (g2 + 1) * 4], 2, [64, 4, 64]), ALU.mult, [stB, cdall], [stB])
                            tt("dve", stB[rws, :], stB[rws, :], SnB[rws, ch, :], ALU.add, [stB, SnB], [stB])
                        if cl == 0 and not ctx:
                            P.dma(stB, [("sp", st_o[l, s_, 1], stB.ap)], reads=[stB], is_out=True)
                ckpt(4)
                use_rot(4)
                P.barrier()
                arena.off = offB2
                szr = Ring([arena.get([128, 512], F32, "sz") for _ in range(2)])
                acTr = Ring([arena.get([16, 128], F32, "acT") for _ in range(2)])
                LTp = arena.get([128, 16, 128], F32, "LTp")
                LTr = Ring([arena.get([128, 16, 128], BF16, "LT") for _ in range(2)])
                Ear = Ring([arena.get([128, 16, 128], BF16, "Ea") for _ in range(2)])
                scr = Ring([arena.get([128, 2, 8, 128], BF16, "sc") for _ in range(2)])
                Csr = Ring([arena.get([128, 16, 128], BF16, "Cs") for _ in range(2)])
                xdtr = Ring([arena.get([128, 2, 512], BF16, "xdt") for _ in range(2)])
                y1r = Ring([arena.get([128, 512], F32, "y1") for _ in range(2)])
                y2r = Ring([arena.get([128, 512], F32, "y2") for _ in range(2)])
                ybr = Ring([arena.get([128, 512], BF16, "yb") for _ in range(2)])
                junk = arena.get([128, 512], BF16, "junk")
                s1r = Ring([arena.get([128, 2], F32, "s1") for _ in range(4)])
                ocr = Ring([arena.get([128, 4, 128], BF16, "ocs") for _ in range(2)])
                def ssd_s1(ch):
                    bkz = nb()
                    for kc in range(8):
                        mm(bkz.ap, hT[:, kc, ch * 128:(ch + 1) * 128], wtm[:, kc, 256:768], kc == 0, kc == 7, [hT, wtm], [bkz])
                    sz = szr.next()
                    act(sz.ap, bkz.ap, AF.Silu, [bkz], [sz])
                    bkt = nb()
                    tr(bkt[0:16, 0:128], acum[:, ch, :], ID_f, [acum, cf32], [bkt])
                    acT = acTr.next()
                    cp("dve", acT.ap, bkt[0:16, 0:128], [bkt], [acT])
                    for hd in range(16):
                        b_ = acc[hd // 4]
                        mm(b_[:, (hd % 4) * 128:(hd % 4 + 1) * 128], selS[0:16, hd * 128:(hd + 1) * 128], acT.ap, True, True, [selS, acT], [b_])
                    ckpt(4.1)
                    for hd in range(16):
                        b_ = acc[hd // 4]
                        stt(LTp[:, hd, :], b_[:, (hd % 4) * 128:(hd % 4 + 1) * 128], acum[:, ch, hd:hd + 1], NMF if hd < 8 else NMB,
                            ALU.subtract, ALU.add, [b_, acum, cf32], [LTp])
                    LT = LTr.next()
                    act(LT.ap, LTp.ap, AF.Exp, [LTp], [LT])
                    Ea = Ear.next()
                    for q in range(4):
                        act(Ea[:, q * 4:(q + 1) * 4, :], acc[q].ap.rearrange("p (a b) -> p a b", a=4), AF.Exp, [acc[q]], [Ea])
                    ckpt(4.2)
                    bkc = nb()
                    for g2 in range(2):
                        mm(bkc[:, g2 * 128:(g2 + 1) * 128], BTz[:, g2, ch * 128:(ch + 1) * 128], CT[:, ch * 128:(ch + 1) * 128], True, True, [BTz, CT], [bkc])
                    sc = scr.next()
                    for d_ in range(2):
                        for g2 in range(2):
                            tt("dve", sc[:, d_, g2 * 4:(g2 + 1) * 4, :], bc(bkc[:, g2 * 128:(g2 + 1) * 128], 1, [128, 4, 128]),
                               LT[:, d_ * 8 + g2 * 4:d_ * 8 + (g2 + 1) * 4, :], ALU.mult, [bkc, LT], [sc])
                    Cs = Csr.next()
                    tt("dve", Cs.ap, bc(CT[:, ch * 128:(ch + 1) * 128], 1, [128, 16, 128]), Ea.ap, ALU.mult, [CT, Ea], [Cs])
                    ckpt(4.3)
                    xdt = xdtr.next()
                    for d_ in range(2):
                        tt("dve", xdt[:, d_, :].rearrange("p (h q) -> p h q", h=8), xtok[:, ch, :].rearrange("p (h q) -> p h q", h=8),
                           bc(dtall[:, ch, d_ * 8:(d_ + 1) * 8], 2, [128, 8, 64]), ALU.mult, [xtok, dtall], [xdt])
                    return sz, sc, Cs, xdt

                def ssd_s2(ch, carry):
                    sz, sc, Cs, xdt = carry
                    bky = nb()
                    for h in range(8):
                        g2, hl = divmod(h, 4)
                        rws = slice(g2 * 64, (g2 + 1) * 64)
                        o_ = bky[:, h * 64:(h + 1) * 64]
                        mm(o_, sc[:, 0, h, :], xdt[:, 0, h * 64:(h + 1) * 64], True, False, [sc, xdt], [bky])
                        mm(o_, sc[:, 1, h, :], xdt[:, 1, h * 64:(h + 1) * 64], False, False, [sc, xdt], [bky])
                        mm(o_, Cs[:, h, :], hpf[:, ch, h * 64:(h + 1) * 64], False, False, [Cs, hpf], [bky])
                        mm(o_, Cs[:, 8 + h, :], hpb[:, ch, h * 64:(h + 1) * 64], False, True, [Cs, hpb], [bky])
                    ckpt(4.4)
                    y1 = y1r.next()
                    tt("pool", y1.ap, xtok[:, ch, :], Dbc, ALU.mult, [xtok, bcl], [y1])
                    tt("dve", y1.ap, bky.ap, y1.ap, ALU.add, [bky, y1], [y1])
                    y2 = y2r.next()
                    tt("dve", y2.ap, y1.ap, sz.ap, ALU.mult, [y1, sz], [y2])
                    ckpt(4.5)
                    s1 = s1r.next()
                    act(junk.ap, y2.ap, AF.Square, [y2], [junk, s1], accum=s1[:, 0:1])
                    rstd_act(s1[:, 1:2], s1[:, 0:1], 512.0, [s1], [s1])
                    yb = ybr.next()
                    stt(yb.ap, y2.ap, s1[:, 1:2], NGbc, ALU.mult, ALU.mult, [y2, s1, bcl], [yb])
                    bk = nb()
                    bkb = bk.ap.bitcast(BF16)
                    for q in range(4):
                        tr(bkb[:, q * 128:(q + 1) * 128], yb[:, q * 128:(q + 1) * 128], ID_b, [yb, cbf], [bk])
                    ckpt(4.6)
                    ocs = ocr.next()
                    cp("act", ocs.ap, bkb[:, 0:512].rearrange("p (q f) -> p q f", q=4), [bk], [ocs])
                    ta_ = t0 + ch * 128
                    P.dma(ocs, [("sp", mixD[:, 4:8, ta_:ta_ + 128], ocs.ap)], reads=[ocs], writes=[mixbufs[ta_ // 512]])

                carry_ = {}
                for ch in range(nch + 1):
                    if ch < nch:
                        carry_[ch] = ssd_s1(ch)
                    if ch >= 1:
                        ssd_s2(ch - 1, carry_.pop(ch - 1))
                ckpt(5)
                use_rot(8)
                P.barrier()
                arena.off = offA
                nk = L + (256 if ctx else 0)
                nkt = nk // 128
                koff = 256 if ctx else 0
                wat = arena.get([128, 8, 928], BF16, "wat")
                P.dma(wat, [("pool", wat.ap, w_fm[l][:, :, 0:928])], writes=[wat])
                rope = arena.get([128, 4, 2048], F32, "rope") if False else None
                ropeA = arena.get([128, 2, 2048], F32, "ropeA")
                if ctx:
                    P.dma(ropeA, [("sp", ropeA.ap, rope_d[:, 0:2, :])], writes=[ropeA])
                rope = ropeA
                rbase = 0
                qaT = arena.get([128, 4, n], BF16, "qaT")
                kaT = arena.get([128, 4, nseq * nk], BF16, "kaT")
                vaA = arena.get([128, nseq * nkt, 4, 128], BF16, "vaA")
                cqn = arena.get([128, 2, 512], BF16, "cqn")
                ckvn = arena.get([128, nseq * nk], BF16, "ckvn")
                krb = arena.get([128, nseq * nk], BF16, "krb")
                offC = arena.off
                memset("pool", vaA.ap, 1.0, [vaA])
                ckvf_r = Ring([arena.get([128, 512], F32, "ckvf") for _ in range(2)])
                krf_r = Ring([arena.get([128, 512], F32, "krf") for _ in range(2)])
                sqr = Ring([arena.get([128, 2, 512], BF16, "sqc") for _ in range(2)])
                rsr = Ring([arena.get([128, 512], F32, "rs") for _ in range(2)])
                cqf_r = Ring([arena.get([128, 2, 512], F32, "cqf") for _ in range(1)])
                qn_r = Ring([arena.get([128, 512], F32, "qn") for _ in range(2)])
                qnb_r = Ring([arena.get([128, 512], BF16, "qnb") for _ in range(2)])
                t1_r = Ring([arena.get([128, 512], F32, "t1") for _ in range(2)])
                t2_r = Ring([arena.get([128, 512], F32, "t2") for _ in range(2)])
                cpyr = [Ring([arena.get([128, 512], F32, "cpy") for _ in range(2)])]

                ckpt(5.1)
                def keycol(s_, tl, w):
                    return slice(s_ * nk + koff + tl, s_ * nk + koff + tl + w)

                if ctx:
                    cst = ckvf_r.next()
                    P.dma(cst, [("sp", cst[:, 0:256], c_ckvT[l])], writes=[cst])
                    cp("act", ckvn[:, 0:256], cst[:, 0:256], [cst], [ckvn])
                    kst = krf_r.next()
                    P.dma(kst, [("sp", kst[:, 0:256], c_krT[l])], writes=[kst])
                    cp("act", krb[0:32, 0:256], kst[0:32, 0:256], [kst], [krb])

                def run_pipe(gens):
                    prev = None
                    for g_ in gens:
                        next(g_)
                        if prev is not None:
                            for _ in prev:
                                pass
                        prev = g_
                    if prev is not None:
                        for _ in prev:
                            pass

                def norm_item(emit_src, rows, lhsT_ones, D, gcol, ncol, plain, roped, P_l=None, post=None):
                    bk = emit_src()
                    sq = sqr.next()
                    cpy = cpyr[0].next()
                    cp("dve", cpy[0:rows, 0:ncol], bk[0:rows, 0:ncol], [bk], [cpy])
                    act(sq[0:rows, 0, 0:ncol], cpy[0:rows, 0:ncol], AF.Square, [cpy], [sq])
                    bk2 = nb()
                    mm(bk2[0:rows, 0:ncol], lhsT_ones, sq[0:rows, 0, 0:ncol], True, True, [sq, cbf], [bk2])
                    rs = rsr.next()
                    act(rs[0:rows, 0:ncol], bk2[0:rows, 0:ncol], AF.Ln, [bk2], [rs], scale=1.0 / D, bias=epsT[0:rows, 0:1])
                    yield
                    act(rs[0:rows, 0:ncol], rs[0:rows, 0:ncol], AF.Exp, [rs], [rs], scale=-0.5)
                    if not roped:
                        for (c0_, w_, dst_ap, dst_t) in plain:
                            stt(dst_ap, cpy[0:rows, c0_:c0_ + w_], gcol, rs[0:rows, c0_:c0_ + w_], ALU.mult, ALU.mult, [cpy, rs, smp], [dst_t])
                    else:
                        qn = qnb_r.next()
                        stt(qn[0:rows, 0:ncol], cpy[0:rows, 0:ncol], gcol, rs[0:rows, 0:ncol], ALU.mult, ALU.mult, [cpy, rs, smp], [qn])
                        for (c0_, w_, dst_ap, dst_t) in plain:
                            cp("act", dst_ap, qn[0:rows, c0_:c0_ + w_], [qn], [dst_t])
                        for (c0_, w_, p0_, dst_ap, dst_t) in roped:
                            bk3 = nb()
                            mm(bk3[0:rows, 0:w_], P_l, qn[0:rows, c0_:c0_ + w_], True, True, [qn, cbf], [bk3])
                            t1 = t1_r.next()
                            tt("pool", t1[0:rows, 0:w_], qn[0:rows, c0_:c0_ + w_], rope[0:rows, 0, p0_:p0_ + w_], ALU.mult, [qn, rope], [t1])
                            t2 = t2_r.next()
                            tt("dve", t2[0:rows, 0:w_], bk3[0:rows, 0:w_], rope[0:rows, 1, p0_:p0_ + w_], ALU.mult, [bk3, rope], [t2])
                            tt("dve", dst_ap, t1[0:rows, 0:w_], t2[0:rows, 0:w_], ALU.add, [t1, t2], [dst_t])
                    if post is not None:
                        post()

                def tile_segs(tt_):
                    if L >= 512:
                        s_, o_ = divmod(tt_ * 512, L)
                        return [(s_, o_, 0, 512)]
                    k_ = 512 // L
                    return [(tt_ * k_ + i, 0, i * L, L) for i in range(k_)]

                def cq_item(tt_):
                    tsl = slice(tt_ * 512, (tt_ + 1) * 512)
                    cqf = cqf_r.next()
                    sq = sqr.next()
                    for c in range(2):
                        bk = nb()
                        for kc in range(8):
                            mm(bk.ap, wat[:, kc, c * 128:(c + 1) * 128], hT[:, kc, tsl], kc == 0, kc == 7, [wat, hT], [bk])
                        cp("dve", cqf[:, c, :], bk.ap, [bk], [cqf])
                        act(sq[:, c, :], cqf[:, c, :], AF.Square, [cqf], [sq])
                    bk = nb()
                    for c in range(2):
                        mm(bk.ap, ONES_b, sq[:, c, :], c == 0, c == 1, [sq, cbf], [bk])
                    rs = rsr.next()
                    act(rs.ap, bk.ap, AF.Ln, [bk], [rs], scale=1.0 / 256.0, bias=epsT[:, 0:1])
                    yield
                    act(rs.ap, rs.ap, AF.Exp, [rs], [rs], scale=-0.5)
                    for c in range(2):
                        stt(cqn[:, c, :], cqf[:, c, :], smp[:, l, c:c + 1], rs.ap, ALU.mult, ALU.mult, [cqf, rs, smp], [cqn])

                def ckv_item(tt_):
                    tsl = slice(tt_ * 512, (tt_ + 1) * 512)
                    ckvf = ckvf_r.next()

                    def src():
                        bk = nb()
                        for kc in range(8):
                            mm(bk.ap, wat[:, kc, 256:384], hT[:, kc, tsl], kc == 0, kc == 7, [wat, hT], [bk])
                        return bk

                    def post():
                        for (s_, o_, c0, w_) in tile_segs(tt_):
                            cp("act", ckvn[:, keycol(s_, o_, w_)], ckvf[:, c0:c0 + w_], [ckvf], [ckvn])
                        if not ctx:
                            P.dma(ckvf, [("sp", ckv_o[l][:, t0 + tt_ * 512:t0 + (tt_ + 1) * 512], ckvf.ap)], reads=[ckvf], is_out=True)
                    return norm_item(src, 128, ONES_b, 128.0, smp[:, l, 2:3], 512, [(0, 512, ckvf.ap, ckvf)], [], post=post)

                def kr_item(tt_):
                    tsl = slice(tt_ * 512, (tt_ + 1) * 512)
                    bk = nb()
                    for kc in range(8):
                        mm(bk[0:32, :], wat[:, kc, 384:416], hT[:, kc, tsl], kc == 0, kc == 7, [wat, hT], [bk])
                    krf = krf_r.next()
                    cp("dve", krf[0:32, :], bk[0:32, :], [bk], [krf])
                    yield
                    for (s_, o_, c0, w_) in tile_segs(tt_):
                        cp("act", krb[0:32, keycol(s_, o_, w_)], krf[0:32, c0:c0 + w_], [krf], [krb])
                    if not ctx:
                        P.dma(krf, [("sp", kr_o[l][:, t0 + tt_ * 512:t0 + (tt_ + 1) * 512], krf[0:32, :])], reads=[krf], is_out=True)

                def q_item(tt_, h):
                    tsl = slice(tt_ * 512, (tt_ + 1) * 512)

                    def src():
                        bk = nb()
                        for c in range(2):
                            mm(bk[0:96, :], wuq[:, c, h * 96:(h + 1) * 96], cqn[:, c, :], c == 0, c == 1, [wuq, cqn], [bk])
                        return bk
                    if ctx:
                        return norm_item(src, 96, ONES_b[0:96, 0:96], 96.0, smp[0:96, l, 3:4], 512, [], [(0, 512, tt_ * 512, qaT[0:96, h, tsl], qaT)], P_l=P96[0:96, 0:96])
                    return norm_item(src, 96, ONES_b[0:96, 0:96], 96.0, smp[0:96, l, 3:4], 512, [(0, 512, qaT[0:96, h, tsl], qaT)], [])

                gens = []
                for tt_ in range(ntile):
                    gens.append(cq_item(tt_))
                    gens.append(ckv_item(tt_))
                    gens.append(kr_item(tt_))
                    for h in range(4):
                        gens.append(q_item(tt_, h))
                run_pipe(gens)

                ckpt(6)
                ktot = nseq * nk

                def k_item(kb, h):
                    w_ = min(512, ktot - kb)
                    ksl = slice(kb, kb + w_)

                    def src():
                        bk = nb()
                        mm(bk[0:96, 0:w_], SHIFT[0:32, 0:96], krb[0:32, ksl], True, False, [cbf, krb], [bk])
                        mm(bk[0:96, 0:w_], wukv[:, h * 96:(h + 1) * 96], ckvn[:, ksl], False, True, [wukv, ckvn], [bk])
                        return bk
                    if ctx:
                        if kb == 0:
                            plain = [(0, 256, kaT[0:96, h, 0:256], kaT)]
                            roped = [(256, 256, 0, kaT[0:96, h, 256:512], kaT)]
                        else:
                            plain = []
                            roped = [(0, w_, kb - 256, kaT[0:96, h, kb:kb + w_], kaT)]
                        return norm_item(src, 96, ONES_b[0:96, 0:96], 96.0, smp[0:96, l, 4:5], w_, plain, roped, P_l=P96[0:96, 0:96])
                    return norm_item(src, 96, ONES_b[0:96, 0:96], 96.0, smp[0:96, l, 4:5], w_, [(0, w_, kaT[0:96, h, ksl], kaT)], [])

                def v_item(kt):
                    bk = nb()
                    mm(bk[:, 0:256], ckvn[:, kt * 128:(kt + 1) * 128], wukv[:, 384:640], True, True, [ckvn, wukv], [bk])
                    yield
                    srcv = bk[:, 0:256].rearrange("p (h d) -> p h d", h=4)
                    cp("act", vaA[:, kt, 0::2, 0:64], srcv[:, 0::2, :], [bk], [vaA])
                    cp("dve", vaA[:, kt, 1::2, 64:128], srcv[:, 1::2, :], [bk], [vaA])

                gens = []
                for kb in range(0, ktot, 512):
                    w_ = min(512, ktot - kb)
                    for h in range(4):
                        gens.append(k_item(kb, h))
                    for kt in range(kb // 128, (kb + w_) // 128):
                        gens.append(v_item(kt))
                run_pipe(gens)

                if l == 0 and t0 == 0:
                    dbg("qaT", qaT)
                    dbg("kaT", kaT)

                ckpt(7)
                use_rot(4)
                P.barrier()
                arena.off = offC
                ptr = Ring([arena.get([128, 512], BF16, "pt") for _ in range(4)])
                rrr = Ring([arena.get([128, 512], F32, "rr") for _ in range(2)])
                odr = Ring([arena.get([128, 512], F32, "od") for _ in range(2)])
                o2r = Ring([arena.get([128, 512], F32, "o2") for _ in range(2)])
                sq2r = Ring([arena.get([128, 512], BF16, "sq2") for _ in range(2)])
                rs2r = Ring([arena.get([128, 512], F32, "rs2") for _ in range(2)])
                NQ = min(512, L)
                mstr = Ring([arena.get([128, 512], BF16, "mst") for _ in range(2)])

                def attn_pass(q_of, k_of, v_of, scale, accs, D=2):
                    items = [(kt, a) for kt in range(nkt) for a in range(len(accs))]
                    pts = {}
                    for j in range(len(items) + D):
                        if j < len(items):
                            kt, a = items[j]
                            ab, qv, kf, vf = accs[a]
                            bs = nb()
                            kv, kreads = kf(kt)
                            mm(bs[:, 0:NQ], kv, qv[0], True, True, kreads + qv[1], [bs])
                            pt = ptr.next()
                            act(pt[:, 0:NQ], bs[:, 0:NQ], AF.Exp, [bs], [pt], scale=scale)
                            pts[j] = pt
                        i = j - D
                        if i >= 0:
                            kt, a = items[i]
                            ab, qv, kf, vf = accs[a]
                            vv, vreads = vf(kt)
                            pt = pts.pop(i)
                            mm(ab[:, 0:NQ], vv, pt[:, 0:NQ], kt == 0, kt == nkt - 1, vreads + [pt], [ab])

                def finish_pair(ab0, ab1, dst_ap, dst_t, use_act=False):
                    rr = rrr.next()
                    if use_act:
                        act(rr[0:64, 0:NQ], ab0[64:128, 0:NQ], AF.Ln, [ab0], [rr])
                        act(rr[64:128, 0:NQ], ab1[0:64, 0:NQ], AF.Ln, [ab1], [rr])
                        act(rr[:, 0:NQ], rr[:, 0:NQ], AF.Exp, [rr], [rr], scale=-1.0)
                    else:
                        P.op("dve", lambda e: e.reciprocal(out=rr[0:64, 0:NQ], in_=ab0[64:128, 0:NQ]), [ab0], [rr])
                        P.op("dve", lambda e: e.reciprocal(out=rr[64:128, 0:NQ], in_=ab1[0:64, 0:NQ]), [ab1], [rr])
                    tt("dve", dst_ap[0:64], ab0[0:64, 0:NQ], rr[0:64, 0:NQ], ALU.mult, [ab0, rr], [dst_t])
                    tt("dve", dst_ap[64:128], ab1[64:128, 0:NQ], rr[64:128, 0:NQ], ALU.mult, [ab1, rr], [dst_t])

                mla_cnt = [0]
                for s_ in range(nseq):
                    for q0 in range(0, L, NQ):
                        qs = slice(s_ * L + q0, s_ * L + q0 + NQ)
                        for pr in range(2):
                            accs = []
                            ab_ = 2 * (mla_cnt[0] % 2)
                            mla_cnt[0] += 1
                            for i in range(2):
                                h = pr * 2 + i
                                accs.append((acc[ab_ + i], (qaT[0:96, h, qs], [qaT]),
                                             (lambda kt, h=h: (kaT[0:96, h, s_ * nk + kt * 128:s_ * nk + (kt + 1) * 128], [kaT])),
                                             (lambda kt, h=h: (vaA[:, s_ * nkt + kt, h, :], [vaA]))))
                            attn_pass(None, None, None, 96.0 ** -0.5, accs)
                            mst = mstr.next()
                            finish_pair(acc[ab_], acc[ab_ + 1], mst[:, 0:NQ], mst)
                            ta_ = t0 + s_ * L + q0
                            P.dma(mst, [("sp", mixD[:, pr, ta_:ta_ + NQ], mst[:, 0:NQ])], reads=[mst], writes=[mixbufs[ta_ // 512]])

                ckpt(8)
                use_rot(8)
                P.barrier()
                arena.off = offA
                wat2 = arena.get([128, 8, 928], BF16, "wat")
                ropeD = arena.get([128, 2, 2048], F32, "ropeD")
                if ctx:
                    P.dma(ropeD, [("sp", ropeD.ap, rope_d[:, 2:4, :])], writes=[ropeD])
                rope = ropeD
                qdT = arena.get([128, 4, n], BF16, "qdT")
                kdT = arena.get([128, 4, nseq * nk], BF16, "kdT")
                vdA = arena.get([128, nseq * nkt, 4, 128], BF16, "vdA")
                offD = arena.off
                memset("pool", vdA.ap, 1.0, [vdA])
                memset("pool", kdT[64:128, :, :], 0.0, [kdT])
                sqr = Ring([arena.get([128, 2, 512], BF16, "sqc") for _ in range(2)])
                rsr = Ring([arena.get([128, 512], F32, "rs") for _ in range(3)])
                qn_r = Ring([arena.get([128, 512], F32, "qn") for _ in range(2)])
                qnb_r = Ring([arena.get([128, 512], BF16, "qnb") for _ in range(2)])
                t1_r = Ring([arena.get([128, 512], F32, "t1") for _ in range(2)])
                t2_r = Ring([arena.get([128, 512], F32, "t2") for _ in range(2)])
                kdf_r = Ring([arena.get([128, 4, 512], F32, "kdf") for _ in range(2)])
                cpyr[0] = Ring([arena.get([128, 512], F32, "cpy") for _ in range(2)])
                vdf_r = Ring([arena.get([128, 4, 256], F32, "vdf") for _ in range(2)])
                if ctx:
                    kst = kdf_r.next()
                    P.dma(kst, [("sp", kst[:, :, 0:256], c_kdT[l])], writes=[kst])
                    cp("act", kdT[0:64, :, 0:256], kst[0:64, :, 0:256], [kst], [kdT])
                    vst = vdf_r.next()
                    P.dma(vst, [("sp", vst[:, 0:2, :], c_vd[l])], writes=[vst])
                    for kt in range(2):
                        srcv = vst[:, kt, :].rearrange("p (h d) -> p h d", h=4)
                        cp("act", vdA[:, kt, 0::2, 0:64], srcv[:, 0::2, :], [vst], [vdA])
                        cp("dve", vdA[:, kt, 1::2, 64:128], srcv[:, 1::2, :], [vst], [vdA])
                def dqk_item(tt_, which, h, kdf):
                    tsl = slice(tt_ * 512, (tt_ + 1) * 512)
                    c0 = 416 + which * 256 + h * 64
                    gcol = smp[0:64, l, 5 + which:6 + which]

                    def src():
                        bk = nb()
                        for kc in range(8):
                            mm(bk[0:64, :], wat[:, kc, c0:c0 + 64], hT[:, kc, tsl], kc == 0, kc == 7, [wat, hT], [bk])
                        return bk
                    if ctx:
                        if which == 0:
                            dst, dst_t = qdT[0:64, h, tsl], qdT
                        else:
                            dst, dst_t = kdT[0:64, h, keycol(0, tt_ * 512, 512)], kdT
                        return norm_item(src, 64, BD32[0:64, 0:64], 32.0, gcol, 512, [], [(0, 512, tt_ * 512, dst, dst_t)], P_l=P64[0:64, 0:64])
                    if which == 0:
                        return norm_item(src, 64, BD32[0:64, 0:64], 32.0, gcol, 512, [(0, 512, qdT[0:64, h, tsl], qdT)], [])

                    def post():
                        for (s_, o_, c0_, w_) in tile_segs(tt_):
                            cp("act", kdT[0:64, h, keycol(s_, o_, w_)], kdf[0:64, h, c0_:c0_ + w_], [kdf], [kdT])
                        if h == 3:
                            P.dma(kdf, [("sp", kd_o[l][:, :, t0 + tt_ * 512:t0 + (tt_ + 1) * 512], kdf[0:64, :, :])], reads=[kdf], is_out=True)
                    return norm_item(src, 64, BD32[0:64, 0:64], 32.0, gcol, 512, [(0, 512, kdf[0:64, h, :], kdf)], [], post=post)

                def vd_item(tt_, j, vdf):
                    tok0 = tt_ * 512 + j * 128
                    bk = nb()
                    for kc in range(8):
                        mm(bk[:, 0:256], hT[:, kc, tok0:tok0 + 128], wtm[:, kc, 0:256], kc == 0, kc == 7, [hT, wtm], [bk])
                    yield
                    s_, tl = divmod(tok0, L)
                    kt = s_ * nkt + (koff + tl) // 128
                    srcv = bk[:, 0:256].rearrange("p (h d) -> p h d", h=4)
                    cp("act", vdA[:, kt, 0::2, 0:64], srcv[:, 0::2, :], [bk], [vdA])
                    cp("dve", vdA[:, kt, 1::2, 64:128], srcv[:, 1::2, :], [bk], [vdA])
                    if not ctx:
                        cp("dve", vdf[:, j, :], bk[:, 0:256], [bk], [vdf])
                        if j == 3:
                            P.dma(vdf, [("sp", vd_o[l], vdf.ap)], reads=[vdf], is_out=True)

                gens = []
                for tt_ in range(ntile):
                    kdf = kdf_r.next()
                    vdf = vdf_r.next()
                    for which in range(2):
                        for h in range(4):
                            gens.append(dqk_item(tt_, which, h, kdf))
                    for j in range(4):
                        gens.append(vd_item(tt_, j, vdf))
                run_pipe(gens)


                ckpt(9)
                use_rot(4)
                P.barrier()
                arena.off = offD
                ptr = Ring([arena.get([128, 512], BF16, "pt") for _ in range(4)])
                rrr = Ring([arena.get([128, 512], F32, "rr") for _ in range(2)])
                odr = Ring([arena.get([128, 512], F32, "od") for _ in range(2)])
                o2r = Ring([arena.get([128, 512], F32, "o2") for _ in range(2)])
                sq2r = Ring([arena.get([128, 512], BF16, "sq2") for _ in range(2)])
                rs2r = Ring([arena.get([128, 512], F32, "rs2") for _ in range(2)])
                mstr = Ring([arena.get([128, 512], BF16, "mst") for _ in range(2)])
                qmr = Ring([arena.get([128, 2, 512], BF16, "qm") for _ in range(4)])
                for qm_ in qmr.tiles:
                    memset("pool", qm_.ap, 0.0, [qm_])
                wout = arena.get([128, 8, 1024], BF16, "wout")
                P.dma(wout, [("pool", wout.ap, w_out[l])], writes=[wout])
                for s_ in range(nseq):
                    for q0 in range(0, L, NQ):
                        qs = slice(s_ * L + q0, s_ * L + q0 + NQ)
                        for pr in range(2):
                            accs = []
                            for i in range(2):
                                h = pr * 2 + i
                                qm = qmr.next()
                                for m in range(2):
                                    cp("pool", qm[m * 32:(m + 1) * 32, m, 0:NQ], qdT[m * 32:(m + 1) * 32, h, qs], [qdT], [qm])
                                for m in range(2):
                                    accs.append((acc[i * 2 + m], (qm[:, m, 0:NQ], [qm]),
                                                 (lambda kt, h=h, m=m: (kdT[:, h, s_ * nk + kt * 128:s_ * nk + (kt + 1) * 128], [kdT])),
                                                 (lambda kt, h=h: (vdA[:, s_ * nkt + kt, h, :], [vdA]))))
                            attn_pass(None, None, None, 32.0 ** -0.5, accs)
                            od = odr.next()
                            o2 = o2r.next()
                            finish_pair(acc[0], acc[2], od[:, 0:NQ], od, use_act=True)
                            finish_pair(acc[1], acc[3], o2[:, 0:NQ], o2, use_act=True)
                            stt(od[:, 0:NQ], o2[:, 0:NQ], nlam, od[:, 0:NQ], ALU.mult, ALU.add, [o2, od, lay], [od])
                            sq2 = sq2r.next()
                            act(sq2[:, 0:NQ], od[:, 0:NQ], AF.Square, [od], [sq2])
                            bk = nb()
                            mm(bk[:, 0:NQ], BD64, sq2[:, 0:NQ], True, True, [sq2, cbf], [bk])
                            rs2 = rs2r.next()
                            rstd_act(rs2[:, 0:NQ], bk[:, 0:NQ], 64.0, [bk], [rs2])
                            mst = mstr.next()
                            stt(mst[:, 0:NQ], od[:, 0:NQ], subg, rs2[:, 0:NQ], ALU.mult, ALU.mult, [od, rs2, lay], [mst])
                            ta_ = t0 + s_ * L + q0
                            P.dma(mst, [("sp", mixD[:, 2 + pr, ta_:ta_ + NQ], mst[:, 0:NQ])], reads=[mst], writes=[mixbufs[ta_ // 512]])

                ckpt(10)
                use_rot(8)
                P.barrier()
                arena.off = offA
                xring = Ring([arena.get([128, 8, 512], F32, "xt") for _ in range(2)])
                mxr = Ring([arena.get([128, 8, 512], BF16, "mx") for _ in range(2)])
                do_mod = ctx and (l + 1 < depth)
                if do_mod:
                    wringD = Ring([arena.get([128, 8, 512], BF16, "wada") for _ in range(3)])
                def stepD_load(tt_):
                    ta_ = t0 + tt_ * 512
                    xt_ = xring.next()
                    P.dma(xt_, [("sp", xt_.ap, xsrc[:, :, ta_:ta_ + 512])], reads=[xbufs[ta_ // 512]], writes=[xt_])
                    mx_ = mxr.next()
                    P.dma(mx_, [("sp", mx_.ap, mixD[:, :, ta_:ta_ + 512])], reads=[mixbufs[ta_ // 512]], writes=[mx_])
                    return xt_, mx_
                nxtD = stepD_load(0)
                for tt_ in range(ntile):
                    xt, mx = nxtD
                    if tt_ + 1 < ntile:
                        nxtD = stepD_load(tt_ + 1)
                    if do_mod:
                        for g_ in range(3):
                            mod_group(l + 1, tt_ * 3 + g_, wringD)
                        if tt_ == ntile - 1:
                            mod_finish(l + 1)
                    ta = t0 + tt_ * 512
                    tsl = slice(tt_ * 512, (tt_ + 1) * 512)
                    xb = xbufs[ta // 512]
                    for m in range(8):
                        bk = nb()
                        for kc in range(8):
                            mm(bk.ap, wout[:, kc, m * 128:(m + 1) * 128], mx[:, kc, :], kc == 0, kc == 7, [wout, mx], [bk])
                        stt(xt[:, m, :], bk.ap, modT[:, l, 2, m, g:g + 1], xt[:, m, :], ALU.mult, ALU.add, [bk, modT, xt], [xt])
                    P.dma(xt, [("sp", xT_out[:, :, ta:ta + 512], xt.ap)], reads=[xt], writes=[xb], is_out=True)
                    if l == 0 and t0 == 0 and "mix" in debug:
                        dbg("mix", mx)

            ckpt(11)
            P.barrier()
            arena.reset()
            xall = arena.get([128, 8, NT], F32, "xall")
            h2T = arena.get([128, 8, NT], BF16, "h2T")
            xts = [T(xall[:, :, i * 512:(i + 1) * 512], "xall%d" % i) for i in range(5)]
            sqF = arena.get([128, 8, 512], BF16, "sqF")
            rsr = Ring([arena.get([128, 512], F32, "rs") for _ in range(2)])
            tmr = Ring([arena.get([128, 512], F32, "tm") for _ in range(2)])
            w1r = Ring([arena.get([128, 8, 512], BF16, "w1") for _ in range(2)])
            w2r = Ring([arena.get([128, 4, 1024], BF16, "w2") for _ in range(2)])
            rlr = Ring([arena.get([128, 512], BF16, "rl") for _ in range(3)])
            ur = Ring([arena.get([128, 4, 512], BF16, "u") for _ in range(2)])
            h2Ts = [T(h2T[:, :, i * 512:(i + 1) * 512], "h2T%d" % i) for i in range(5)]
            for i in range(5):
                P.dma(xts[i], [("sp", xts[i].ap, xT_out[:, :, i * 512:(i + 1) * 512])], reads=[xbufs[i]], writes=[xts[i]])

            def ffn_load(e8_):
                w1_ = w1r.next()
                w2_ = w2r.next()
                P.dma(w1_, [("pool", w1_.ap, w_ff1[l][:, :, e8_ * 512:(e8_ + 1) * 512])], writes=[w1_])
                P.dma(w2_, [("pool", w2_.ap, w_ff2[l][:, e8_ * 4:(e8_ + 1) * 4, :])], writes=[w2_])
                return w1_, w2_

            def do_norm(i):
                norm_mod(xts[i], h2Ts[i].ap, h2Ts[i], l, 1, 0 if i == 0 else 1, sqF, rsr, tmr)
            wsets = {0: ffn_load(0)}
            do_norm(0)
            do_norm(1)

            def ffn_item(e8, i):
                g = 0 if i == 0 else 1
                if e8 == 0 and i + 2 < 5:
                    do_norm(i + 2)
                w1, w2 = wsets[e8]
                u = ur.next()
                for jc in range(4):
                    bk = nb()
                    for kc in range(8):
                        mm(bk.ap, w1[:, kc, jc * 128:(jc + 1) * 128], h2Ts[i][:, kc, :], kc == 0, kc == 7, [w1, h2Ts[i]], [bk])
                    rl = rlr.next()
                    act(rl.ap, bk.ap, AF.Relu, [bk], [rl])
                    tt("pool", u[:, jc, :], rl.ap, rl.ap, ALU.mult, [rl], [u])
                yield
                if i == 0 and e8 + 1 < 8:
                    wsets[e8 + 1] = ffn_load(e8 + 1)
                for m in range(8):
                    bk = nb()
                    for jc in range(4):
                        mm(bk.ap, w2[:, jc, m * 128:(m + 1) * 128], u[:, jc, :], jc == 0, jc == 3, [w2, u], [bk])
                    stt(xts[i][:, m, :], bk.ap, modT[:, l, 5, m, g:g + 1], xts[i][:, m, :], ALU.mult, ALU.add, [bk, modT, xts[i]], [xts[i]])
            run_pipe2([ffn_item(e8_, i_) for e8_ in range(8) for i_ in range(5)])
            for i in range(5):
                P.dma(xts[i], [("sp", xT_out[:, :, i * 512:(i + 1) * 512], xts[i].ap)], reads=[xts[i]], writes=[xbufs[i]], is_out=True)

    except StopBuild:
        pass
    P.finish()
    return nc, dbg_outs, P, arena


_CACHE = {}


def kernel(**inputs):
    inp = {k: np.asarray(v) for k, v in inputs.items()}
    consts = host_consts()
    shared = prep_shared(inp)
    in_maps = []
    for core in range(8):
        d = {}
        d.update(shared)
        d.update(consts)
        d.update(prep_core(inp, core, consts))
        d.update(prep_cache(inp, core))
        in_maps.append({k: np.ascontiguousarray(v, dtype=np.float32) for k, v in d.items()})
    if "nc" not in _CACHE:
        _CACHE["nc"] = build(DEPTH_RUN)[0]
    nc = _CACHE["nc"]
    res = run_bass_kernel_spmd(nc, in_maps, core_ids=list(range(8)))
    R = res.results
    B, S, Dm = 16, 256, 1024
    y_prompt = np.zeros((16, 256, 1024), np.float32)
    y_sample = np.zeros((4, 2048, 1024), np.float32)
    new_ckv = np.zeros((16, NL, 256, 128), np.float32)
    new_kr = np.zeros((16, NL, 256, 32), np.float32)
    new_kd = np.zeros((16, NL, 256, 4, 64), np.float32)
    new_vd = np.zeros((16, NL, 256, 4, 64), np.float32)
    new_st = np.zeros((16, NL, 2, 8, 64, 64), np.float32)
    for core in range(8):
        r = R[core]
        xo = r["xT_out"].transpose(2, 1, 0).reshape(NT, 1024)
        y_prompt[2 * core] = xo[0:256]
        y_prompt[2 * core + 1] = xo[256:512]
        if core % 2 == 0:
            y_sample[core // 2] = xo[512:]
        for s in range(2):
            bidx = 2 * core + s
            new_ckv[bidx] = r["ckv_o"][:, :, s * 256:(s + 1) * 256].transpose(0, 2, 1)
            new_kr[bidx] = r["kr_o"][:, :, s * 256:(s + 1) * 256].transpose(0, 2, 1)
            kd = r["kd_o"][:, :, :, s * 256:(s + 1) * 256]
            new_kd[bidx] = kd.transpose(0, 3, 2, 1)
            vd = r["vd_o"].reshape(NL, 128, 4, 4, 64).transpose(0, 2, 1, 3, 4).reshape(NL, 512, 4, 64)
            new_vd[bidx] = vd[:, s * 256:(s + 1) * 256]
            st = r["st_o"][:, s]
            st = st.reshape(NL, 2, 2, 64, 4, 64).transpose(0, 1, 2, 4, 5, 3)
            new_st[bidx] = st.reshape(NL, 2, 8, 64, 64)
    return (y_prompt, y_sample, new_ckv, new_kr, new_kd, new_vd, new_st)
```

```python
import math
from contextlib import ExitStack
import numpy as np
import concourse.bass as bass
import concourse.mybir as mybir
from concourse.bass_utils import run_bass_kernel_spmd

F32 = mybir.dt.float32
BF16 = mybir.dt.bfloat16
AF = mybir.ActivationFunctionType
ALU = mybir.AluOpType
AX = mybir.AxisListType

NL = 4
NT = 2560
EPS = 1e-6
SEM_ROT = 30000
DEPTH_RUN = NL
DEBUG = {}


class StopBuild(Exception):
    pass


STOP_AT = [None]


def ckpt(k):
    if STOP_AT[0] is not None and k >= STOP_AT[0]:
        raise StopBuild()


class Buf:
    __slots__ = ("name", "w", "r", "dkey")

    def __init__(self, name):
        self.name = name
        self.w = None
        self.r = {}
        self.dkey = None


class T:
    __slots__ = ("ap", "b")

    def __init__(self, ap, name):
        self.ap = ap
        self.b = Buf(name)

    def __getitem__(self, k):
        return self.ap[k]


class Prog:
    ENGS = ("pe", "act", "dve", "pool", "sp")

    def __init__(self, nc):
        self.nc = nc
        self.st = ExitStack()
        self.streams = {e: [] for e in self.ENGS}
        self.cnt = {}
        self.known = {e: {} for e in self.ENGS}
        self.engkey = {e: e + "0" for e in self.ENGS}
        self.engrot = {e: 0 for e in self.ENGS}
        self.out_tokens = []
        self.nops = 0
        self.uid = 0
        self.free_dkeys = []
        self.recent = []

    def sb(self, name, shape, dt=F32):
        return self.st.enter_context(self.nc.sbuf_tensor(name, list(shape), dt))

    def ps(self, name, shape, dt=F32):
        return self.st.enter_context(self.nc.psum_tensor(name, list(shape), dt))

    def _deps(self, reads, writes, eng=None):
        deps = []
        for b in reads:
            if b.w is not None:
                deps.append(b.w)

        def own(k):
            return eng is not None and k.startswith(eng) and k[len(eng):].isdigit()
        for b in writes:
            if b.w is not None and not own(b.w[0]):
                deps.append(b.w)
            for k, v in b.r.items():
                if not own(k):
                    deps.append((k, v))
        return deps

    def _waits(self, eng, deps):
        need = {}
        kn = self.known[eng]
        for k, v in deps:
            if eng == "pe" and k.startswith("pe"):
                continue
            if kn.get(k, 0) >= v:
                continue
            if need.get(k, 0) < v:
                need[k] = v
        for k, v in need.items():
            kn[k] = v
            self.streams[eng].append(("wait", k, v))

    def _mark(self, tok, reads, writes):
        k, v = tok
        for b in reads:
            if b.r.get(k, 0) < v:
                b.r[k] = v
        for b in writes:
            b.w = tok
            b.r = {}

    def op(self, eng, fn, reads=(), writes=()):
        reads = [r.b if isinstance(r, T) else r for r in reads]
        writes = [w.b if isinstance(w, T) else w for w in writes]
        self._waits(eng, self._deps(reads, writes, eng))
        k = self.engkey[eng]
        c = self.cnt.get(k, 0) + 1
        if c > SEM_ROT:
            self.engrot[eng] += 1
            k = self.engkey[eng] = eng + str(self.engrot[eng])
            c = 1
        self.cnt[k] = c
        tok = (k, c)
        self.streams[eng].append(("op", fn, k, 1))
        self._mark(tok, reads, writes)
        self.nops += 1
        return tok

    def dma(self, buf, items, reads=(), writes=(), is_out=False):
        reads = [r.b if isinstance(r, T) else r for r in reads]
        writes = [w.b if isinstance(w, T) else w for w in writes]
        if isinstance(buf, T):
            buf = buf.b
        if buf.dkey is None:
            if self.free_dkeys:
                buf.dkey = self.free_dkeys.pop()
            else:
                self.uid += 1
                buf.dkey = "d%d" % self.uid
            self.recent.append(buf)
        k = buf.dkey
        deps = self._deps(reads, writes)
        for q in dict.fromkeys(it[0] for it in items):
            self._waits(q, deps)
        c = self.cnt.get(k, 0)
        for it in items:
            q, o, i = it[0], it[1], it[2]
            c += 16
            self.streams[q].append(("op", (lambda e, o=o, i=i: e.dma_start(out=o, in_=i)), k, 16))
        assert c < 1000000, (k, c)
        self.cnt[k] = c
        tok = (k, c)
        self._mark(tok, reads, writes)
        if is_out:
            self.out_tokens.append(tok)
        return tok

    def barrier(self):
        deps = list(self.cnt.items())
        for e in self.ENGS:
            kn = self.known[e]
            for k, v in deps:
                if v > 0 and kn.get(k, 0) < v:
                    kn[k] = v
                    self.streams[e].append(("wait", k, v))
        for b in self.recent:
            if b.dkey is not None:
                self.free_dkeys.append(b.dkey)
                b.dkey = None
        self.recent = []

    def finish(self):
        nc = self.nc
        fin = {}
        for k, v in self.out_tokens:
            fin[k] = max(fin.get(k, 0), v)
        for k, v in fin.items():
            if self.known["sp"].get(k, 0) < v:
                self.streams["sp"].append(("wait", k, v))
        sems = {}
        for k in self.cnt:
            sems[k] = self.st.enter_context(nc.semaphore(k))
        block = self.st.enter_context(nc.Block())
        streams = self.streams

        def replay(e, lst):
            for item in lst:
                if item[0] == "wait":
                    e.wait_ge(sems[item[1]], item[2])
                else:
                    item[1](e).then_inc(sems[item[2]], item[3])

        @block.tensor
        def _(e):
            replay(e, streams["pe"])

        @block.scalar
        def _(e):
            replay(e, streams["act"])

        @block.vector
        def _(e):
            replay(e, streams["dve"])

        @block.gpsimd
        def _(e):
            replay(e, streams["pool"])

        @block.sync
        def _(e):
            replay(e, streams["sp"])

        self.st.close()


class Arena:
    def __init__(self, P, name, nbytes):
        self.t = P.sb(name, [128, nbytes // 2], BF16)
        self.cap = nbytes // 2
        self.off = 0
        self.n = 0
        self.hi = 0

    def reset(self):
        self.off = 0

    def get(self, shape, dt, name="t"):
        free = 1
        for s in shape[1:]:
            free *= s
        nb = free * (4 if dt == F32 else 2)
        ne = ((nb + 3) // 4) * 2
        assert self.off + ne <= self.cap, ("arena overflow", name, self.off, ne, self.cap)
        v = self.t[0:shape[0], self.off:self.off + nb // 2]
        self.off += ne
        self.hi = max(self.hi, self.off)
        if dt == F32:
            v = v.bitcast(F32)
        if len(shape) == 3:
            v = v.rearrange("p (a b) -> p a b", a=shape[1])
        elif len(shape) == 4:
            v = v.rearrange("p (a b c) -> p a b c", a=shape[1], b=shape[2])
        self.n += 1
        return T(v, "%s%d" % (name, self.n))


class Ring:
    def __init__(self, tiles):
        self.tiles = tiles
        self.i = 0

    def next(self):
        t = self.tiles[self.i % len(self.tiles)]
        self.i += 1
        return t


def run_pipe2(gens):
    prev = None
    for g_ in gens:
        next(g_)
        if prev is not None:
            for _ in prev:
                pass
        prev = g_
    if prev is not None:
        for _ in prev:
            pass


def bc(ap, axis, shape):
    return ap.unsqueeze(axis).broadcast_to(list(shape))


def host_consts():
    c = {}
    ii = np.arange(128)
    U = (ii[:, None] <= ii[None, :]).astype(np.float32)
    UT = (ii[:, None] >= ii[None, :]).astype(np.float32)
    nmf = np.where(ii[:, None] <= ii[None, :], 0.0, -30000.0).astype(np.float32)
    nmb = np.where(ii[:, None] >= ii[None, :], 0.0, -30000.0).astype(np.float32)
    ones = np.ones((128, 128), np.float32)
    ident = np.eye(128, dtype=np.float32)
    bd64 = np.zeros((128, 128), np.float32)
    bd64[:64, :64] = 1
    bd64[64:, 64:] = 1
    bd32 = np.zeros((128, 128), np.float32)
    for k in range(4):
        bd32[k * 32:(k + 1) * 32, k * 32:(k + 1) * 32] = 1
    d = np.arange(32)
    dd = d % 16
    j = dd % 8
    partner = np.where(dd < 8, d + 8, d - 8)
    freqs = (10000.0 ** (-(np.arange(0, 16, 2, dtype=np.float32)) / 16.0)).astype(np.float32)
    t = np.arange(2048)
    rows = (t // 64).astype(np.float32)
    cols = (t % 64).astype(np.float32)
    pos = np.where((d // 16)[:, None] == 0, rows[None, :], cols[None, :]).astype(np.float32)
    ang = (pos * freqs[j][:, None]).astype(np.float32)
    cos32 = np.cos(ang).astype(np.float32)
    sin32 = np.sin(ang).astype(np.float32)
    sins32 = np.where((dd < 8)[:, None], -sin32, sin32).astype(np.float32)
    p32 = np.zeros((32, 32), np.float32)
    p32[partner, d] = 1.0
    p96 = np.zeros((128, 128), np.float32)
    p96[64:96, 64:96] = p32
    p64 = np.zeros((128, 128), np.float32)
    p64[0:32, 0:32] = p32
    p64[32:64, 32:64] = p32
    cos96 = np.ones((128, 2048), np.float32)
    sin96 = np.zeros((128, 2048), np.float32)
    cos96[64:96] = cos32
    sin96[64:96] = sins32
    cos96[0:32] = cos32
    cos96[32:64] = cos32
    shift = np.zeros((128, 128), np.float32)
    for i in range(32):
        shift[i, 64 + i] = 1.0
    sel = np.zeros((16, 16, 128), np.float32)
    for h in range(16):
        sel[h, h, :] = 1.0
    c["cf32"] = np.stack([U, UT, nmf, nmb, ones, ident], axis=1).astype(np.float32)
    c["sel"] = sel.reshape(16, 2048)
    c["cbf"] = np.stack([ones, ident, bd64, bd32, p96, p64, shift], axis=1).astype(np.float32)
    sin64 = np.zeros((128, 2048), np.float32)
    sin64[0:32] = sins32
    sin64[32:64] = sins32
    c["rope"] = np.stack([cos96, sin96, sin64], axis=1).astype(np.float32)
    cosd = np.ones((128, 2048), np.float32)
    cosd[0:32] = cos32
    cosd[32:64] = cos32
    cosa = np.ones((128, 2048), np.float32)
    cosa[64:96] = cos32
    c["rope"] = np.stack([cosa, sin96, cosd, sin64], axis=1).astype(np.float32)
    return c


SM_PER = 44
BC_PER = 1184


def prep_core(inp, core, consts):
    f = np.float32
    b = core // 2
    d = {}
    xs = np.concatenate([inp["x_prompt"][2 * core], inp["x_prompt"][2 * core + 1], inp["x_sample"][b]], axis=0)
    d["xT_in"] = np.ascontiguousarray(xs.reshape(NT, 8, 128).transpose(2, 1, 0))
    cv = np.stack([inp["c_ctx"], inp["c"][b]], axis=-1)
    d["cvec"] = np.ascontiguousarray(cv.reshape(8, 128, 2).transpose(1, 0, 2))
    return d


def prep_shared(inp):
    f = np.float32
    d = {}
    d["w_ada"] = np.ascontiguousarray(inp["w_ada"].reshape(NL, 8, 128, 6144).transpose(0, 2, 1, 3))
    d["b_ada"] = np.ascontiguousarray(inp["b_ada"].reshape(NL, 48, 128).transpose(2, 0, 1))
    d["n1g"] = np.ascontiguousarray(inp["norm1_g"].reshape(NL, 8, 128).transpose(2, 0, 1))
    d["n2g"] = np.ascontiguousarray(inp["norm2_g"].reshape(NL, 8, 128).transpose(2, 0, 1))
    w_in = inp["w_in"]
    wfm = np.concatenate([w_in[:, :, 0:928], w_in[:, :, 1696:2464]], axis=2)
    wtm = np.concatenate([w_in[:, :, 928:1696], w_in[:, :, 2464:2480]], axis=2)
    d["w_fm"] = np.ascontiguousarray(wfm.reshape(NL, 8, 128, 1696).transpose(0, 2, 1, 3))
    d["w_tm"] = np.ascontiguousarray(wtm.reshape(NL, 8, 128, 784).transpose(0, 2, 1, 3))
    d["w_uq"] = np.ascontiguousarray(inp["w_uq"].reshape(NL, 2, 128, 384).transpose(0, 2, 1, 3))
    wukv = inp["w_ukv"].reshape(NL, 128, 4, 128)
    kn = wukv[:, :, :, 0:64]
    vv = wukv[:, :, :, 64:128]
    kn96 = np.concatenate([kn, np.zeros((NL, 128, 4, 32), f)], axis=3)
    d["w_ukv"] = np.ascontiguousarray(np.concatenate([kn96.reshape(NL, 128, 384), vv.reshape(NL, 128, 256)], axis=2))
    d["w_out"] = np.ascontiguousarray(inp["w_out"].reshape(NL, 8, 128, 1024).transpose(0, 2, 1, 3))
    d["w_ff1"] = np.ascontiguousarray(inp["w_ff1"].reshape(NL, 8, 128, 4096).transpose(0, 2, 1, 3))
    d["w_ff2"] = np.ascontiguousarray(inp["w_ff2"].reshape(NL, 32, 128, 1024).transpose(0, 2, 1, 3))
    sm = np.zeros((128, NL, SM_PER), f)
    for l in range(NL):
        sm[:, l, 0:2] = inp["mla_q_norm_g"][l].reshape(2, 128).T
        sm[:, l, 2] = inp["mla_kv_norm_g"][l]
        sm[0:96, l, 3] = inp["mla_qk_norm_q"][l]
        sm[0:96, l, 4] = inp["mla_qk_norm_k"][l]
        sm[0:64, l, 5] = np.tile(inp["diff_q_norm_g"][l], 2)
        sm[0:64, l, 6] = np.tile(inp["diff_k_norm_g"][l], 2)
        sm[:, l, 7] = np.tile(inp["diff_subln_g"][l], 2)
        sm[:, l, 8:38] = inp["ssm_conv_w"][l].reshape(5, 6, 128).transpose(2, 1, 0).reshape(128, 30)
        sm[:, l, 38:44] = inp["ssm_conv_b"][l].reshape(6, 128).T
    d["smallp"] = sm
    bcp = np.zeros((NL, BC_PER), f)
    for l in range(NL):
        bcp[l, 0:512] = np.repeat(inp["ssm_D"][l], 64)
        bcp[l, 512:1024] = inp["ssm_norm_g"][l]
        bcp[l, 1024:1040] = inp["ssm_dt_bias"][l].reshape(16)
        bcp[l, 1040:1056] = inp["ssm_A_log"][l].reshape(16)
        bcp[l, 1056:1088] = inp["diff_lq1"][l]
        bcp[l, 1088:1120] = inp["diff_lk1"][l]
        bcp[l, 1120:1152] = inp["diff_lq2"][l]
        bcp[l, 1152:1184] = inp["diff_lk2"][l]
    d["bcp"] = bcp
    return d


def prep_cache(inp, core):
    b = core // 2
    d = {}
    d["c_ckvT"] = np.ascontiguousarray(inp["cache_mla_ckv"][b].transpose(0, 2, 1))
    kr = np.zeros((NL, 128, 256), np.float32)
    kr[:, 0:32, :] = inp["cache_mla_krope"][b].transpose(0, 2, 1)
    d["c_krT"] = kr
    kd = np.zeros((NL, 128, 4, 256), np.float32)
    kd[:, 0:64] = inp["cache_diff_k"][b].transpose(0, 3, 2, 1)
    d["c_kdT"] = kd
    d["c_vd"] = np.ascontiguousarray(inp["cache_diff_v"][b].reshape(NL, 2, 128, 256).transpose(0, 2, 1, 3))
    st = inp["state_ssm"][b]
    st = st.reshape(NL, 2, 2, 4, 64, 64)
    st = st.transpose(0, 1, 2, 5, 3, 4)
    d["st0"] = np.ascontiguousarray(st.reshape(NL, 2, 128, 256))
    return d


def build(depth=NL, debug=()):
    nc = bass.Bass("TRN2", target_bir_lowering=False)

    def din(name, shape):
        return nc.dram_tensor(name, list(shape), F32, kind="ExternalInput").ap()

    def dout(name, shape):
        return nc.dram_tensor(name, list(shape), F32, kind="ExternalOutput").ap()

    xT_in = din("xT_in", [128, 8, NT])
    cvec = din("cvec", [128, 8, 2])
    w_ada = din("w_ada", [NL, 128, 8, 6144])
    b_ada = din("b_ada", [128, NL, 48])
    n1g = din("n1g", [128, NL, 8])
    n2g = din("n2g", [128, NL, 8])
    w_fm = din("w_fm", [NL, 128, 8, 1696])
    w_tm = din("w_tm", [NL, 128, 8, 784])
    w_uq = din("w_uq", [NL, 128, 2, 384])
    w_ukv = din("w_ukv", [NL, 128, 640])
    w_out = din("w_out", [NL, 128, 8, 1024])
    w_ff1 = din("w_ff1", [NL, 128, 8, 4096])
    w_ff2 = din("w_ff2", [NL, 128, 32, 1024])
    smallp = din("smallp", [128, NL, SM_PER])
    bcp = din("bcp", [NL, BC_PER])
    cf32_d = din("cf32", [128, 6, 128])
    sel_d = din("sel", [16, 2048])
    cbf_d = din("cbf", [128, 7, 128])
    rope_d = din("rope", [128, 4, 2048])
    c_ckvT = din("c_ckvT", [NL, 128, 256])
    c_krT = din("c_krT", [NL, 128, 256])
    c_kdT = din("c_kdT", [NL, 128, 4, 256])
    c_vd = din("c_vd", [NL, 128, 2, 256])
    st0 = din("st0", [NL, 2, 128, 256])

    xT_out = dout("xT_out", [128, 8, NT])
    ckv_o = dout("ckv_o", [NL, 128, 512])
    kr_o = dout("kr_o", [NL, 32, 512])
    kd_o = dout("kd_o", [NL, 64, 4, 512])
    vd_o = dout("vd_o", [NL, 128, 4, 256])
    st_o = dout("st_o", [NL, 2, 2, 128, 256])

    P = Prog(nc)
    dbg_outs = {}

    cf32 = T(P.sb("cf32s", [128, 6, 128], F32)[:], "cf32")
    selS = T(P.sb("selS", [16, 2048], F32)[:], "sel")
    cbf = T(P.sb("cbfs", [128, 7, 128], BF16)[:], "cbf")
    smp = T(P.sb("smp", [128, NL, SM_PER], F32)[:], "smp")
    modT = T(P.sb("modT", [128, NL, 6, 8, 2], F32)[:], "modT")
    gsT = T(P.sb("gsT", [128, NL, 2, 8, 2], F32)[:], "gsT")
    bcl = T(P.sb("bcl", [128, BC_PER], F32)[:], "bcl")
    lay = T(P.sb("lay", [128, 64], F32)[:], "lay")
    U_f = cf32[:, 0, :]
    UT_f = cf32[:, 1, :]
    NMF = cf32[:, 2, :]
    NMB = cf32[:, 3, :]
    ONES_f = cf32[:, 4, :]
    ID_f = cf32[:, 5, :]
    ONES_b = cbf[:, 0, :]
    ID_b = cbf[:, 1, :]
    BD64 = cbf[:, 2, :]
    BD32 = cbf[:, 3, :]
    P96 = cbf[:, 4, :]
    P64 = cbf[:, 5, :]
    SHIFT = cbf[:, 6, :]

    banks = [T(P.ps("bank%d" % i, [128, 512], F32)[:], "bank%d" % i) for i in range(8)]
    rot = Ring(banks[0:4])
    rot8 = Ring(banks[0:8])
    acc = banks[4:8]
    cur_rot = [rot8]

    def nb():
        return cur_rot[0].next()

    def use_rot(n):
        cur_rot[0] = rot8 if n == 8 else rot

    arena = Arena(P, "arena", 184 * 1024)
    mixD = nc.dram_tensor("mixD", [128, 8, NT], BF16, kind="Internal").ap()
    mixbufs = [Buf("mixD%d" % i) for i in range(5)]

    xbufs = [Buf("xres%d" % i) for i in range(5)]

    def mm(out, lhsT, rhs, start, stop, reads, writes):
        P.op("pe", lambda e: e.matmul(out, lhsT=lhsT, rhs=rhs, start=start, stop=stop), reads, writes)

    def tr(out, in_, ident, reads, writes):
        P.op("pe", lambda e: e.transpose(out, in_, ident), reads, writes)

    def act(out, in_, func, reads, writes, bias=None, scale=None, accum=None):
        kw = {}
        if bias is not None:
            kw["bias"] = bias
        if scale is not None:
            kw["scale"] = scale
        if accum is not None:
            kw["accum_out"] = accum
        P.op("act", lambda e: e.activation(out=out, in_=in_, func=func, **kw), reads, writes)

    def tt(eng, out, in0, in1, op, reads, writes):
        P.op(eng, lambda e: e.tensor_tensor(out=out, in0=in0, in1=in1, op=op), reads, writes)

    def ts(eng, out, in0, s1, s2, op0, op1, reads, writes):
        if s2 is None:
            P.op(eng, lambda e: e.tensor_scalar(out=out, in0=in0, scalar1=s1, scalar2=None, op0=op0), reads, writes)
        else:
            P.op(eng, lambda e: e.tensor_scalar(out=out, in0=in0, scalar1=s1, scalar2=s2, op0=op0, op1=op1), reads, writes)

    def stt(out, in0, scalar, in1, op0, op1, reads, writes):
        P.op("dve", lambda e: e.scalar_tensor_tensor(out=out, in0=in0, scalar=scalar, in1=in1, op0=op0, op1=op1), reads, writes)

    def cp(eng, out, in_, reads, writes):
        if eng == "act":
            P.op("act", lambda e: e.copy(out=out, in_=in_), reads, writes)
        else:
            P.op(eng, lambda e: e.tensor_copy(out=out, in_=in_), reads, writes)

    def memset(eng, ap, val, writes):
        P.op(eng, lambda e: e.memset(ap, val), (), writes)

    def rstd_act(out, in_, D, reads, writes):
        act(out, in_, AF.Ln, reads, writes, scale=1.0 / D, bias=epsT[0:out.shape[0], 0:1])
        act(out, out, AF.Exp, writes, writes, scale=-0.5)

    def dbg(name, t, ap=None):
        if name not in debug:
            return
        ap = t.ap if ap is None else ap
        shp = list(ap.shape)
        o = nc.dram_tensor("dbg_" + name, shp, ap.dtype, kind="ExternalOutput").ap()
        dbg_outs[name] = shp
        tmpb = Buf("dbg_" + name)
        P.dma(tmpb, [("sp", o, ap)], reads=[t], is_out=True)

    epsT_t = T(P.sb("epsT", [128, 4], F32)[:], "epsT")
    epsT = epsT_t.ap
    memset("dve", epsT[:, 0:1], EPS, [epsT_t])
    memset("dve", epsT[:, 1:2], 1.0, [epsT_t])
    P.dma(cf32, [("sp", cf32.ap, cf32_d)], writes=[cf32])
    P.dma(selS, [("sp", selS.ap, sel_d)], writes=[selS])
    P.dma(cbf, [("pool", cbf.ap, cbf_d)], writes=[cbf])
    P.dma(smp, [("sp", smp.ap, smallp)], writes=[smp])

    arena.reset()
    cvs = arena.get([128, 8, 2], F32, "cvs")
    csb = T(P.sb("csb", [128, 8, 2], BF16)[:], "csb")
    badaS = T(P.sb("badaS", [128, NL, 48], F32)[:], "bada")
    ngS = T(P.sb("ngS", [128, 2, NL, 8], F32)[:], "ngS")
    P.dma(cvs, [("sp", cvs.ap, cvec)], writes=[cvs])
    P.dma(badaS, [("sp", badaS.ap, b_ada)], writes=[badaS])
    P.dma(ngS, [("sp", ngS[:, 0], n1g), ("sp", ngS[:, 1], n2g)], writes=[ngS])
    act(csb.ap, cvs.ap, AF.Silu, [cvs], [csb])

    def mod_group(lm, ng, wring_):
        w = wring_.next()
        P.dma(w, [("pool", w.ap, w_ada[lm][:, :, ng * 512:(ng + 1) * 512])], writes=[w])
        bk = nb()
        for j in range(4):
            for kc in range(8):
                mm(bk[:, 2 * j:2 * j + 2], w[:, kc, j * 128:(j + 1) * 128], csb[:, kc, :], kc == 0, kc == 7, [w, csb], [bk])
        m6, c0 = divmod(ng * 4, 8)
        tt("dve", modT[:, lm, m6, c0:c0 + 4, :], bk[:, 0:8].rearrange("p (j g) -> p j g", g=2),
           bc(badaS[:, lm, ng * 4:ng * 4 + 4], 2, [128, 4, 2]), ALU.add, [bk, badaS], [modT])

    def mod_finish(lm):
        for which, mi in ((0, 1), (1, 4)):
            ts("dve", gsT[:, lm, which], modT[:, lm, mi], 1.0, None, ALU.add, None, [modT], [gsT])
            tt("dve", gsT[:, lm, which], gsT[:, lm, which], bc(ngS[:, which, lm, :], 2, [128, 8, 2]), ALU.mult, [gsT, ngS], [gsT])

    wring = Ring([arena.get([128, 8, 512], BF16, "wada") for _ in range(3)])
    try:
        ckpt(0)
        for ng in range(12):
            mod_group(0, ng, wring)
        mod_finish(0)
        dbg("modT", modT)
        P.barrier()

        def norm_mod(xt, hdst, hdst_t, l, which, g, sq, rs_ring, tmp_ring):
            act(sq.ap, xt.ap, AF.Square, [xt], [sq])
            bk = nb()
            for c in range(8):
                mm(bk.ap, ONES_b, sq[:, c, :], c == 0, c == 7, [sq, cbf], [bk])
            rs = rs_ring.next()
            rstd_act(rs.ap, bk.ap, 1024.0, [bk], [rs])
            sh = 0 if which == 0 else 3
            for c in range(8):
                tm = tmp_ring.next()
                tt("dve", tm.ap, xt[:, c, :], rs.ap, ALU.mult, [xt, rs], [tm])
                act(hdst[:, c, :], tm.ap, AF.Identity, [tm, gsT, modT], [hdst_t],
                    scale=gsT[:, l, which, c, g:g + 1], bias=modT[:, l, sh, c, g:g + 1])

        for l in range(depth):
            xsrc = xT_in if l == 0 else xT_out
            lam_init = 0.8 - 0.6 * math.exp(-0.3 * l)
            P.barrier()
            arena.reset()
            P.dma(bcl, [("sp", bcl.ap, bcp[l:l + 1, :].partition_broadcast(128))], writes=[bcl])
            Dbc = bcl[:, 0:512]
            NGbc = bcl[:, 512:1024]
            dtb = bcl[:, 1024:1040]
            act(lay[:, 0:16], bcl[:, 1040:1056], AF.Exp, [bcl], [lay])
            ts("dve", lay[:, 0:16], lay[:, 0:16], -1.0, None, ALU.mult, None, [lay], [lay])
            aneg = lay[:, 0:16]
            tt("dve", lay[:, 32:64], bcl[:, 1056:1088], bcl[:, 1088:1120], ALU.mult, [bcl], [lay])
            P.op("dve", lambda e: e.reduce_sum(out=lay[:, 16:17], in_=lay[:, 32:64], axis=AX.X), [lay], [lay])
            tt("dve", lay[:, 32:64], bcl[:, 1120:1152], bcl[:, 1152:1184], ALU.mult, [bcl], [lay])
            P.op("dve", lambda e: e.reduce_sum(out=lay[:, 17:18], in_=lay[:, 32:64], axis=AX.X), [lay], [lay])
            act(lay[:, 16:18], lay[:, 16:18], AF.Exp, [lay], [lay])
            tt("dve", lay[:, 18:19], lay[:, 17:18], lay[:, 16:17], ALU.subtract, [lay], [lay])
            ts("dve", lay[:, 18:19], lay[:, 18:19], -lam_init, None, ALU.add, None, [lay], [lay])
            nlam = lay[:, 18:19]
            ts("dve", lay[:, 19:20], smp[:, l, 7:8], 1.0 - lam_init, None, ALU.mult, None, [smp], [lay])
            subg = lay[:, 19:20]

            wuq = arena.get([128, 2, 384], BF16, "wuq")
            wukv = arena.get([128, 640], BF16, "wukv")
            wtm = arena.get([128, 8, 784], BF16, "wtm")
            P.dma(wuq, [("pool", wuq.ap, w_uq[l])], writes=[wuq])
            P.dma(wukv, [("pool", wukv.ap, w_ukv[l])], writes=[wukv])
            P.dma(wtm, [("pool", wtm.ap, w_tm[l])], writes=[wtm])
            base_off = arena.off

            for (t0, n, nseq, L, ctx, g) in ((0, 512, 2, 256, False, 0), (512, 2048, 1, 2048, True, 1)):
                P.barrier()
                arena.off = base_off
                ntile = n // 512
                nch = n // 128
                nchs = L // 128
                hT = arena.get([128, 8, n], BF16, "hT")
                offA = arena.off
                xring = Ring([arena.get([128, 8, 512], F32, "xt") for _ in range(2)])
                sqA = arena.get([128, 8, 512], BF16, "sq")
                rsr = Ring([arena.get([128, 512], F32, "rs") for _ in range(2)])
                tmr = Ring([arena.get([128, 512], F32, "tm") for _ in range(2)])
                for tt_ in range(ntile):
                    ta = t0 + tt_ * 512
                    xt = xring.next()
                    xb = xbufs[ta // 512]
                    P.dma(xt, [("sp", xt.ap, xsrc[:, :, ta:ta + 512])], reads=[xb], writes=[xt])
                    norm_mod(xt, hT[:, :, tt_ * 512:(tt_ + 1) * 512], hT, l, 0, g, sqA, rsr, tmr)
                if l == 0 and t0 == 0:
                    dbg("hT", hT)

                ckpt(1)
                P.barrier()
                arena.off = offA
                xtok = arena.get([128, nch, 512], BF16, "xtok")
                CT = arena.get([128, n], BF16, "CT")
                dtall = arena.get([128, nch, 16], F32, "dtall")
                acum = arena.get([128, nch, 16], F32, "acum")
                hpf = arena.get([128, nch, 512], BF16, "hpf")
                hpb = arena.get([128, nch, 512], BF16, "hpb")
                BTz = arena.get([128, 2, n], BF16, "BTz")
                memset("pool", hpf.ap, 0.0, [hpf])
                memset("pool", hpb.ap, 0.0, [hpb])
                memset("pool", BTz.ap, 0.0, [BTz])
                offB2 = arena.off
                Btok = arena.get([128, nch, 128], BF16, "Btok")
                BT = arena.get([128, n], BF16, "BT")
                aall = arena.get([128, nch, 16], F32, "aall")
                cdall = arena.get([128, nch, 16], F32, "cdall")
                SnB = arena.get([128, nch, 256], BF16, "SnB")
                stF = arena.get([128, 256], F32, "stF")
                stB = arena.get([128, 256], F32, "stB")
                offB = arena.off
                wxr = Ring([arena.get([128, 8, 128], BF16, "wx") for _ in range(2)])
                prer = Ring([arena.get([128, nseq, L + 4], BF16, "pre") for _ in range(2)])
                accr = Ring([arena.get([128, nseq, L], F32, "cacc") for _ in range(2)])
                xcr = Ring([arena.get([128, n], BF16, "xc") for _ in range(2)])
                def conv_item(c6):
                    wx = wxr.next()
                    P.dma(wx, [("pool", wx.ap, w_fm[l][:, :, 928 + c6 * 128:928 + (c6 + 1) * 128])], writes=[wx])
                    pre = prer.next()
                    memset("pool", pre[:, :, 0:2], 0.0, [pre])
                    memset("pool", pre[:, :, L + 2:L + 4], 0.0, [pre])
                    for tt_ in range(ntile):
                        bk = nb()
                        for kc in range(8):
                            mm(bk.ap, wx[:, kc, :], hT[:, kc, tt_ * 512:(tt_ + 1) * 512], kc == 0, kc == 7, [wx, hT], [bk])
                        if L >= 512:
                            s_, o_ = divmod(tt_ * 512, L)
                            cp("act", pre[:, s_, 2 + o_:2 + o_ + 512], bk.ap, [bk], [pre])
                        else:
                            k_ = 512 // L
                            cp("act", pre[:, tt_ * k_:(tt_ + 1) * k_, 2:2 + L], bk.ap.rearrange("p (s t) -> p s t", s=k_), [bk], [pre])
                    yield
                    ca = accr.next()
                    ts("dve", ca.ap, pre[:, :, 0:L], smp[:, l, 8 + c6 * 5:9 + c6 * 5], None, ALU.mult, None, [pre, smp], [ca])
                    for k in range(1, 5):
                        stt(ca.ap, pre[:, :, k:k + L], smp[:, l, 8 + c6 * 5 + k:9 + c6 * 5 + k], ca.ap, ALU.mult, ALU.add, [pre, smp, ca], [ca])
                    caf = ca.ap.rearrange("p s t -> p (s t)")
                    if c6 < 4:
                        xc = xcr.next()
                        dstT = xc
                    elif c6 == 4:
                        dstT = BT
                    else:
                        dstT = CT
                    act(dstT.ap, caf, AF.Silu, [ca, smp], [dstT], bias=smp[:, l, 38 + c6:39 + c6])
                    if c6 <= 4:
                        for ch0 in range(0, nch, 4):
                            bk = nb()
                            bkb = bk.ap.bitcast(BF16)
                            for q in range(4):
                                tr(bkb[:, q * 128:(q + 1) * 128], dstT[:, (ch0 + q) * 128:(ch0 + q + 1) * 128], ID_b, [dstT, cbf], [bk])
                            src = bkb[:, 0:512].rearrange("p (q f) -> p q f", q=4)
                            if c6 < 4:
                                cp("dve", xtok[:, ch0:ch0 + 4, c6 * 128:(c6 + 1) * 128], src, [bk], [xtok])
                            else:
                                cp("dve", Btok[:, ch0:ch0 + 4, :], src, [bk], [Btok])
                run_pipe2([conv_item(c6_) for c6_ in range(6)])
                if l == 0 and t0 == 0:
                    dbg("xtok", xtok)
                    dbg("CT", CT)
                ckpt(2)
                P.barrier()
                arena.off = offB
                cp("act", BTz[0:64, 0, :], BT[0:64, :], [BT], [BTz])
                cp("act", BTz[64:128, 1, :], BT[64:128, :], [BT], [BTz])
                xddr = Ring([arena.get([128, 2, 512], BF16, "xdd") for _ in range(2)])
                tmpD = arena.get([128, nch, 16], F32, "tmpD")
                wdd = arena.get([128, nch, 16], F32, "wdd")
                bkd = nb()
                for ch in range(nch):
                    for kc in range(8):
                        mm(bkd[:, ch * 16:(ch + 1) * 16], hT[:, kc, ch * 128:(ch + 1) * 128], wtm[:, kc, 768:784], kc == 0, kc == 7, [hT, wtm], [bkd])
                bkd3 = bkd[:, 0:nch * 16].rearrange("p (c k) -> p c k", k=16)
                tt("dve", tmpD.ap, bkd3, bc(dtb, 1, [128, nch, 16]), ALU.add, [bkd, bcl], [tmpD])
                act(tmpD.ap, tmpD.ap, AF.Exp, [tmpD], [tmpD])
                act(dtall.ap, tmpD.ap, AF.Ln, [tmpD], [dtall], bias=epsT[:, 1:2])
                tt("dve", aall.ap, dtall.ap, bc(aneg, 1, [128, nch, 16]), ALU.mult, [dtall, lay], [aall])
                bkc2 = nb()
                for ch in range(nch):
                    mm(bkc2[:, ch * 32:ch * 32 + 8], U_f, aall[:, ch, 0:8], True, True, [cf32, aall], [bkc2])
                    mm(bkc2[:, ch * 32 + 8:ch * 32 + 16], UT_f, aall[:, ch, 8:16], True, True, [cf32, aall], [bkc2])
                    mm(bkc2[:, ch * 32 + 16:ch * 32 + 32], ONES_f, aall[:, ch, :], True, True, [cf32, aall], [bkc2])
                bkc3 = bkc2[:, 0:nch * 32].rearrange("p (c k) -> p c k", k=32)
                cp("dve", acum.ap, bkc3[:, :, 0:16], [bkc2], [acum])
                tt("dve", wdd.ap, bkc3[:, :, 16:32], acum.ap, ALU.subtract, [bkc2, acum], [wdd])
                act(wdd.ap, wdd.ap, AF.Exp, [wdd], [wdd])
                act(cdall.ap, bkc3[:, :, 16:32], AF.Exp, [bkc2], [cdall])
                tt("dve", wdd.ap, wdd.ap, dtall.ap, ALU.mult, [wdd, dtall], [wdd])

                def passA_item(ch):
                    s_, cl = divmod(ch, nchs)
                    xdd = xddr.next()
                    for d_ in range(2):
                        tt("dve", xdd[:, d_, :].rearrange("p (h q) -> p h q", h=8), xtok[:, ch, :].rearrange("p (h q) -> p h q", h=8),
                           bc(wdd[:, ch, d_ * 8:(d_ + 1) * 8], 2, [128, 8, 64]), ALU.mult, [xtok, wdd], [xdd])
                    bk3 = nb()
                    mm(bk3.ap, Btok[:, ch, :], xdd[:, 0, :], True, True, [Btok, xdd], [bk3])
                    bk4 = nb()
                    mm(bk4.ap, Btok[:, ch, :], xdd[:, 1, :], True, True, [Btok, xdd], [bk4])
                    yield
                    if cl == 0:
                        if ctx:
                            P.dma(stF, [("sp", stF.ap, st0[l, 0])], writes=[stF])
                        else:
                            memset("dve", stF.ap, 0.0, [stF])
                    cp("act", hpf[0:64, ch, 0:256], stF[0:64, :], [stF], [hpf])
                    cp("act", hpf[64:128, ch, 256:512], stF[64:128, :], [stF], [hpf])
                    for g2 in range(2):
                        rws = slice(g2 * 64, (g2 + 1) * 64)
                        v = stF[rws, :].rearrange("p (h q) -> p h q", h=4)
                        tt("dve", v, v, bc(cdall[rws, ch, g2 * 4:(g2 + 1) * 4], 2, [64, 4, 64]), ALU.mult, [stF, cdall], [stF])
                        tt("dve", stF[rws, :], stF[rws, :], bk3[rws, g2 * 256:(g2 + 1) * 256], ALU.add, [stF, bk3], [stF])
                        cp("act", SnB[rws, ch, :], bk4[rws, g2 * 256:(g2 + 1) * 256], [bk4], [SnB])
                    if cl == nchs - 1 and not ctx:
                        P.dma(stF, [("sp", st_o[l, s_, 0], stF.ap)], reads=[stF], is_out=True)
                run_pipe2([passA_item(ch_) for ch_ in range(nch)])
                ckpt(3)
                for s_ in range(nseq):
                    for cl in range(nchs - 1, -1, -1):
                        ch = s_ * nchs + cl
                        if cl == nchs - 1:
                            if ctx:
                                P.dma(stB, [("sp", stB.ap, st0[l, 1])], writes=[stB])
                            else:
                                memset("dve", stB.ap, 0.0, [stB])
                        cp("act", hpb[0:64, ch, 0:256], stB[0:64, :], [stB], [hpb])
                        cp("act", hpb[64:128, ch, 256:512], stB[64:128, :], [stB], [hpb])
                        for g2 in range(2):
                            rws = slice(g2 * 64, (g2 + 1) * 64)
                            v = stB[rws, :].rearrange("p (h q) -> p h q", h=4)
                            tt("dve", v, v, bc(cdall[rws, ch, 8 + g2 * 4:8 + (g2 + 1) * 4], 2, [64, 4, 64]), ALU.mult, [stB, cdall], [stB])
                            tt("dve", stB[rws, :], stB[rws, :], SnB[rws, ch, :], ALU.add, [stB, SnB], [stB])
                        if cl == 0 and not ctx:
                            P.dma(stB, [("sp", st_o[l, s_, 1], stB.ap)], reads=[stB], is_out=True)
                ckpt(4)
                use_rot(4)
                P.barrier()
                arena.off = offB2
                szr = Ring([arena.get([128, 512], F32, "sz") for _ in range(2)])
                acTr = Ring([arena.get([16, 128], F32, "acT") for _ in range(2)])
                LTp = arena.get([128, 16, 128], F32, "LTp")
                LTr = Ring([arena.get([128, 16, 128], BF16, "LT") for _ in range(2)])
                Ear = Ring([arena.get([128, 16, 128], BF16, "Ea") for _ in range(2)])
                scr = Ring([arena.get([128, 2, 8, 128], BF16, "sc") for _ in range(2)])
                Csr = Ring([arena.get([128, 16, 128], BF16, "Cs") for _ in range(2)])
                xdtr = Ring([arena.get([128, 2, 512], BF16, "xdt") for _ in range(2)])
                y1r = Ring([arena.get([128, 512], F32, "y1") for _ in range(2)])
                y2r = Ring([arena.get([128, 512], F32, "y2") for _ in range(2)])
                ybr = Ring([arena.get([128, 512], BF16, "yb") for _ in range(2)])
                junk = arena.get([128, 512], BF16, "junk")
                s1r = Ring([arena.get([128, 2], F32, "s1") for _ in range(4)])
                ocr = Ring([arena.get([128, 4, 128], BF16, "ocs") for _ in range(2)])
                def ssd_s1(ch):
                    bkz = nb()
                    for kc in range(8):
                        mm(bkz.ap, hT[:, kc, ch * 128:(ch + 1) * 128], wtm[:, kc, 256:768], kc == 0, kc == 7, [hT, wtm], [bkz])
                    sz = szr.next()
                    act(sz.ap, bkz.ap, AF.Silu, [bkz], [sz])
                    bkt = nb()
                    tr(bkt[0:16, 0:128], acum[:, ch, :], ID_f, [acum, cf32], [bkt])
                    acT = acTr.next()
                    cp("dve", acT.ap, bkt[0:16, 0:128], [bkt], [acT])
                    for hd in range(16):
                        b_ = acc[hd // 4]
                        mm(b_[:, (hd % 4) * 128:(hd % 4 + 1) * 128], selS[0:16, hd * 128:(hd + 1) * 128], acT.ap, True, True, [selS, acT], [b_])
                    ckpt(4.1)
                    for hd in range(16):
                        b_ = acc[hd // 4]
                        stt(LTp[:, hd, :], b_[:, (hd % 4) * 128:(hd % 4 + 1) * 128], acum[:, ch, hd:hd + 1], NMF if hd < 8 else NMB,
                            ALU.subtract, ALU.add, [b_, acum, cf32], [LTp])
                    LT = LTr.next()
                    act(LT.ap, LTp.ap, AF.Exp, [LTp], [LT])
                    Ea = Ear.next()
                    for q in range(4):
                        act(Ea[:, q * 4:(q + 1) * 4, :], acc[q].ap.rearrange("p (a b) -> p a b", a=4), AF.Exp, [acc[q]], [Ea])
                    ckpt(4.2)
                    bkc = nb()
                    for g2 in range(2):
                        mm(bkc[:, g2 * 128:(g2 + 1) * 128], BTz[:, g2, ch * 128:(ch + 1) * 128], CT[:, ch * 128:(ch + 1) * 128], True, True, [BTz, CT], [bkc])
                    sc = scr.next()
                    for d_ in range(2):
                        for g2 in range(2):
                            tt("dve", sc[:, d_, g2 * 4:(g2 + 1) * 4, :], bc(bkc[:, g2 * 128:(g2 + 1) * 128], 1, [128, 4, 128]),
                               LT[:, d_ * 8 + g2 * 4:d_ * 8 + (g2 + 1) * 4, :], ALU.mult, [bkc, LT], [sc])
                    Cs = Csr.next()
                    tt("dve", Cs.ap, bc(CT[:, ch * 128:(ch + 1) * 128], 1, [128, 16, 128]), Ea.ap, ALU.mult, [CT, Ea], [Cs])
                    ckpt(4.3)
                    xdt = xdtr.next()
                    for d_ in range(2):
                        tt("dve", xdt[:, d_, :].rearrange("p (h q) -> p h q", h=8), xtok[:, ch, :].rearrange("p (h q) -> p h q", h=8),
                           bc(dtall[:, ch, d_ * 8:(d_ + 1) * 8], 2, [128, 8, 64]), ALU.mult, [xtok, dtall], [xdt])
                    return sz, sc, Cs, xdt

                def ssd_s2(ch, carry):
                    sz, sc, Cs, xdt = carry
                    bky = nb()
                    for h in range(8):
                        g2, hl = divmod(h, 4)
                        rws = slice(g2 * 64, (g2 + 1) * 64)
                        o_ = bky[:, h * 64:(h + 1) * 64]
                        mm(o_, sc[:, 0, h, :], xdt[:, 0, h * 64:(h + 1) * 64], True, False, [sc, xdt], [bky])
                        mm(o_, sc[:, 1, h, :], xdt[:, 1, h * 64:(h + 1) * 64], False, False, [sc, xdt], [bky])
                        mm(o_, Cs[:, h, :], hpf[:, ch, h * 64:(h + 1) * 64], False, False, [Cs, hpf], [bky])
                        mm(o_, Cs[:, 8 + h, :], hpb[:, ch, h * 64:(h + 1) * 64], False, True, [Cs, hpb], [bky])
                    ckpt(4.4)
                    y1 = y1r.next()
                    tt("pool", y1.ap, xtok[:, ch, :], Dbc, ALU.mult, [xtok, bcl], [y1])
                    tt("dve", y1.ap, bky.ap, y1.ap, ALU.add, [bky, y1], [y1])
                    y2 = y2r.next()
                    tt("dve", y2.ap, y1.ap, sz.ap, ALU.mult, [y1, sz], [y2])
                    ckpt(4.5)
                    s1 = s1r.next()
                    act(junk.ap, y2.ap, AF.Square, [y2], [junk, s1], accum=s1[:, 0:1])
                    rstd_act(s1[:, 1:2], s1[:, 0:1], 512.0, [s1], [s1])
                    yb = ybr.next()
                    stt(yb.ap, y2.ap, s1[:, 1:2], NGbc, ALU.mult, ALU.mult, [y2, s1, bcl], [yb])
                    bk = nb()
                    bkb = bk.ap.bitcast(BF16)
                    for q in range(4):
                        tr(bkb[:, q * 128:(q + 1) * 128], yb[:, q * 128:(q + 1) * 128], ID_b, [yb, cbf], [bk])
                    ckpt(4.6)
                    ocs = ocr.next()
                    cp("act", ocs.ap, bkb[:, 0:512].rearrange("p (q f) -> p q f", q=4), [bk], [ocs])
                    ta_ = t0 + ch * 128
                    P.dma(ocs, [("sp", mixD[:, 4:8, ta_:ta_ + 128], ocs.ap)], reads=[ocs], writes=[mixbufs[ta_ // 512]])

                carry_ = {}
                for ch in range(nch + 1):
                    if ch < nch:
                        carry_[ch] = ssd_s1(ch)
                    if ch >= 1:
                        ssd_s2(ch - 1, carry_.pop(ch - 1))
                ckpt(5)
                use_rot(8)
                P.barrier()
                arena.off = offA
                nk = L + (256 if ctx else 0)
                nkt = nk // 128
                koff = 256 if ctx else 0
                wat = arena.get([128, 8, 928], BF16, "wat")
                P.dma(wat, [("pool", wat.ap, w_fm[l][:, :, 0:928])], writes=[wat])
                rope = arena.get([128, 4, 2048], F32, "rope") if False else None
                ropeA = arena.get([128, 2, 2048], F32, "ropeA")
                if ctx:
                    P.dma(ropeA, [("sp", ropeA.ap, rope_d[:, 0:2, :])], writes=[ropeA])
                rope = ropeA
                rbase = 0
                qaT = arena.get([128, 4, n], BF16, "qaT")
                kaT = arena.get([128, 4, nseq * nk], BF16, "kaT")
                vaA = arena.get([128, nseq * nkt, 4, 128], BF16, "vaA")
                cqn = arena.get([128, 2, 512], BF16, "cqn")
                ckvn = arena.get([128, nseq * nk], BF16, "ckvn")
                krb = arena.get([128, nseq * nk], BF16, "krb")
                offC = arena.off
                memset("pool", vaA.ap, 1.0, [vaA])
                ckvf_r = Ring([arena.get([128, 512], F32, "ckvf") for _ in range(2)])
                krf_r = Ring([arena.get([128, 512], F32, "krf") for _ in range(2)])
                sqr = Ring([arena.get([128, 2, 512], BF16, "sqc") for _ in range(2)])
                rsr = Ring([arena.get([128, 512], F32, "rs") for _ in range(2)])
                cqf_r = Ring([arena.get([128, 2, 512], F32, "cqf") for _ in range(1)])
                qn_r = Ring([arena.get([128, 512], F32, "qn") for _ in range(2)])
                qnb_r = Ring([arena.get([128, 512], BF16, "qnb") for _ in range(2)])
                t1_r = Ring([arena.get([128, 512], F32, "t1") for _ in range(2)])
                t2_r = Ring([arena.get([128, 512], F32, "t2") for _ in range(2)])
                cpyr = [Ring([arena.get([128, 512], F32, "cpy") for _ in range(2)])]

                ckpt(5.1)
                def keycol(s_, tl, w):
                    return slice(s_ * nk + koff + tl, s_ * nk + koff + tl + w)

                if ctx:
                    cst = ckvf_r.next()
                    P.dma(cst, [("sp", cst[:, 0:256], c_ckvT[l])], writes=[cst])
                    cp("act", ckvn[:, 0:256], cst[:, 0:256], [cst], [ckvn])
                    kst = krf_r.next()
                    P.dma(kst, [("sp", kst[:, 0:256], c_krT[l])], writes=[kst])
                    cp("act", krb[0:32, 0:256], kst[0:32, 0:256], [kst], [krb])

                def run_pipe(gens):
                    prev = None
                    for g_ in gens:
                        next(g_)
                        if prev is not None:
                            for _ in prev:
                                pass
                        prev = g_
                    if prev is not None:
                        for _ in prev:
                            pass

                def norm_item(emit_src, rows, lhsT_ones, D, gcol, ncol, plain, roped, P_l=None, post=None):
                    bk = emit_src()
                    sq = sqr.next()
                    cpy = cpyr[0].next()
                    cp("dve", cpy[0:rows, 0:ncol], bk[0:rows, 0:ncol], [bk], [cpy])
                    act(sq[0:rows, 0, 0:ncol], cpy[0:rows, 0:ncol], AF.Square, [cpy], [sq])
                    bk2 = nb()
                    mm(bk2[0:rows, 0:ncol], lhsT_ones, sq[0:rows, 0, 0:ncol], True, True, [sq, cbf], [bk2])
                    yield
                    rs = rsr.next()
                    rstd_act(rs[0:rows, 0:ncol], bk2[0:rows, 0:ncol], D, [bk2], [rs])
                    if not roped:
                        for (c0_, w_, dst_ap, dst_t) in plain:
                            stt(dst_ap, cpy[0:rows, c0_:c0_ + w_], gcol, rs[0:rows, c0_:c0_ + w_], ALU.mult, ALU.mult, [cpy, rs, smp], [dst_t])
                    else:
                        qn = qnb_r.next()
                        stt(qn[0:rows, 0:ncol], cpy[0:rows, 0:ncol], gcol, rs[0:rows, 0:ncol], ALU.mult, ALU.mult, [cpy, rs, smp], [qn])
                        for (c0_, w_, dst_ap, dst_t) in plain:
                            cp("act", dst_ap, qn[0:rows, c0_:c0_ + w_], [qn], [dst_t])
                        for (c0_, w_, p0_, dst_ap, dst_t) in roped:
                            bk3 = nb()
                            mm(bk3[0:rows, 0:w_], P_l, qn[0:rows, c0_:c0_ + w_], True, True, [qn, cbf], [bk3])
                            t1 = t1_r.next()
                            tt("pool", t1[0:rows, 0:w_], qn[0:rows, c0_:c0_ + w_], rope[0:rows, 0, p0_:p0_ + w_], ALU.mult, [qn, rope], [t1])
                            t2 = t2_r.next()
                            tt("dve", t2[0:rows, 0:w_], bk3[0:rows, 0:w_], rope[0:rows, 1, p0_:p0_ + w_], ALU.mult, [bk3, rope], [t2])
                            tt("dve", dst_ap, t1[0:rows, 0:w_], t2[0:rows, 0:w_], ALU.add, [t1, t2], [dst_t])
                    if post is not None:
                        post()

                def tile_segs(tt_):
                    if L >= 512:
                        s_, o_ = divmod(tt_ * 512, L)
                        return [(s_, o_, 0, 512)]
                    k_ = 512 // L
                    return [(tt_ * k_ + i, 0, i * L, L) for i in range(k_)]

                def cq_item(tt_):
                    tsl = slice(tt_ * 512, (tt_ + 1) * 512)
                    cqf = cqf_r.next()
                    sq = sqr.next()
                    for c in range(2):
                        bk = nb()
                        for kc in range(8):
                            mm(bk.ap, wat[:, kc, c * 128:(c + 1) * 128], hT[:, kc, tsl], kc == 0, kc == 7, [wat, hT], [bk])
                        cp("dve", cqf[:, c, :], bk.ap, [bk], [cqf])
                        act(sq[:, c, :], cqf[:, c, :], AF.Square, [cqf], [sq])
                    bk = nb()
                    for c in range(2):
                        mm(bk.ap, ONES_b, sq[:, c, :], c == 0, c == 1, [sq, cbf], [bk])
                    yield
                    rs = rsr.next()
                    rstd_act(rs.ap, bk.ap, 256.0, [bk], [rs])
                    for c in range(2):
                        stt(cqn[:, c, :], cqf[:, c, :], smp[:, l, c:c + 1], rs.ap, ALU.mult, ALU.mult, [cqf, rs, smp], [cqn])

                def ckv_item(tt_):
                    tsl = slice(tt_ * 512, (tt_ + 1) * 512)
                    ckvf = ckvf_r.next()

                    def src():
                        bk = nb()
                        for kc in range(8):
                            mm(bk.ap, wat[:, kc, 256:384], hT[:, kc, tsl], kc == 0, kc == 7, [wat, hT], [bk])
                        return bk

                    def post():
                        for (s_, o_, c0, w_) in tile_segs(tt_):
                            cp("act", ckvn[:, keycol(s_, o_, w_)], ckvf[:, c0:c0 + w_], [ckvf], [ckvn])
                        if not ctx:
                            P.dma(ckvf, [("sp", ckv_o[l][:, t0 + tt_ * 512:t0 + (tt_ + 1) * 512], ckvf.ap)], reads=[ckvf], is_out=True)
                    return norm_item(src, 128, ONES_b, 128.0, smp[:, l, 2:3], 512, [(0, 512, ckvf.ap, ckvf)], [], post=post)

                def kr_item(tt_):
                    tsl = slice(tt_ * 512, (tt_ + 1) * 512)
                    bk = nb()
                    for kc in range(8):
                        mm(bk[0:32, :], wat[:, kc, 384:416], hT[:, kc, tsl], kc == 0, kc == 7, [wat, hT], [bk])
                    krf = krf_r.next()
                    cp("dve", krf[0:32, :], bk[0:32, :], [bk], [krf])
                    yield
                    for (s_, o_, c0, w_) in tile_segs(tt_):
                        cp("act", krb[0:32, keycol(s_, o_, w_)], krf[0:32, c0:c0 + w_], [krf], [krb])
                    if not ctx:
                        P.dma(krf, [("sp", kr_o[l][:, t0 + tt_ * 512:t0 + (tt_ + 1) * 512], krf[0:32, :])], reads=[krf], is_out=True)

                def q_item(tt_, h):
                    tsl = slice(tt_ * 512, (tt_ + 1) * 512)

                    def src():
                        bk = nb()
                        for c in range(2):
                            mm(bk[0:96, :], wuq[:, c, h * 96:(h + 1) * 96], cqn[:, c, :], c == 0, c == 1, [wuq, cqn], [bk])
                        return bk
                    if ctx:
                        return norm_item(src, 96, ONES_b[0:96, 0:96], 96.0, smp[0:96, l, 3:4], 512, [], [(0, 512, tt_ * 512, qaT[0:96, h, tsl], qaT)], P_l=P96[0:96, 0:96])
                    return norm_item(src, 96, ONES_b[0:96, 0:96], 96.0, smp[0:96, l, 3:4], 512, [(0, 512, qaT[0:96, h, tsl], qaT)], [])

                gens = []
                for tt_ in range(ntile):
                    gens.append(cq_item(tt_))
                    gens.append(ckv_item(tt_))
                    gens.append(kr_item(tt_))
                    for h in range(4):
                        gens.append(q_item(tt_, h))
                run_pipe(gens)

                ckpt(6)
                ktot = nseq * nk

                def k_item(kb, h):
                    w_ = min(512, ktot - kb)
                    ksl = slice(kb, kb + w_)

                    def src():
                        bk = nb()
                        mm(bk[0:96, 0:w_], SHIFT[0:32, 0:96], krb[0:32, ksl], True, False, [cbf, krb], [bk])
                        mm(bk[0:96, 0:w_], wukv[:, h * 96:(h + 1) * 96], ckvn[:, ksl], False, True, [wukv, ckvn], [bk])
                        return bk
                    if ctx:
                        if kb == 0:
                            plain = [(0, 256, kaT[0:96, h, 0:256], kaT)]
                            roped = [(256, 256, 0, kaT[0:96, h, 256:512], kaT)]
                        else:
                            plain = []
                            roped = [(0, w_, kb - 256, kaT[0:96, h, kb:kb + w_], kaT)]
                        return norm_item(src, 96, ONES_b[0:96, 0:96], 96.0, smp[0:96, l, 4:5], w_, plain, roped, P_l=P96[0:96, 0:96])
                    return norm_item(src, 96, ONES_b[0:96, 0:96], 96.0, smp[0:96, l, 4:5], w_, [(0, w_, kaT[0:96, h, ksl], kaT)], [])

                def v_item(kt):
                    bk = nb()
                    mm(bk[:, 0:256], ckvn[:, kt * 128:(kt + 1) * 128], wukv[:, 384:640], True, True, [ckvn, wukv], [bk])
                    yield
                    srcv = bk[:, 0:256].rearrange("p (h d) -> p h d", h=4)
                    cp("act", vaA[:, kt, 0::2, 0:64], srcv[:, 0::2, :], [bk], [vaA])
                    cp("dve", vaA[:, kt, 1::2, 64:128], srcv[:, 1::2, :], [bk], [vaA])

                gens = []
                for kb in range(0, ktot, 512):
                    w_ = min(512, ktot - kb)
                    for h in range(4):
                        gens.append(k_item(kb, h))
                    for kt in range(kb // 128, (kb + w_) // 128):
                        gens.append(v_item(kt))
                run_pipe(gens)

                if l == 0 and t0 == 0:
                    dbg("qaT", qaT)
                    dbg("kaT", kaT)

                ckpt(7)
                use_rot(4)
                P.barrier()
                arena.off = offC
                ptr = Ring([arena.get([128, 512], BF16, "pt") for _ in range(4)])
                rrr = Ring([arena.get([128, 512], F32, "rr") for _ in range(2)])
                odr = Ring([arena.get([128, 512], F32, "od") for _ in range(2)])
                o2r = Ring([arena.get([128, 512], F32, "o2") for _ in range(2)])
                sq2r = Ring([arena.get([128, 512], BF16, "sq2") for _ in range(2)])
                rs2r = Ring([arena.get([128, 512], F32, "rs2") for _ in range(2)])
                NQ = min(512, L)
                mstr = Ring([arena.get([128, 512], BF16, "mst") for _ in range(2)])

                def attn_pass(q_of, k_of, v_of, scale, accs, D=2):
                    items = [(kt, a) for kt in range(nkt) for a in range(len(accs))]
                    pts = {}
                    for j in range(len(items) + D):
                        if j < len(items):
                            kt, a = items[j]
                            ab, qv, kf, vf = accs[a]
                            bs = nb()
                            kv, kreads = kf(kt)
                            mm(bs[:, 0:NQ], kv, qv[0], True, True, kreads + qv[1], [bs])
                            pt = ptr.next()
                            act(pt[:, 0:NQ], bs[:, 0:NQ], AF.Exp, [bs], [pt], scale=scale)
                            pts[j] = pt
                        i = j - D
                        if i >= 0:
                            kt, a = items[i]
                            ab, qv, kf, vf = accs[a]
                            vv, vreads = vf(kt)
                            pt = pts.pop(i)
                            mm(ab[:, 0:NQ], vv, pt[:, 0:NQ], kt == 0, kt == nkt - 1, vreads + [pt], [ab])

                def finish_pair(ab0, ab1, dst_ap, dst_t, use_act=False):
                    rr = rrr.next()
                    if use_act:
                        act(rr[0:64, 0:NQ], ab0[64:128, 0:NQ], AF.Ln, [ab0], [rr])
                        act(rr[64:128, 0:NQ], ab1[0:64, 0:NQ], AF.Ln, [ab1], [rr])
                        act(rr[:, 0:NQ], rr[:, 0:NQ], AF.Exp, [rr], [rr], scale=-1.0)
                    else:
                        P.op("dve", lambda e: e.reciprocal(out=rr[0:64, 0:NQ], in_=ab0[64:128, 0:NQ]), [ab0], [rr])
                        P.op("dve", lambda e: e.reciprocal(out=rr[64:128, 0:NQ], in_=ab1[0:64, 0:NQ]), [ab1], [rr])
                    tt("dve", dst_ap[0:64], ab0[0:64, 0:NQ], rr[0:64, 0:NQ], ALU.mult, [ab0, rr], [dst_t])
                    tt("dve", dst_ap[64:128], ab1[64:128, 0:NQ], rr[64:128, 0:NQ], ALU.mult, [ab1, rr], [dst_t])

                mla_cnt = [0]
                for s_ in range(nseq):
                    for q0 in range(0, L, NQ):
                        qs = slice(s_ * L + q0, s_ * L + q0 + NQ)
                        for pr in range(2):
                            accs = []
                            ab_ = 2 * (mla_cnt[0] % 2)
                            mla_cnt[0] += 1
                            for i in range(2):
                                h = pr * 2 + i
                                accs.append((acc[ab_ + i], (qaT[0:96, h, qs], [qaT]),
                                             (lambda kt, h=h: (kaT[0:96, h, s_ * nk + kt * 128:s_ * nk + (kt + 1) * 128], [kaT])),
                                             (lambda kt, h=h: (vaA[:, s_ * nkt + kt, h, :], [vaA]))))
                            attn_pass(None, None, None, 96.0 ** -0.5, accs)
                            mst = mstr.next()
                            finish_pair(acc[ab_], acc[ab_ + 1], mst[:, 0:NQ], mst)
                            ta_ = t0 + s_ * L + q0
                            P.dma(mst, [("sp", mixD[:, pr, ta_:ta_ + NQ], mst[:, 0:NQ])], reads=[mst], writes=[mixbufs[ta_ // 512]])

                ckpt(8)
                use_rot(8)
                P.barrier()
                arena.off = offA
                wat2 = arena.get([128, 8, 928], BF16, "wat")
                ropeD = arena.get([128, 2, 2048], F32, "ropeD")
                if ctx:
                    P.dma(ropeD, [("sp", ropeD.ap, rope_d[:, 2:4, :])], writes=[ropeD])
                rope = ropeD
                qdT = arena.get([128, 4, n], BF16, "qdT")
                kdT = arena.get([128, 4, nseq * nk], BF16, "kdT")
                vdA = arena.get([128, nseq * nkt, 4, 128], BF16, "vdA")
                offD = arena.off
                memset("pool", vdA.ap, 1.0, [vdA])
                memset("pool", kdT[64:128, :, :], 0.0, [kdT])
                sqr = Ring([arena.get([128, 2, 512], BF16, "sqc") for _ in range(2)])
                rsr = Ring([arena.get([128, 512], F32, "rs") for _ in range(3)])
                qn_r = Ring([arena.get([128, 512], F32, "qn") for _ in range(2)])
                qnb_r = Ring([arena.get([128, 512], BF16, "qnb") for _ in range(2)])
                t1_r = Ring([arena.get([128, 512], F32, "t1") for _ in range(2)])
                t2_r = Ring([arena.get([128, 512], F32, "t2") for _ in range(2)])
                kdf_r = Ring([arena.get([128, 4, 512], F32, "kdf") for _ in range(2)])
                cpyr[0] = Ring([arena.get([128, 512], F32, "cpy") for _ in range(2)])
                vdf_r = Ring([arena.get([128, 4, 256], F32, "vdf") for _ in range(2)])
                if ctx:
                    kst = kdf_r.next()
                    P.dma(kst, [("sp", kst[:, :, 0:256], c_kdT[l])], writes=[kst])
                    cp("act", kdT[0:64, :, 0:256], kst[0:64, :, 0:256], [kst], [kdT])
                    vst = vdf_r.next()
                    P.dma(vst, [("sp", vst[:, 0:2, :], c_vd[l])], writes=[vst])
                    for kt in range(2):
                        srcv = vst[:, kt, :].rearrange("p (h d) -> p h d", h=4)
                        cp("act", vdA[:, kt, 0::2, 0:64], srcv[:, 0::2, :], [vst], [vdA])
                        cp("dve", vdA[:, kt, 1::2, 64:128], srcv[:, 1::2, :], [vst], [vdA])
                def dqk_item(tt_, which, h, kdf):
                    tsl = slice(tt_ * 512, (tt_ + 1) * 512)
                    c0 = 416 + which * 256 + h * 64
                    gcol = smp[0:64, l, 5 + which:6 + which]

                    def src():
                        bk = nb()
                        for kc in range(8):
                            mm(bk[0:64, :], wat[:, kc, c0:c0 + 64], hT[:, kc, tsl], kc == 0, kc == 7, [wat, hT], [bk])
                        return bk
                    if ctx:
                        if which == 0:
                            dst, dst_t = qdT[0:64, h, tsl], qdT
                        else:
                            dst, dst_t = kdT[0:64, h, keycol(0, tt_ * 512, 512)], kdT
                        return norm_item(src, 64, BD32[0:64, 0:64], 32.0, gcol, 512, [], [(0, 512, tt_ * 512, dst, dst_t)], P_l=P64[0:64, 0:64])
                    if which == 0:
                        return norm_item(src, 64, BD32[0:64, 0:64], 32.0, gcol, 512, [(0, 512, qdT[0:64, h, tsl], qdT)], [])

                    def post():
                        for (s_, o_, c0_, w_) in tile_segs(tt_):
                            cp("act", kdT[0:64, h, keycol(s_, o_, w_)], kdf[0:64, h, c0_:c0_ + w_], [kdf], [kdT])
                        if h == 3:
                            P.dma(kdf, [("sp", kd_o[l][:, :, t0 + tt_ * 512:t0 + (tt_ + 1) * 512], kdf[0:64, :, :])], reads=[kdf], is_out=True)
                    return norm_item(src, 64, BD32[0:64, 0:64], 32.0, gcol, 512, [(0, 512, kdf[0:64, h, :], kdf)], [], post=post)

                def vd_item(tt_, j, vdf):
                    tok0 = tt_ * 512 + j * 128
                    bk = nb()
                    for kc in range(8):
                        mm(bk[:, 0:256], hT[:, kc, tok0:tok0 + 128], wtm[:, kc, 0:256], kc == 0, kc == 7, [hT, wtm], [bk])
                    yield
                    s_, tl = divmod(tok0, L)
                    kt = s_ * nkt + (koff + tl) // 128
                    srcv = bk[:, 0:256].rearrange("p (h d) -> p h d", h=4)
                    cp("act", vdA[:, kt, 0::2, 0:64], srcv[:, 0::2, :], [bk], [vdA])
                    cp("dve", vdA[:, kt, 1::2, 64:128], srcv[:, 1::2, :], [bk], [vdA])
                    if not ctx:
                        cp("dve", vdf[:, j, :], bk[:, 0:256], [bk], [vdf])
                        if j == 3:
                            P.dma(vdf, [("sp", vd_o[l], vdf.ap)], reads=[vdf], is_out=True)

                gens = []
                for tt_ in range(ntile):
                    kdf = kdf_r.next()
                    vdf = vdf_r.next()
                    for which in range(2):
                        for h in range(4):
                            gens.append(dqk_item(tt_, which, h, kdf))
                    for j in range(4):
                        gens.append(vd_item(tt_, j, vdf))
                run_pipe(gens)


                ckpt(9)
                use_rot(4)
                P.barrier()
                arena.off = offD
                ptr = Ring([arena.get([128, 512], BF16, "pt") for _ in range(4)])
                rrr = Ring([arena.get([128, 512], F32, "rr") for _ in range(2)])
                odr = Ring([arena.get([128, 512], F32, "od") for _ in range(2)])
                o2r = Ring([arena.get([128, 512], F32, "o2") for _ in range(2)])
                sq2r = Ring([arena.get([128, 512], BF16, "sq2") for _ in range(2)])
                rs2r = Ring([arena.get([128, 512], F32, "rs2") for _ in range(2)])
                mstr = Ring([arena.get([128, 512], BF16, "mst") for _ in range(2)])
                qmr = Ring([arena.get([128, 2, 512], BF16, "qm") for _ in range(4)])
                for qm_ in qmr.tiles:
                    memset("pool", qm_.ap, 0.0, [qm_])
                wout = arena.get([128, 8, 1024], BF16, "wout")
                P.dma(wout, [("pool", wout.ap, w_out[l])], writes=[wout])
                for s_ in range(nseq):
                    for q0 in range(0, L, NQ):
                        qs = slice(s_ * L + q0, s_ * L + q0 + NQ)
                        for pr in range(2):
                            accs = []
                            for i in range(2):
                                h = pr * 2 + i
                                qm = qmr.next()
                                for m in range(2):
                                    cp("pool", qm[m * 32:(m + 1) * 32, m, 0:NQ], qdT[m * 32:(m + 1) * 32, h, qs], [qdT], [qm])
                                for m in range(2):
                                    accs.append((acc[i * 2 + m], (qm[:, m, 0:NQ], [qm]),
                                                 (lambda kt, h=h, m=m: (kdT[:, h, s_ * nk + kt * 128:s_ * nk + (kt + 1) * 128], [kdT])),
                                                 (lambda kt, h=h: (vdA[:, s_ * nkt + kt, h, :], [vdA]))))
                            attn_pass(None, None, None, 32.0 ** -0.5, accs)
                            od = odr.next()
                            o2 = o2r.next()
                            finish_pair(acc[0], acc[2], od[:, 0:NQ], od, use_act=True)
                            finish_pair(acc[1], acc[3], o2[:, 0:NQ], o2, use_act=True)
                            stt(od[:, 0:NQ], o2[:, 0:NQ], nlam, od[:, 0:NQ], ALU.mult, ALU.add, [o2, od, lay], [od])
                            sq2 = sq2r.next()
                            act(sq2[:, 0:NQ], od[:, 0:NQ], AF.Square, [od], [sq2])
                            bk = nb()
                            mm(bk[:, 0:NQ], BD64, sq2[:, 0:NQ], True, True, [sq2, cbf], [bk])
                            rs2 = rs2r.next()
                            rstd_act(rs2[:, 0:NQ], bk[:, 0:NQ], 64.0, [bk], [rs2])
                            mst = mstr.next()
                            stt(mst[:, 0:NQ], od[:, 0:NQ], subg, rs2[:, 0:NQ], ALU.mult, ALU.mult, [od, rs2, lay], [mst])
                            ta_ = t0 + s_ * L + q0
                            P.dma(mst, [("sp", mixD[:, 2 + pr, ta_:ta_ + NQ], mst[:, 0:NQ])], reads=[mst], writes=[mixbufs[ta_ // 512]])

                ckpt(10)
                use_rot(8)
                P.barrier()
                arena.off = offA
                xring = Ring([arena.get([128, 8, 512], F32, "xt") for _ in range(2)])
                mxr = Ring([arena.get([128, 8, 512], BF16, "mx") for _ in range(2)])
                do_mod = ctx and (l + 1 < depth)
                if do_mod:
                    wringD = Ring([arena.get([128, 8, 512], BF16, "wada") for _ in range(3)])
                def stepD_load(tt_):
                    ta_ = t0 + tt_ * 512
                    xt_ = xring.next()
                    P.dma(xt_, [("sp", xt_.ap, xsrc[:, :, ta_:ta_ + 512])], reads=[xbufs[ta_ // 512]], writes=[xt_])
                    mx_ = mxr.next()
                    P.dma(mx_, [("sp", mx_.ap, mixD[:, :, ta_:ta_ + 512])], reads=[mixbufs[ta_ // 512]], writes=[mx_])
                    return xt_, mx_
                nxtD = stepD_load(0)
                for tt_ in range(ntile):
                    xt, mx = nxtD
                    if tt_ + 1 < ntile:
                        nxtD = stepD_load(tt_ + 1)
                    if do_mod:
                        for g_ in range(3):
                            mod_group(l + 1, tt_ * 3 + g_, wringD)
                        if tt_ == ntile - 1:
                            mod_finish(l + 1)
                    ta = t0 + tt_ * 512
                    tsl = slice(tt_ * 512, (tt_ + 1) * 512)
                    xb = xbufs[ta // 512]
                    for m in range(8):
                        bk = nb()
                        for kc in range(8):
                            mm(bk.ap, wout[:, kc, m * 128:(m + 1) * 128], mx[:, kc, :], kc == 0, kc == 7, [wout, mx], [bk])
                        stt(xt[:, m, :], bk.ap, modT[:, l, 2, m, g:g + 1], xt[:, m, :], ALU.mult, ALU.add, [bk, modT, xt], [xt])
                    P.dma(xt, [("sp", xT_out[:, :, ta:ta + 512], xt.ap)], reads=[xt], writes=[xb], is_out=True)
                    if l == 0 and t0 == 0 and "mix" in debug:
                        dbg("mix", mx)

            ckpt(11)
            P.barrier()
            arena.reset()
            xall = arena.get([128, 8, NT], F32, "xall")
            h2T = arena.get([128, 8, NT], BF16, "h2T")
            xts = [T(xall[:, :, i * 512:(i + 1) * 512], "xall%d" % i) for i in range(5)]
            sqF = arena.get([128, 8, 512], BF16, "sqF")
            rsr = Ring([arena.get([128, 512], F32, "rs") for _ in range(2)])
            tmr = Ring([arena.get([128, 512], F32, "tm") for _ in range(2)])
            w1r = Ring([arena.get([128, 8, 512], BF16, "w1") for _ in range(2)])
            w2r = Ring([arena.get([128, 4, 1024], BF16, "w2") for _ in range(2)])
            rlr = Ring([arena.get([128, 512], BF16, "rl") for _ in range(3)])
            ur = Ring([arena.get([128, 4, 512], BF16, "u") for _ in range(2)])
            h2Ts = [T(h2T[:, :, i * 512:(i + 1) * 512], "h2T%d" % i) for i in range(5)]
            for i in range(5):
                P.dma(xts[i], [("sp", xts[i].ap, xT_out[:, :, i * 512:(i + 1) * 512])], reads=[xbufs[i]], writes=[xts[i]])

            def ffn_load(e8_):
                w1_ = w1r.next()
                w2_ = w2r.next()
                P.dma(w1_, [("pool", w1_.ap, w_ff1[l][:, :, e8_ * 512:(e8_ + 1) * 512])], writes=[w1_])
                P.dma(w2_, [("pool", w2_.ap, w_ff2[l][:, e8_ * 4:(e8_ + 1) * 4, :])], writes=[w2_])
                return w1_, w2_

            def do_norm(i):
                norm_mod(xts[i], h2Ts[i].ap, h2Ts[i], l, 1, 0 if i == 0 else 1, sqF, rsr, tmr)
            wsets = {0: ffn_load(0)}
            do_norm(0)
            do_norm(1)

            def ffn_item(e8, i):
                g = 0 if i == 0 else 1
                if e8 == 0 and i + 2 < 5:
                    do_norm(i + 2)
                w1, w2 = wsets[e8]
                u = ur.next()
                for jc in range(4):
                    bk = nb()
                    for kc in range(8):
                        mm(bk.ap, w1[:, kc, jc * 128:(jc + 1) * 128], h2Ts[i][:, kc, :], kc == 0, kc == 7, [w1, h2Ts[i]], [bk])
                    rl = rlr.next()
                    act(rl.ap, bk.ap, AF.Relu, [bk], [rl])
                    tt("pool", u[:, jc, :], rl.ap, rl.ap, ALU.mult, [rl], [u])
                yield
                if i == 0 and e8 + 1 < 8:
                    wsets[e8 + 1] = ffn_load(e8 + 1)
                for m in range(8):
                    bk = nb()
                    for jc in range(4):
                        mm(bk.ap, w2[:, jc, m * 128:(m + 1) * 128], u[:, jc, :], jc == 0, jc == 3, [w2, u], [bk])
                    stt(xts[i][:, m, :], bk.ap, modT[:, l, 5, m, g:g + 1], xts[i][:, m, :], ALU.mult, ALU.add, [bk, modT, xts[i]], [xts[i]])
            run_pipe2([ffn_item(e8_, i_) for e8_ in range(8) for i_ in range(5)])
            for i in range(5):
                P.dma(xts[i], [("sp", xT_out[:, :, i * 512:(i + 1) * 512], xts[i].ap)], reads=[xts[i]], writes=[xbufs[i]], is_out=True)

    except StopBuild:
        pass
    P.finish()
    return nc, dbg_outs, P, arena


_CACHE = {}


def kernel(**inputs):
    inp = {k: np.asarray(v) for k, v in inputs.items()}
    consts = host_consts()
    shared = prep_shared(inp)
    in_maps = []
    for core in range(8):
        d = {}
        d.update(shared)
        d.update(consts)
        d.update(prep_core(inp, core, consts))
        d.update(prep_cache(inp, core))
        in_maps.append({k: np.ascontiguousarray(v, dtype=np.float32) for k, v in d.items()})
    if "nc" not in _CACHE:
        _CACHE["nc"] = build(DEPTH_RUN)[0]
    nc = _CACHE["nc"]
    res = run_bass_kernel_spmd(nc, in_maps, core_ids=list(range(8)))
    R = res.results
    B, S, Dm = 16, 256, 1024
    y_prompt = np.zeros((16, 256, 1024), np.float32)
    y_sample = np.zeros((4, 2048, 1024), np.float32)
    new_ckv = np.zeros((16, NL, 256, 128), np.float32)
    new_kr = np.zeros((16, NL, 256, 32), np.float32)
    new_kd = np.zeros((16, NL, 256, 4, 64), np.float32)
    new_vd = np.zeros((16, NL, 256, 4, 64), np.float32)
    new_st = np.zeros((16, NL, 2, 8, 64, 64), np.float32)
    for core in range(8):
        r = R[core]
        xo = r["xT_out"].transpose(2, 1, 0).reshape(NT, 1024)
        y_prompt[2 * core] = xo[0:256]
        y_prompt[2 * core + 1] = xo[256:512]
        if core % 2 == 0:
            y_sample[core // 2] = xo[512:]
        for s in range(2):
            bidx = 2 * core + s
            new_ckv[bidx] = r["ckv_o"][:, :, s * 256:(s + 1) * 256].transpose(0, 2, 1)
            new_kr[bidx] = r["kr_o"][:, :, s * 256:(s + 1) * 256].transpose(0, 2, 1)
            kd = r["kd_o"][:, :, :, s * 256:(s + 1) * 256]
            new_kd[bidx] = kd.transpose(0, 3, 2, 1)
            vd = r["vd_o"].reshape(NL, 128, 4, 4, 64).transpose(0, 2, 1, 3, 4).reshape(NL, 512, 4, 64)
            new_vd[bidx] = vd[:, s * 256:(s + 1) * 256]
            st = r["st_o"][:, s]
            st = st.reshape(NL, 2, 2, 64, 4, 64).transpose(0, 1, 2, 4, 5, 3)
            new_st[bidx] = st.reshape(NL, 2, 8, 64, 64)
    return (y_prompt, y_sample, new_ckv, new_kr, new_kd, new_vd, new_st)
```
